# Optimizing a Trainium2 kernel written in Bass

```python
import math
import jax, jax.numpy as jnp
from jax import lax
import numpy as np

D_MODEL = 1024
BATCH = 2
SEQ = 8192
DEPTH = 1
DEC_BATCH = 8
DEC_SEQ = 32
PAST_LEN = 1024

CHUNK = 64
H_A = 4
HD_A = 64
W_A = H_A * 2 * HD_A
H_B = 8
HD_B = 64
W_B = H_B * HD_B
BAND_CHUNKS = 8
BAND_PAST = BAND_CHUNKS * CHUNK
REL_CLIP_B = 128
T5_BUCKETS = 32
T5_MAX_EXACT = 8
T5_MAX_DIST = 128
Q_BLOCK = 128
EPS = 1e-6
NEG = -1e30
IN_COLS = 4 * W_A + 4 * W_B + 2 * D_MODEL
SPLITS = [W_A, 2 * W_A, 3 * W_A, 4 * W_A, 4 * W_A + W_B, 4 * W_A + 2 * W_B, 4 * W_A + 3 * W_B, 4 * W_A + 4 * W_B]

kernel_name = 'hybrid_diffattn_chunkband_stream_step'


def rms_norm(x, g):
    xf = x.astype(jnp.float32)
    y = xf * lax.rsqrt(jnp.mean(xf * xf, axis=-1, keepdims=True) + EPS)
    return (y * g.astype(jnp.float32)).astype(x.dtype)


def t5_bucket(rel):
    half = T5_BUCKETS // 2
    ret = jnp.where(rel > 0, half, 0)
    n = jnp.abs(rel)
    nf = jnp.maximum(n, 1).astype(jnp.float32)
    large = T5_MAX_EXACT + (jnp.log(nf / T5_MAX_EXACT) / math.log(T5_MAX_DIST / T5_MAX_EXACT) * (half - T5_MAX_EXACT)).astype(jnp.int32)
    large = jnp.minimum(large, half - 1)
    return ret + jnp.where(n < T5_MAX_EXACT, n, large)


def diff_attn(q, k, v, qpos, kpos, t5_bias, lam, g_subln, lam_init):
    logits = jnp.einsum('bqhmd,bkhmd->bmhqk', q, k).astype(jnp.float32) * (HD_A ** -0.5)
    bias = jnp.transpose(t5_bias[t5_bucket(kpos[None, :] - qpos[:, None])], (2, 0, 1)).astype(jnp.float32)
    visible = (kpos[None, :] // CHUNK) <= (qpos[:, None] // CHUNK)
    p = jax.nn.softmax(jnp.where(visible, logits + bias, NEG), axis=-1)
    w = (p[:, 0] - lam * p[:, 1]).astype(v.dtype)
    o = jnp.einsum('bhqk,bkhe->bqhe', w, v)
    return rms_norm(o, g_subln) * (1.0 - lam_init)


def band_attn(q, k, v, qpos, kpos, rel_bias):
    logits = jnp.einsum('bnqhd,bnkhd->bnhqk', q, k).astype(jnp.float32) * (HD_B ** -0.5)
    rel = kpos[:, None, :] - qpos[:, :, None]
    bias = jnp.transpose(rel_bias[jnp.clip(rel, -REL_CLIP_B, REL_CLIP_B) + REL_CLIP_B], (0, 3, 1, 2)).astype(jnp.float32)
    qc = (qpos // CHUNK)[:, :, None]
    kc = (kpos // CHUNK)[:, None, :]
    visible = (kpos[:, None, :] >= 0) & (kc <= qc) & (kc >= qc - BAND_CHUNKS)
    p = jax.nn.softmax(jnp.where(visible[:, None], logits + bias, NEG), axis=-1)
    return jnp.einsum('bnhqk,bnkhd->bnqhd', p.astype(v.dtype), v)


def modulate_and_project(x, c, g_norm, w_ada, b_ada, w_in, g_qa, g_ka, g_qb, g_kb):
    B, T = x.shape[0], x.shape[1]
    mod = jax.nn.silu(c) @ w_ada + b_ada
    shift, scale, gate = jnp.split(mod, 3, axis=-1)
    h = rms_norm(x, g_norm) * (1.0 + scale[:, None]) + shift[:, None]
    qa, ka, va, ga, qb, kb, vb, gb, mg = jnp.split(h @ w_in, SPLITS, axis=-1)
    qa = rms_norm(qa.reshape(B, T, H_A, 2, HD_A), g_qa)
    ka = rms_norm(ka.reshape(B, T, H_A, 2, HD_A), g_ka)
    va = va.reshape(B, T, H_A, 2 * HD_A)
    qb = rms_norm(qb.reshape(B, T, H_B, HD_B), g_qb)
    kb = rms_norm(kb.reshape(B, T, H_B, HD_B), g_kb)
    vb = vb.reshape(B, T, H_B, HD_B)
    return gate, qa, ka, va, ga, qb, kb, vb, gb, mg


def merge_output(x, gate, oa, ga, ob, gb, mg, w_oa, w_ob, w_out):
    ya = (oa * jax.nn.silu(ga)) @ w_oa
    yb = (ob * jax.nn.silu(gb)) @ w_ob
    mga, mgb = jnp.split(mg, 2, axis=-1)
    m = jax.nn.sigmoid(mga) * ya + jax.nn.sigmoid(mgb) * yb
    return x + gate[:, None] * (m @ w_out)


def diff_attn_prompt(qa, ka, va, t5_bias, lam, g_subln, lam_init):
    B, S = qa.shape[0], qa.shape[1]
    nb = S // Q_BLOCK
    qblocks = jnp.moveaxis(qa.reshape(B, nb, Q_BLOCK, H_A, 2, HD_A), 1, 0)
    kpos = jnp.arange(S)

    def one_block(args):
        qi, i = args
        qpos = i * Q_BLOCK + jnp.arange(Q_BLOCK)
        return diff_attn(qi, ka, va, qpos, kpos, t5_bias, lam, g_subln, lam_init)

    o = lax.map(one_block, (qblocks, jnp.arange(nb)))
    return jnp.moveaxis(o, 0, 1).reshape(B, S, W_A)


def band_attn_prompt(qb, kb, vb, rel_bias):
    B, S = qb.shape[0], qb.shape[1]
    nc = S // CHUNK
    pad = ((0, 0), (BAND_CHUNKS, 0), (0, 0), (0, 0), (0, 0))
    kp = jnp.pad(kb.reshape(B, nc, CHUNK, H_B, HD_B), pad)
    vp = jnp.pad(vb.reshape(B, nc, CHUNK, H_B, HD_B), pad)
    idx = jnp.arange(nc)[:, None] + jnp.arange(BAND_CHUNKS + 1)[None, :]
    band_len = (BAND_CHUNKS + 1) * CHUNK
    kband = kp[:, idx].reshape(B, nc, band_len, H_B, HD_B)
    vband = vp[:, idx].reshape(B, nc, band_len, H_B, HD_B)
    kpos = ((idx - BAND_CHUNKS)[:, :, None] * CHUNK + jnp.arange(CHUNK)[None, None, :]).reshape(nc, band_len)
    qpos = jnp.arange(nc)[:, None] * CHUNK + jnp.arange(CHUNK)[None, :]
    o = band_attn(qb.reshape(B, nc, CHUNK, H_B, HD_B), kband, vband, qpos, kpos, rel_bias)
    return o.reshape(B, S, W_B)


def setup_inputs(seed: int = 0) -> dict:
    key = jax.random.key(seed)
    ks = jax.random.split(key, 32)
    f32 = jnp.float32
    nrm = lambda k, shp, s: jax.random.normal(k, shp, f32) * s
    lb = min(BAND_PAST, PAST_LEN)
    return {
        'x_prompt': nrm(ks[0], (BATCH, SEQ, D_MODEL), 1.0),
        'x_sample': nrm(ks[1], (DEC_BATCH, DEC_SEQ, D_MODEL), 1.0),
        'cache_a_k': nrm(ks[2], (DEPTH, DEC_BATCH, PAST_LEN, H_A, 2 * HD_A), 1.0),
        'cache_a_v': nrm(ks[3], (DEPTH, DEC_BATCH, PAST_LEN, H_A, 2 * HD_A), 1.0),
        'cache_b_k': nrm(ks[4], (DEPTH, DEC_BATCH, lb, H_B, HD_B), 1.0),
        'cache_b_v': nrm(ks[5], (DEPTH, DEC_BATCH, lb, H_B, HD_B), 1.0),
        'c_prompt': nrm(ks[6], (BATCH, D_MODEL), 1.0),
        'c_sample': nrm(ks[7], (DEC_BATCH, D_MODEL), 1.0),
        'g_norm': 1.0 + nrm(ks[8], (DEPTH, D_MODEL), 0.01),
        'w_ada': nrm(ks[9], (DEPTH, D_MODEL, 3 * D_MODEL), 0.5 * D_MODEL ** -0.5),
        'b_ada': nrm(ks[10], (DEPTH, 3 * D_MODEL), 0.01),
        'w_in': nrm(ks[11], (DEPTH, D_MODEL, IN_COLS), D_MODEL ** -0.5),
        'g_qa': 1.0 + nrm(ks[12], (DEPTH, HD_A), 0.01),
        'g_ka': 1.0 + nrm(ks[13], (DEPTH, HD_A), 0.01),
        'lam_q1': nrm(ks[14], (DEPTH, HD_A), 0.1),
        'lam_k1': nrm(ks[15], (DEPTH, HD_A), 0.1),
        'lam_q2': nrm(ks[16], (DEPTH, HD_A), 0.1),
        'lam_k2': nrm(ks[17], (DEPTH, HD_A), 0.1),
        'g_subln': 1.0 + nrm(ks[18], (DEPTH, 2 * HD_A), 0.01),
        't5_bias': nrm(ks[19], (T5_BUCKETS, H_A), 0.1),
        'g_qb': 1.0 + nrm(ks[20], (DEPTH, HD_B), 0.01),
        'g_kb': 1.0 + nrm(ks[21], (DEPTH, HD_B), 0.01),
        'rel_bias_b': nrm(ks[22], (DEPTH, 2 * REL_CLIP_B + 1, H_B), 0.1),
        'w_oa': nrm(ks[23], (DEPTH, W_A, D_MODEL), W_A ** -0.5),
        'w_ob': nrm(ks[24], (DEPTH, W_B, D_MODEL), W_B ** -0.5),
        'w_out': nrm(ks[25], (DEPTH, D_MODEL, D_MODEL), D_MODEL ** -0.5),
    }


def reference(x_prompt, x_sample, cache_a_k, cache_a_v, cache_b_k, cache_b_v, c_prompt, c_sample, g_norm, w_ada, b_ada, w_in, g_qa, g_ka, lam_q1, lam_k1, lam_q2, lam_k2, g_subln, t5_bias, g_qb, g_kb, rel_bias_b, w_oa, w_ob, w_out):
    xp, xs = x_prompt, x_sample
    BP, S = xp.shape[0], xp.shape[1]
    BS, T = xs.shape[0], xs.shape[1]
    past_len = cache_a_k.shape[2]
    n_keep = min(BAND_PAST, S)
    akp, avp, bkp, bvp, aks, avs, bks, bvs = [], [], [], [], [], [], [], []
    for l in range(DEPTH):
        lam_init = 0.8 - 0.6 * math.exp(-0.3 * l)
        lam = (jnp.exp(jnp.sum((lam_q1[l] * lam_k1[l]).astype(jnp.float32)))
               - jnp.exp(jnp.sum((lam_q2[l] * lam_k2[l]).astype(jnp.float32))) + lam_init)
        proj = (g_norm[l], w_ada[l], b_ada[l], w_in[l], g_qa[l], g_ka[l], g_qb[l], g_kb[l])
        outp = (w_oa[l], w_ob[l], w_out[l])

        gate, qa, ka, va, ga, qb, kb, vb, gb, mg = modulate_and_project(xp, c_prompt, *proj)
        oa = diff_attn_prompt(qa, ka, va, t5_bias, lam, g_subln[l], lam_init)
        ob = band_attn_prompt(qb, kb, vb, rel_bias_b[l])
        xp = merge_output(xp, gate, oa, ga, ob, gb, mg, *outp)
        akp.append(ka.reshape(BP, S, H_A, 2 * HD_A))
        avp.append(va)
        bkp.append(kb[:, S - n_keep:])
        bvp.append(vb[:, S - n_keep:])

        gate, qa, ka, va, ga, qb, kb, vb, gb, mg = modulate_and_project(xs, c_sample, *proj)
        k_all = jnp.concatenate([cache_a_k[l].reshape(BS, past_len, H_A, 2, HD_A), ka], axis=1)
        v_all = jnp.concatenate([cache_a_v[l], va], axis=1)
        qpos = past_len + jnp.arange(T)
        kpos = jnp.arange(past_len + T)
        oa = diff_attn(qa, k_all, v_all, qpos, kpos, t5_bias, lam, g_subln[l], lam_init).reshape(BS, T, W_A)
        lb = cache_b_k.shape[2]
        kb_all = jnp.concatenate([cache_b_k[l], kb], axis=1)[:, None]
        vb_all = jnp.concatenate([cache_b_v[l], vb], axis=1)[:, None]
        kpos_b = jnp.concatenate([past_len - lb + jnp.arange(lb), qpos])[None]
        ob = band_attn(qb[:, None], kb_all, vb_all, qpos[None], kpos_b, rel_bias_b[l]).reshape(BS, T, W_B)
        xs = merge_output(xs, gate, oa, ga, ob, gb, mg, *outp)
        aks.append(ka.reshape(BS, T, H_A, 2 * HD_A))
        avs.append(va)
        bks.append(kb)
        bvs.append(vb)

    return (xp, xs, jnp.stack(akp), jnp.stack(avp), jnp.stack(bkp), jnp.stack(bvp), jnp.stack(aks), jnp.stack(avs), jnp.stack(bks), jnp.stack(bvs))
```

```python
import math
from contextlib import ExitStack

import numpy as np
import concourse.bass as bass
import concourse.mybir as mybir
from concourse.bass_utils import run_bass_kernel_spmd

F32 = mybir.dt.float32
BF16 = mybir.dt.bfloat16
AF = mybir.ActivationFunctionType
ALU = mybir.AluOpType
AX = mybir.AxisListType

D = 1024
S = 8192
NT = 65
TOK = NT * 128
EPS = 1e-6
NEGM = -30000.0
LAM_INIT = 0.8 - 0.6 * math.exp(0.0)
MT = 2048 + 32
STOP_AFTER = None


class Buf:
    def __init__(self, name):
        self.name = name
        self.w = None
        self.r = []
        self.dsem = None
        self.dcnt = 0


class Sched:
    ENGS = ["pe", "act", "dve", "pool", "sp"]

    def __init__(self, nc, stack):
        self.nc = nc
        self.stack = stack
        self.ops = {e: [] for e in self.ENGS}
        self.esem = {e: stack.enter_context(nc.semaphore("es_" + e)) for e in self.ENGS}
        self.nsem = 5

    def buf(self, name):
        return Buf(name)

    def _dsem(self, b):
        if b.dsem is None:
            b.dsem = self.stack.enter_context(self.nc.semaphore("ds_" + b.name))
            self.nsem += 1
        return b.dsem

    def op(self, eng, fn, reads=(), writes=(), dma=False):
        deps = []
        for b in reads:
            if b.w is not None:
                deps.append(b.w)
        for b in writes:
            if b.w is not None:
                deps.append(b.w)
            deps.extend(b.r)
        k = len(self.ops[eng])
        if dma:
            tgt = writes[0] if writes else reads[0]
            sem = self._dsem(tgt)
            tgt.dcnt += 16
            ev = ("d", sem, tgt.dcnt)
        else:
            ev = ("c", eng, k)
        self.ops[eng].append(dict(fn=fn, deps=deps, ev=ev, dma=dma))
        for b in writes:
            b.w = ev
            b.r = []
        for b in reads:
            if b not in writes:
                b.r.append(ev)
        return ev

    def barrier_all(self, bufs):
        evs = []
        for b in bufs:
            if b.w is not None:
                evs.append(b.w)
            evs.extend(b.r)
        return evs

    def emit(self, block):
        need = {e: set() for e in self.ENGS}
        for e in self.ENGS:
            for o in self.ops[e]:
                for d in o["deps"]:
                    if d[0] == "c" and d[1] != e:
                        need[d[1]].add(d[2])
                    elif d[0] == "c" and d[1] == e and e != "pe":
                        need[e].add(d[2])
        rank = {}
        for e in self.ENGS:
            for i, k in enumerate(sorted(need[e])):
                rank[(e, k)] = i + 1
        esem = self.esem

        def run(e, handle):
            seen = {}
            for k, o in enumerate(self.ops[e]):
                for d in o["deps"]:
                    if d[0] == "c":
                        if d[1] == e and e == "pe":
                            continue
                        sem, val = esem[d[1]], rank[(d[1], d[2])]
                    else:
                        sem, val = d[1], d[2]
                    key = id(sem)
                    if seen.get(key, 0) >= val:
                        continue
                    seen[key] = val
                    handle.wait_ge(sem, val)
                ins = o["fn"](handle)
                if o["dma"]:
                    ins.then_inc(o["ev"][1], 16)
                elif (e, k) in rank:
                    ins.then_inc(esem[e], 1)

        @block.tensor
        def _(h):
            run("pe", h)

        @block.scalar
        def _(h):
            run("act", h)

        @block.vector
        def _(h):
            run("dve", h)

        @block.gpsimd
        def _(h):
            run("pool", h)

        @block.sync
        def _(h):
            run("sp", h)


def _t5_bucket_np(rel):
    rel = np.asarray(rel, np.int64)
    half = 16
    ret = np.where(rel > 0, half, 0)
    n = np.abs(rel)
    nf = np.maximum(n, 1).astype(np.float32)
    large = 8 + (np.log(nf / np.float32(8)) / np.float32(math.log(128 / 8)) * np.float32(8)).astype(np.int32)
    large = np.minimum(large, half - 1)
    return ret + np.where(n < 8, n, large)


def build_program():
    nc = bass.Bass("TRN2", target_bir_lowering=False)
    try:
        nc.allow_low_precision("bf16 matmul operands with fp32 accumulation (reference tolerance is bf16-level)")
    except Exception:
        pass
    try:
        nc.allow_non_contiguous_dma("small strided parameter loads")
    except Exception:
        pass

    def din(name, shape, dt=F32):
        return nc.dram_tensor(name, list(shape), dt, kind="ExternalInput").ap()

    def dout(name, shape, dt=F32):
        return nc.dram_tensor(name, list(shape), dt, kind="ExternalOutput").ap()

    xp = din("xp", [S, D])
    xs4 = din("xs4", [128, D])
    xm = din("xm", [MT, D])
    cvec = din("cvec", [D, 6])
    w_ada = din("w_ada", [D, 3 * D])
    b_ada = din("b_ada", [128, 24])
    g_norm = din("g_norm", [128, 8])
    w_c = din("w_c", [D, 1024])
    w_mg = din("w_mg", [D, 2048])
    w_oa = din("w_oa", [512, D])
    w_ob = din("w_ob", [512, D])
    w_out = din("w_out", [D, D])
    gvec = din("gvec", [128, 8])
    lamv = din("lamv", [128, 4, 64])
    bA = din("bA", [128, 5, 512])
    mA = din("mA", [128, 5, 512])
    c15 = din("c15", [128, 1])
    msk = din("msk", [128, 4])
    cak = din("cak", [4, 1024, 128]); cav = din("cav", [4, 1024, 128])
    cbk = din("cbk", [4, 512, 128]); cbv = din("cbv", [4, 512, 128])
    bAs = din("bAs", [128, 8, 32]); bAsn = din("bAsn", [128, 128])
    bB = din("bB", [128, 2, 640]); bBs = din("bBs", [128, 2, 4, 32]); bBsn = din("bBsn", [128, 2, 128])
    onesf = din("onesf", [128, 128])
    ident = din("ident", [128, 128])
    bones = din("bones", [128, 128])
    y = dout("y", [MT, D])
    ak = dout("ak", [S, 128])
    av = dout("av", [S, 128])
    bk = dout("bk", [512, 128])
    bv = dout("bv", [512, 128])
    aks = dout("aks", [128, 128])
    avs = dout("avs", [128, 128])
    bks = dout("bks", [128, 128])
    bvs = dout("bvs", [128, 128])

    with ExitStack() as st:
        sc = Sched(nc, st)

        def sb(name, shape, dt=F32):
            return st.enter_context(nc.sbuf_tensor(name, list(shape), dt))

        def ps(name, shape, dt=F32):
            return st.enter_context(nc.psum_tensor(name, list(shape), dt))

        qaT = sb("qaT", [128, TOK], BF16); B_qaT = sc.buf("qaT")
        kaT = sb("kaT", [128, TOK], BF16); B_kaT = sc.buf("kaT")
        gaT = sb("gaT", [128, TOK], BF16); B_gaT = sc.buf("gaT")
        qbT = sb("qbT", [128, TOK], BF16); B_qbT = sc.buf("qbT")
        kbT = sb("kbT", [128, TOK], BF16); B_kbT = sc.buf("kbT")
        gbT = sb("gbT", [128, TOK], BF16); B_gbT = sc.buf("gbT")
        va = sb("va", [128, NT, 128], BF16); B_va = sc.buf("va")
        vb = sb("vb", [128, NT, 128], BF16); B_vb = sc.buf("vb")
        feat_sb = [qaT, kaT, gaT, qbT, kbT, gbT]
        feat_B = [B_qaT, B_kaT, B_gaT, B_qbT, B_kbT, B_gbT]

        ident_s = sb("ident_s", [128, 128]); B_ident = sc.buf("ident")
        bones_s = sb("bones_s", [128, 128]); B_bones = sc.buf("bones")
        gvec_s = sb("gvec_s", [128, 8]); B_gvec = sc.buf("gvec")
        gsc = sb("gsc", [128, 8]); B_gsc = sc.buf("gsc")
        gn_s = sb("gn_s", [128, 8]); B_gn = sc.buf("gn")
        bada_s = sb("bada_s", [128, 24]); B_bada = sc.buf("bada")
        cv_s = sb("cv_s", [128, 8, 6]); B_cv = sc.buf("cv")
        scv = sb("scv", [128, 8, 6]); B_scv = sc.buf("scv")
        modT = sb("modT", [128, 24, 6]); B_modT = sc.buf("modT")
        Amod = sb("Amod", [128, 8, 6]); B_Amod = sc.buf("Amod")
        wc_s = sb("wc_s", [128, 8, 1024], BF16); B_wc = sc.buf("wc")

        sc.op("sp", lambda e: e.dma_start(out=ident_s[:], in_=ident), writes=[B_ident], dma=True)
        sc.op("sp", lambda e: e.dma_start(out=bones_s[:], in_=bones), writes=[B_bones], dma=True)
        identb = sb("identb", [128, 128], BF16); B_identb = sc.buf("identb")
        bonesb = sb("bonesb", [128, 128], BF16); B_bonesb = sc.buf("bonesb")
        sc.op("pool", lambda e: e.dma_start(out=identb[:], in_=ident), writes=[B_identb], dma=True)
        sc.op("pool", lambda e: e.dma_start(out=bonesb[:], in_=bones), writes=[B_bonesb], dma=True)
        sc.op("sp", lambda e: e.dma_start(out=gvec_s[:], in_=gvec), writes=[B_gvec], dma=True)
        sc.op("sp", lambda e: e.dma_start(out=gn_s[:], in_=g_norm), writes=[B_gn], dma=True)
        sc.op("sp", lambda e: e.dma_start(out=bada_s[:], in_=b_ada), writes=[B_bada], dma=True)
        sc.op("sp", lambda e: e.dma_start(out=cv_s[:], in_=cvec.rearrange("(kc p) n -> p kc n", p=128)),
              writes=[B_cv], dma=True)
        sc.op("pool", lambda e: e.dma_start(out=wc_s[:], in_=w_c.rearrange("(kc p) n -> p kc n", p=128)),
              writes=[B_wc], dma=True)

        sc.op("act", lambda e: e.activation(out=scv[:], in_=cv_s[:], func=AF.Exp, scale=-1.0), reads=[B_cv], writes=[B_scv])
        sc.op("dve", lambda e: e.tensor_scalar(out=scv[:], in0=scv[:], scalar1=1.0, scalar2=None, op0=ALU.add),
              reads=[B_scv], writes=[B_scv])
        sc.op("dve", lambda e: e.reciprocal(out=scv[:], in_=scv[:]), reads=[B_scv], writes=[B_scv])
        sc.op("dve", lambda e: e.tensor_tensor(out=scv[:], in0=scv[:], in1=cv_s[:], op=ALU.mult),
              reads=[B_scv, B_cv], writes=[B_scv])
        xsl = [sb("xsl%d" % i, [128, D]) for i in range(2)]; B_xsl = [sc.buf("xsl%d" % i) for i in range(2)]
        xnb = [sb("xnb%d" % i, [128, D], BF16) for i in range(2)]; B_xnb = [sc.buf("xnb%d" % i) for i in range(2)]
        wada_s = [xnb[i][:].rearrange("p (k n) -> p k n", k=8) for i in range(2)]
        B_wada = B_xnb
        scvb = sb("scvb", [128, 8, 6], BF16); B_scvb = sc.buf("scvb")
        sc.op("dve", lambda e: e.tensor_copy(out=scvb[:], in_=scv[:]), reads=[B_scv], writes=[B_scvb])
        PA = ps("PA", [128, 2, 512]); PB = ps("PB", [128, 2, 512]); PC = ps("PC", [128, 2, 512]); PD = ps("PD", [128, 2, 512])
        pf = [PB[:, i, :] for i in range(2)]; B_pf = [sc.buf("pf%d" % i) for i in range(2)]
        pmod = pf[0][:, 0:192].rearrange("p (m n) -> p m n", n=8); B_pmod = B_pf[0]
        wada_v = w_ada.rearrange("(kc p) n -> p kc n", p=128)
        for m in range(24):
            sl = m % 2
            sc.op("pool", lambda e, m=m, sl=sl: e.dma_start(out=wada_s[sl], in_=wada_v[:, :, m * 128:(m + 1) * 128]),
                  writes=[B_wada[sl]], dma=True)
            for kc in range(8):
                sc.op("pe", lambda e, m=m, sl=sl, kc=kc: e.matmul(pmod[:, m, 0:6], lhsT=wada_s[sl][:, kc, :],
                                                                  rhs=scvb[:, kc, :], start=(kc == 0), stop=(kc == 7)),
                      reads=[B_wada[sl], B_scvb], writes=[B_pmod])
        for m in range(24):
            sc.op("dve", lambda e, m=m: e.tensor_scalar(out=modT[:, m, :], in0=pmod[:, m, 0:6],
                                                        scalar1=bada_s[:, m:m + 1], scalar2=None, op0=ALU.add),
                  reads=[B_pmod, B_bada], writes=[B_modT])
        sc.op("dve", lambda e: e.tensor_scalar(out=Amod[:], in0=modT[:, 8:16, :], scalar1=1.0, scalar2=32.0,
                                               op0=ALU.add, op1=ALU.mult), reads=[B_modT], writes=[B_Amod])
        for kc in range(8):
            sc.op("dve", lambda e, kc=kc: e.tensor_scalar(out=Amod[:, kc, :], in0=Amod[:, kc, :],
                                                          scalar1=gn_s[:, kc:kc + 1], scalar2=None, op0=ALU.mult),
                  reads=[B_gn], writes=[B_Amod])
        sc.op("dve", lambda e: e.tensor_scalar(out=gsc[:, 0:4], in0=gvec_s[:, 0:4], scalar1=1.0, scalar2=None,
                                               op0=ALU.mult), reads=[B_gvec], writes=[B_gsc])
        sc.op("dve", lambda e: e.tensor_scalar(out=gsc[:, 1:2], in0=gvec_s[:, 1:2], scalar1=8.0, scalar2=None,
                                               op0=ALU.mult), reads=[B_gvec], writes=[B_gsc])
        sc.op("dve", lambda e: e.tensor_scalar(out=gsc[:, 3:4], in0=gvec_s[:, 3:4], scalar1=8.0, scalar2=None,
                                               op0=ALU.mult), reads=[B_gvec], writes=[B_gsc])

        epsc = sb("epsc", [128, 4]); B_epsc = sc.buf("epsc")
        sc.op("pool", lambda e: e.memset(epsc[:, 0:1], float(D * EPS)), writes=[B_epsc])
        sc.op("pool", lambda e: e.memset(epsc[:, 1:2], float(64 * EPS)), writes=[B_epsc])
        sc.op("pool", lambda e: e.memset(epsc[:, 2:3], float(128 * EPS)), writes=[B_epsc])
        sc.op("pool", lambda e: e.memset(epsc[:, 3:4], 1.0), writes=[B_epsc])
        xn = [sb("xn%d" % i, [128, D]) for i in range(2)]; B_xn = [sc.buf("xn%d" % i) for i in range(2)]
        ssq = sb("ssq", [128, 2]); B_ssq = [sc.buf("ssq0"), sc.buf("ssq1")]
        rstd = sb("rstd", [128, 2]); B_rstd = [sc.buf("rstd0"), sc.buf("rstd1")]
        hT = [sb("hT%d" % i, [128, 8, 512], BF16) for i in range(2)]; B_hT = [sc.buf("hT%d" % i) for i in range(2)]
        pt = PA[:].rearrange("p a (b c) -> p (a b) c", c=128); B_pt = sc.buf("pt")
        pn = PC[:, 0, :]; B_pn = sc.buf("pn")
        pk = PC[:, 1, :].rearrange("p (t c) -> p t c", c=128); B_pk = sc.buf("pk")
        pv = [PD[:, i, :] for i in range(2)]; B_pv = [sc.buf("pv0"), sc.buf("pv1")]
        sq = sb("sq", [128, 512]); B_sq = sc.buf("sq")
        rs = sb("rs", [128, 512]); B_rs = sc.buf("rs")
        tmpn = sb("tmpn", [128, 512]); B_tmpn = sc.buf("tmpn")
        kf32 = sb("kf32", [128, 512]); B_kf32 = sc.buf("kf32")
        kout = [sb("kout%d" % i, [128, 4, 128]) for i in range(2)]; B_kout = [sc.buf("kout%d" % i) for i in range(2)]
        vst = [sb("vst%d" % i, [128, 256]) for i in range(2)]; B_vst = [sc.buf("vst%d" % i) for i in range(2)]
        B_ak = sc.buf("ak_out"); B_av = sc.buf("av_out"); B_bk = sc.buf("bk_out"); B_bv = sc.buf("bv_out")
        out_bufs = [B_ak, B_av, B_bk, B_bv]

        tile_ctr = [0]
        kout_ctr = [0]

        sq2 = [sq, kf32]; B_sq2 = [B_sq, B_kf32]
        sqb = [sb("sqb%d" % i, [128, 512], BF16) for i in range(2)]; B_sqb = [sc.buf("sqb%d" % i) for i in range(2)]
        rs2 = [rs, sb("rs_b", [128, 512])]; B_rs2 = [B_rs, sc.buf("rs_b")]
        nrm_ctr = [0]
        later = []
        later2 = []

        def grp_ntile(gi):
            return 4 if gi < 16 else 1

        def grp_ctx(gi):
            return [(0, 128, 0)] if gi < 16 else [(32 * i, 32 * i + 32, 1 + i) for i in range(4)]

        tile_slot = {}

        def prep_a(gi, t):
            tg = gi * 4 + t
            sl = tile_ctr[0] % 2
            tile_ctr[0] += 1
            tile_slot[(gi, t)] = sl
            src = xp[tg * 128:(tg + 1) * 128, :] if tg < 64 else xs4
            sc.op("sp", lambda e, sl=sl, src=src: e.dma_start(out=xsl[sl][:], in_=src), writes=[B_xsl[sl]], dma=True)
            sc.op("act", lambda e, sl=sl: e.activation(out=xnb[sl][:], in_=xsl[sl][:], func=AF.Square,
                                                       accum_out=ssq[:, sl:sl + 1]),
                  reads=[B_xsl[sl]], writes=[B_xnb[sl], B_ssq[sl]])
            sc.op("act", lambda e, sl=sl: e.activation(out=ssq[:, sl:sl + 1], in_=ssq[:, sl:sl + 1], func=AF.Ln,
                                                       bias=epsc[:, 0:1]),
                  reads=[B_ssq[sl], B_epsc], writes=[B_ssq[sl]])
            sc.op("act", lambda e, sl=sl: e.activation(out=rstd[:, sl:sl + 1], in_=ssq[:, sl:sl + 1], func=AF.Exp, scale=-0.5),
                  reads=[B_ssq[sl]], writes=[B_rstd[sl]])
            sc.op("act", lambda e, sl=sl: e.mul(out=xnb[sl][:], in_=xsl[sl][:], mul=rstd[:, sl:sl + 1]),
                  reads=[B_xsl[sl], B_rstd[sl]], writes=[B_xnb[sl]])

        def prep_b(gi, t):
            hs = gi % 2
            sl = tile_slot[(gi, t)]
            for kc in range(8):
                sc.op("pe", lambda e, sl=sl, kc=kc: e.matmul(pt[:, kc, :], lhsT=xnb[sl][:, kc * 128:(kc + 1) * 128], rhs=identb[:],
                                                             start=True, stop=True),
                      reads=[B_xnb[sl], B_identb], writes=[B_pt])
            if gi < 16:
                xv = xn[sl][:].rearrange("p (k n) -> p k n", k=8)
                sc.op("dve", lambda e, xv=xv: e.tensor_tensor(out=xv, in0=pt, in1=Amod[:, :, 0:1].to_broadcast([128, 8, 128]), op=ALU.mult),
                      reads=[B_pt, B_Amod], writes=[B_xn[sl]])
                sc.op("dve", lambda e, xv=xv, t=t: e.tensor_tensor(out=hT[hs][:, :, t * 128:(t + 1) * 128], in0=xv,
                                                                   in1=modT[:, 0:8, 0:1].to_broadcast([128, 8, 128]), op=ALU.add),
                      reads=[B_xn[sl], B_modT], writes=[B_hT[hs]])
            for (c0, c1, ctx) in (grp_ctx(gi) if gi >= 16 else []):
                for kc in range(8):
                    sc.op("dve", lambda e, kc=kc, c0=c0, c1=c1, ctx=ctx, t=t: e.tensor_scalar(
                        out=hT[hs][:, kc, t * 128 + c0:t * 128 + c1], in0=pt[:, kc, c0:c1],
                        scalar1=Amod[:, kc, ctx:ctx + 1], scalar2=modT[:, kc, ctx:ctx + 1],
                        op0=ALU.mult, op1=ALU.add),
                        reads=[B_pt, B_Amod, B_modT], writes=[B_hT[hs]])

        def feat(gi, f):
            hs = gi % 2
            ntile = grp_ntile(gi)
            ncol = ntile * 128
            tok0 = gi * 512
            pfi = f % 2
            for kc in range(8):
                sc.op("pe", lambda e, f=f, kc=kc, pfi=pfi: e.matmul(pf[pfi][:, 0:ncol], lhsT=wc_s[:, kc, f * 128:(f + 1) * 128],
                                                                    rhs=hT[hs][:, kc, 0:ncol], start=(kc == 0), stop=(kc == 7)),
                      reads=[B_wc, B_hT[hs]], writes=[B_pf[pfi]])
            dst = feat_sb[f]; Bd = feat_B[f]
            if f in (2, 5):
                sc.op("act", lambda e, pfi=pfi: e.activation(out=tmpn[:, 0:ncol], in_=pf[pfi][:, 0:ncol], func=AF.Exp, scale=-1.0),
                      reads=[B_pf[pfi]], writes=[B_tmpn])
                sc.op("act", lambda e: e.activation(out=tmpn[:, 0:ncol], in_=tmpn[:, 0:ncol], func=AF.Ln, bias=epsc[:, 3:4]),
                      reads=[B_tmpn, B_epsc], writes=[B_tmpn])
                sc.op("act", lambda e: e.activation(out=tmpn[:, 0:ncol], in_=tmpn[:, 0:ncol], func=AF.Exp, scale=-1.0),
                      reads=[B_tmpn], writes=[B_tmpn])
                sc.op("dve", lambda e, pfi=pfi, dst=dst: e.tensor_tensor(out=dst[:, tok0:tok0 + ncol], in0=pf[pfi][:, 0:ncol],
                                                                         in1=tmpn[:, 0:ncol], op=ALU.mult),
                      reads=[B_pf[pfi], B_tmpn], writes=[Bd])
                return
            gcol = {0: 0, 1: 1, 3: 2, 4: 3}[f]
            ni = nrm_ctr[0] % 2
            nrm_ctr[0] += 1
            sqx, B_sqx, rsx, B_rsx = sqb[ni], B_sqb[ni], rs2[ni], B_rs2[ni]
            sc.op("act", lambda e, pfi=pfi: e.activation(out=sqx[:, 0:ncol], in_=pf[pfi][:, 0:ncol], func=AF.Square),
                  reads=[B_pf[pfi]], writes=[B_sqx])
            later.append(lambda: feat_b(gi, f, ni))

        def feat_b(gi, f, ni):
            hs = gi % 2
            ntile = grp_ntile(gi)
            ncol = ntile * 128
            tok0 = gi * 512
            pfi = f % 2
            dst = feat_sb[f]; Bd = feat_B[f]
            gcol = {0: 0, 1: 1, 3: 2, 4: 3}[f]
            sqx, B_sqx, rsx, B_rsx = sqb[ni], B_sqb[ni], rs2[ni], B_rs2[ni]
            sc.op("pe", lambda e: e.matmul(pn[:, 0:ncol], lhsT=bonesb[:], rhs=sqx[:, 0:ncol], start=True, stop=True),
                  reads=[B_bonesb, B_sqx], writes=[B_pn])
            sc.op("act", lambda e: e.activation(out=rsx[:, 0:ncol], in_=pn[:, 0:ncol], func=AF.Ln, bias=epsc[:, 1:2]),
                  reads=[B_pn, B_epsc], writes=[B_rsx])
            sc.op("act", lambda e: e.activation(out=rsx[:, 0:ncol], in_=rsx[:, 0:ncol], func=AF.Exp, scale=-0.5),
                  reads=[B_rsx], writes=[B_rsx])
            if f in (0, 3):
                sc.op("dve", lambda e, pfi=pfi, dst=dst, gcol=gcol: e.scalar_tensor_tensor(
                    out=dst[:, tok0:tok0 + ncol], in0=pf[pfi][:, 0:ncol], scalar=gsc[:, gcol:gcol + 1], in1=rsx[:, 0:ncol],
                    op0=ALU.mult, op1=ALU.mult), reads=[B_pf[pfi], B_rsx, B_gsc], writes=[Bd])
                return
            sc.op("dve", lambda e, pfi=pfi, gcol=gcol: e.scalar_tensor_tensor(
                out=kf32[:, 0:ncol], in0=pf[pfi][:, 0:ncol], scalar=gsc[:, gcol:gcol + 1], in1=rsx[:, 0:ncol],
                op0=ALU.mult, op1=ALU.mult), reads=[B_pf[pfi], B_rsx, B_gsc], writes=[B_kf32])
            sc.op("pool", lambda e, dst=dst: e.tensor_copy(out=dst[:, tok0:tok0 + ncol], in_=kf32[:, 0:ncol]),
                  reads=[B_kf32], writes=[Bd])
            need_out = (f == 1) or (gi >= 15)
            if need_out:
                later2.append(lambda: feat_c(gi, f))

        def feat_c(gi, f):
            ntile = grp_ntile(gi)
            tok0 = gi * 512
            if True:
                ko = kout_ctr[0] % 2
                kout_ctr[0] += 1
                for t in range(ntile):
                    sc.op("pe", lambda e, t=t: e.transpose(out=pk[:, t, :], in_=kf32[:, t * 128:(t + 1) * 128],
                                                           identity=ident_s[:]),
                          reads=[B_kf32, B_ident], writes=[B_pk])
                sc.op("dve", lambda e, ko=ko: e.tensor_copy(out=kout[ko][:, 0:ntile, :], in_=pk[:, 0:ntile, :]),
                      reads=[B_pk], writes=[B_kout[ko]])
                if gi < 16:
                    if f == 1:
                        dstd = ak[tok0:tok0 + 512, :].rearrange("(t p) e -> p t e", p=128); Bo = B_ak
                    else:
                        dstd = bk.rearrange("(t p) e -> p t e", p=128); Bo = B_bk
                    sc.op("pool", lambda e, ko=ko, dstd=dstd: e.dma_start(out=dstd, in_=kout[ko][:]),
                          reads=[B_kout[ko]], writes=[Bo], dma=True)
                else:
                    dstd = aks if f == 1 else bks
                    Bo = B_ak if f == 1 else B_bk
                    sc.op("pool", lambda e, ko=ko, dstd=dstd: e.dma_start(out=dstd, in_=kout[ko][:, 0, :]),
                          reads=[B_kout[ko]], writes=[Bo], dma=True)

        def vtile(gi, t):
            hs = gi % 2
            tg = gi * 4 + t
            pvi = tg % 2
            for kc in range(8):
                sc.op("pe", lambda e, t=t, kc=kc, pvi=pvi: e.matmul(pv[pvi][:, 0:256], lhsT=hT[hs][:, kc, t * 128:(t + 1) * 128],
                                                                    rhs=wc_s[:, kc, 768:1024], start=(kc == 0), stop=(kc == 7)),
                      reads=[B_wc, B_hT[hs]], writes=[B_pv[pvi]])
            sc.op("dve", lambda e, pvi=pvi: e.tensor_copy(out=vst[pvi][:], in_=pv[pvi][:, 0:256]),
                  reads=[B_pv[pvi]], writes=[B_vst[pvi]])
            sc.op("pool", lambda e, pvi=pvi, tg=tg: e.tensor_copy(out=va[:, tg, :], in_=vst[pvi][:, 0:128]),
                  reads=[B_vst[pvi]], writes=[B_va])
            sc.op("pool", lambda e, pvi=pvi, tg=tg: e.tensor_copy(out=vb[:, tg, :], in_=vst[pvi][:, 128:256]),
                  reads=[B_vst[pvi]], writes=[B_vb])
            if tg < 64:
                sc.op("pool", lambda e, pvi=pvi, tg=tg: e.dma_start(out=av[tg * 128:(tg + 1) * 128, :], in_=vst[pvi][:, 0:128]),
                      reads=[B_vst[pvi]], writes=[B_av], dma=True)
                if tg >= 60:
                    sc.op("pool", lambda e, pvi=pvi, tg=tg: e.dma_start(out=bv[(tg - 60) * 128:(tg - 59) * 128, :],
                                                                        in_=vst[pvi][:, 128:256]),
                          reads=[B_vst[pvi]], writes=[B_bv], dma=True)
            else:
                sc.op("pool", lambda e, pvi=pvi: e.dma_start(out=avs, in_=vst[pvi][:, 0:128]),
                      reads=[B_vst[pvi]], writes=[B_av], dma=True)
                sc.op("pool", lambda e, pvi=pvi: e.dma_start(out=bvs, in_=vst[pvi][:, 128:256]),
                      reads=[B_vst[pvi]], writes=[B_bv], dma=True)

        import os
        NG = int(os.environ.get("DBG_NG", "16"))
        NGRP = NG + 1 if NG == 16 else NG
        tiles = [(gi, t) for gi in range(NGRP) for t in range(grp_ntile(gi))]
        tidx = {tl: i for i, tl in enumerate(tiles)}
        a_done = [0]

        def ensure_a(n):
            while a_done[0] < min(n, len(tiles)):
                prep_a(*tiles[a_done[0]])
                a_done[0] += 1

        for t in range(grp_ntile(0)):
            ensure_a(tidx[(0, t)] + 2)
            prep_b(0, t)
        for gi in range(NGRP):
            n_next = grp_ntile(gi + 1) if gi + 1 < NGRP else 0
            for i in range(6):
                if i < n_next:
                    ensure_a(tidx[(gi + 1, i)] + 2)
                feat(gi, i)
                while later2:
                    later2.pop(0)()
                if i < grp_ntile(gi):
                    vtile(gi, i)
                if i < n_next:
                    prep_b(gi + 1, i)
                while later:
                    later.pop(0)()
        while later2:
            later2.pop(0)()

        class _Stop(Exception):
            pass
        STAGE = int(os.environ.get("DBG_STAGE", "99"))

        def stage(k):
            if STAGE <= k:
                raise _Stop()
        try:
          if NG == 16:
            stage(1)
            def fence(newb, oldbs):
                for ob in oldbs:
                    if ob.w is not None:
                        newb.r.append(ob.w)
                    newb.r.extend(ob.r)
                return newb

            rs_srcA1_2 = nc.dram_tensor("rs_srcA1", [4 * 4 * 128, 1024], BF16).ap()
            rs_dstA1_2 = nc.dram_tensor("rs_dstA1", [4 * 128, 1024], BF16).ap()
            rs_srcA2_2 = nc.dram_tensor("rs_srcA2", [4 * 4 * 128, 1056], BF16).ap()
            rs_dstA2_2 = nc.dram_tensor("rs_dstA2", [4 * 128, 1056], BF16).ap()
            rs_srcA1 = rs_srcA1_2.rearrange("(j s p) c -> j s p c", j=4, s=4)
            rs_srcA2 = rs_srcA2_2.rearrange("(j s p) c -> j s p c", j=4, s=4)
            B_srcA1 = sc.buf("rs_srcA1"); B_dstA1 = sc.buf("rs_dstA1"); B_srcA2 = sc.buf("rs_srcA2"); B_dstA2 = sc.buf("rs_dstA2")
            rs_srcB2 = nc.dram_tensor("rs_srcB", [4 * 4 * 128, MT], BF16).ap()
            rs_dstB2 = nc.dram_tensor("rs_dstB", [4 * 128, MT], BF16).ap()
            rs_srcs = {4: rs_srcB2.rearrange("(j s p) c -> j s p c", j=4, s=4)}
            rs_dstB = rs_dstB2.rearrange("(s p) c -> s p c", s=4)
            B_srcB = sc.buf("rs_srcB"); B_dstB = sc.buf("rs_dstB")
            B_srcs = {4: B_srcB}

            def do_rs(src2, dst2, Bs, Bd):
                if os.environ.get("DBG_NOCC"):
                    sc.op("pool", lambda e: e.dma_start(out=dst2, in_=src2[0:4 * 128, :]), reads=[Bs], writes=[Bd], dma=True)
                else:
                    sc.op("pool", lambda e: e.collective_compute("ReduceScatter", ALU.add, replica_groups=[[0, 1, 2, 3], [4, 5, 6, 7]],
                                                                 ins=[src2], outs=[dst2]), reads=[Bs], writes=[Bd], dma=False)

            msk_s = sb("msk_s", [128, 4]); B_msk = sc.buf("msk")
            c15_s = sb("c15_s", [128, 1]); B_c15 = sc.buf("c15")
            lam_s = kout[0][:].rearrange("p a b -> p (a b)")[:, 0:256].rearrange("p (a b) -> p a b", a=4)
            B_lam = fence(sc.buf("lam"), [B_kout[0]])
            lamt = sb("lamt", [128, 8]); B_lamt = sc.buf("lamt")
            onesf_s = sb("onesf_s", [128, 128]); B_onesf = sc.buf("onesf")
            onesb = sb("onesb", [128, 128], BF16); B_onesb = sc.buf("onesb")
            bAs_s = sb("bAs_s", [128, 8, 32]); B_bAs = sc.buf("bAs")
            bAsn_s = sb("bAsn_s", [128, 128]); B_bAsn = sc.buf("bAsn")

            bBs_s = sb("bBs_s", [128, 2, 4, 32]); B_bBs = sc.buf("bBs")
            bBsn_s = sb("bBsn_s", [128, 2, 128]); B_bBsn = sc.buf("bBsn")
            es_s = hT[1][:, 6, :].rearrange("p (k m q) -> p k m q", k=8, m=2)
            B_es = fence(sc.buf("es"), [B_hT[1]])
            for (dst_t, src_t, Bb) in [(msk_s, msk, B_msk), (c15_s, c15, B_c15), (lam_s, lamv, B_lam), (onesf_s, onesf, B_onesf),
                                       (bAs_s, bAs, B_bAs), (bAsn_s, bAsn, B_bAsn), (bBs_s, bBs, B_bBs),
                                       (bBsn_s, bBsn, B_bBsn)]:
                sc.op("sp", lambda e, d=dst_t, s_=src_t: e.dma_start(out=d[:], in_=s_), writes=[Bb], dma=True)
            sc.op("pool", lambda e: e.memset(onesb[:], 1.0), writes=[B_onesb])
            sc.op("dve", lambda e: e.tensor_tensor(out=lam_s[:, 0, :], in0=lam_s[:, 0, :], in1=lam_s[:, 1, :], op=ALU.mult),
                  reads=[B_lam], writes=[B_lam])
            sc.op("dve", lambda e: e.tensor_tensor(out=lam_s[:, 2, :], in0=lam_s[:, 2, :], in1=lam_s[:, 3, :], op=ALU.mult),
                  reads=[B_lam], writes=[B_lam])
            sc.op("dve", lambda e: e.reduce_sum(out=lamt[:, 0:1], in_=lam_s[:, 0, :], axis=AX.X), reads=[B_lam], writes=[B_lamt])
            sc.op("dve", lambda e: e.reduce_sum(out=lamt[:, 1:2], in_=lam_s[:, 2, :], axis=AX.X), reads=[B_lam], writes=[B_lamt])
            sc.op("act", lambda e: e.activation(out=lamt[:, 2:4], in_=lamt[:, 0:2], func=AF.Exp), reads=[B_lamt], writes=[B_lamt])
            sc.op("dve", lambda e: e.tensor_tensor(out=lamt[:, 4:5], in0=lamt[:, 3:4], in1=lamt[:, 2:3], op=ALU.subtract),
                  reads=[B_lamt], writes=[B_lamt])
            sc.op("dve", lambda e: e.tensor_scalar(out=lamt[:, 4:5], in0=lamt[:, 4:5], scalar1=-float(LAM_INIT), scalar2=None,
                                                   op0=ALU.add), reads=[B_lamt], writes=[B_lamt])
            sc.op("dve", lambda e: e.tensor_scalar(out=lamt[:, 5:6], in0=gvec_s[:, 4:5], scalar1=float(1.0 - LAM_INIT),
                                                   scalar2=None, op0=ALU.mult), reads=[B_gvec, B_lamt], writes=[B_lamt])

            B_e = [fence(sc.buf("e0"), [B_hT[0]]), fence(sc.buf("e1"), [B_hT[0]]), fence(sc.buf("e2"), [B_hT[1]])]
            e_t = [hT[0][:, 0:2, :], hT[0][:, 2:4, :], hT[1][:, 4:6, :]]
            B_EBA = fence(sc.buf("EBA"), [B_wc])
            EBA = wc_s[:, 2:5, :].rearrange("p a b -> p (a b)")[:, 0:2560].rearrange("p (v c) -> p v c", v=5)
            B_kcT = fence(sc.buf("kcT"), [B_hT[0]]); kcT = hT[0][:, 4:6, :].rearrange("p a b -> p (a b)")
            B_vc = fence(sc.buf("vc"), [B_hT[0]]); vc = hT[0][:, 6:8, :].rearrange("p a (b c) -> p (a b) c", c=128)
            B_gst = fence(sc.buf("gst"), [B_hT[1]]); gst = hT[1][:, 0:4, :]
            B_eB = fence(sc.buf("eB"), [B_wc]); eB = [wc_s[:, 0, 0:640], wc_s[:, 1, 0:640]]
            bA_v = [xsl[0][:, 0:512], xsl[0][:, 512:1024], xsl[1][:, 0:512], xsl[1][:, 512:1024], kf32[:, 0:512]]
            bA_B = [B_xsl[0], B_xsl[0], B_xsl[1], B_xsl[1], B_kf32]
            sc.op("sp", lambda e: e.dma_start(out=xsl[0][:].rearrange("p (a b) -> p a b", a=2), in_=bA[:, 0:2, :]),
                  writes=[B_xsl[0]], dma=True)
            sc.op("sp", lambda e: e.dma_start(out=xsl[1][:].rearrange("p (a b) -> p a b", a=2), in_=bA[:, 2:4, :]),
                  writes=[B_xsl[1]], dma=True)
            sc.op("sp", lambda e: e.dma_start(out=kf32[:], in_=bA[:, 4, :]), writes=[B_kf32], dma=True)
            for i5 in range(5):
                sc.op("act", lambda e, i5=i5: e.activation(out=EBA[:, i5, :], in_=bA_v[i5], func=AF.Exp),
                      reads=[bA_B[i5]], writes=[B_EBA])
            rz = xn[0][:].rearrange("p (a b) -> p a b", a=2); B_rz = B_xn[0]
            dtmp = xn[1][:].rearrange("p (a b) -> p a b", a=2); B_dtmp = B_xn[1]
            psS = [PA, PB]; B_psS = [[B_pt], [B_pf[0], B_pf[1]]]
            po = PC; B_po = [B_pn, B_pk]
            pz = PD; B_pz = [B_pv[0], B_pv[1]]
            o_b = tmpn

            def write_slots(gated_ap_fn, ncol, tok0, slot_base, Bsrcs):
                for s_ in range(4):
                    sc.op("dve", lambda e, s_=s_: gated_ap_fn(e, gst[:, s_, 0:ncol], msk_s[:, s_:s_ + 1]),
                          reads=Bsrcs + [B_msk], writes=[B_gst])
                if tok0 < S:
                    j, c0 = tok0 // 2048, tok0 % 2048
                    if slot_base == 4:
                        dsrc, Bsrc_ = rs_srcs[4][j, :, :, c0:c0 + ncol], B_srcB
                    elif c0 < 1024:
                        dsrc, Bsrc_ = rs_srcA1[j, :, :, c0:c0 + ncol], B_srcA1
                    else:
                        dsrc, Bsrc_ = rs_srcA2[j, :, :, c0 - 1024:c0 - 1024 + ncol], B_srcA2
                    sc.op("pool", lambda e: e.dma_start(out=dsrc.rearrange("s p c -> p s c"), in_=gst[:, :, 0:ncol]),
                          reads=[B_gst], writes=[Bsrc_], dma=True)
                    if slot_base == 0 and tok0 == 13 * 512:
                        do_rs(rs_srcA1_2, rs_dstA1_2, B_srcA1, B_dstA1)
                else:
                    for i in range(4):
                        if slot_base == 4:
                            dsrc, Bsrc_ = rs_srcs[4][i, :, :, 2048:2080], B_srcB
                        else:
                            dsrc, Bsrc_ = rs_srcA2[i, :, :, 1024:1056], B_srcA2
                        sc.op("pool", lambda e, i=i, dsrc=dsrc: e.dma_start(out=dsrc.rearrange("s p c -> p s c"),
                                                                         in_=gst[:, :, 32 * i:32 * i + 32]),
                              reads=[B_gst], writes=[Bsrc_], dma=True)

            def finalize_A_steps(ncol, tok0, ssq_ps, B_ssq_ps):
                def s_a():
                    sc.op("act", lambda e: e.activation(out=dtmp[:, :, 0:ncol], in_=pz[:, :, 0:ncol], func=AF.Ln), reads=B_pz, writes=[B_dtmp])
                    sc.op("act", lambda e: e.activation(out=dtmp[:, :, 0:ncol], in_=dtmp[:, :, 0:ncol], func=AF.Exp, scale=-1.0),
                          reads=[B_dtmp], writes=[B_dtmp])
                    sc.op("dve", lambda e: e.tensor_tensor(out=sq[:, 0:ncol], in0=po[:, 0, 0:ncol], in1=dtmp[:, 0, 0:ncol], op=ALU.mult),
                          reads=B_po + [B_dtmp], writes=[B_sq])
                    sc.op("dve", lambda e: e.tensor_tensor(out=rs[:, 0:ncol], in0=po[:, 1, 0:ncol], in1=dtmp[:, 1, 0:ncol], op=ALU.mult),
                          reads=B_po + [B_dtmp], writes=[B_rs])

                def s_b():
                    sc.op("dve", lambda e: e.scalar_tensor_tensor(out=o_b[:, 0:ncol], in0=rs[:, 0:ncol], scalar=lamt[:, 4:5],
                                                                  in1=sq[:, 0:ncol], op0=ALU.mult, op1=ALU.add),
                          reads=[B_rs, B_sq, B_lamt], writes=[B_tmpn])

                def s_c():
                    sc.op("act", lambda e: e.activation(out=sq[:, 0:ncol], in_=o_b[:, 0:ncol], func=AF.Square),
                          reads=[B_tmpn], writes=[B_sq])

                def s_d():
                    sc.op("pe", lambda e: e.matmul(ssq_ps[:, 0:ncol], lhsT=onesf_s[:], rhs=sq[:, 0:ncol], start=True, stop=True),
                          reads=[B_onesf, B_sq], writes=B_ssq_ps)

                def s_e():
                    sc.op("act", lambda e: e.activation(out=rs[:, 0:ncol], in_=ssq_ps[:, 0:ncol], func=AF.Ln, bias=epsc[:, 2:3]),
                          reads=B_ssq_ps + [B_epsc], writes=[B_rs])
                    sc.op("act", lambda e: e.activation(out=rs[:, 0:ncol], in_=rs[:, 0:ncol], func=AF.Exp, scale=-0.5),
                          reads=[B_rs], writes=[B_rs])

                def s_f():
                    sc.op("dve", lambda e: e.tensor_tensor(out=o_b[:, 0:ncol], in0=o_b[:, 0:ncol], in1=rs[:, 0:ncol], op=ALU.mult),
                          reads=[B_tmpn, B_rs], writes=[B_tmpn])
                    sc.op("dve", lambda e: e.scalar_tensor_tensor(out=sq[:, 0:ncol], in0=o_b[:, 0:ncol], scalar=lamt[:, 5:6],
                                                                  in1=gaT[:, tok0:tok0 + ncol], op0=ALU.mult, op1=ALU.mult),
                          reads=[B_tmpn, B_lamt, B_gaT], writes=[B_sq])

                def s_g():
                    write_slots(lambda e, o_ap, m_ap: e.tensor_scalar(out=o_ap, in0=sq[:, 0:ncol], scalar1=m_ap, scalar2=float(math.sqrt(128.0)),
                                                                      op0=ALU.mult, op1=ALU.mult), ncol, tok0, 0, [B_sq])
                return [s_a, s_b, s_c, s_d, s_e, s_f, s_g]

            def finalize_A(ncol, tok0, ssq_ps, B_ssq_ps):
                for st_ in finalize_A_steps(ncol, tok0, ssq_ps, B_ssq_ps):
                    st_()

            TS0 = 8192
            tmpB = [xn[0], xn[1]]; B_tmpB = [B_xn[0], B_xn[1]]
            psB = [PA[:].rearrange("p a b -> p (a b)"), PB[:].rearrange("p a b -> p (a b)")]
            gB = sq

            poB2 = [PC[:, 0, :], PC[:, 1, :]]; B_poB2 = [B_pn, B_pk]
            pzB2 = [PD[:, 0, :], PD[:, 1, :]]; B_pzB2 = [B_pv[0], B_pv[1]]
            EBB = [wc_s[:, 5, 0:640], wc_s[:, 6, 0:640]]; B_EBB = fence(sc.buf("EBB"), [B_wc])
            B_eBa = [fence(sc.buf("eB0"), [B_wc, B_eB]), fence(sc.buf("eB1"), [B_wc, B_eB])]
            for a in range(2):
                sc.op("sp", lambda e, a=a: e.dma_start(out=xsl[a][:, 0:640], in_=bB[:, a, :]), writes=[B_xsl[a]], dma=True)
                sc.op("act", lambda e, a=a: e.activation(out=EBB[a], in_=xsl[a][:, 0:640], func=AF.Exp), reads=[B_xsl[a]], writes=[B_EBB])

            def finalize_B(ncol, c0, tok0, pi=0):
                sc.op("dve", lambda e: e.reciprocal(out=rs[:, 0:ncol], in_=pzB2[pi][:, 0:ncol]), reads=[B_pzB2[pi]], writes=[B_rs])
                sc.op("dve", lambda e: e.tensor_tensor(out=rs[:, 0:ncol], in0=poB2[pi][:, 0:ncol], in1=rs[:, 0:ncol], op=ALU.mult),
                      reads=[B_poB2[pi], B_rs], writes=[B_rs])
                sc.op("dve", lambda e: e.tensor_tensor(out=gB[:, c0:c0 + ncol], in0=rs[:, 0:ncol], in1=gbT[:, tok0:tok0 + ncol],
                                                       op=ALU.mult), reads=[B_rs, B_gbT], writes=[B_sq])

            eBd = [[eB[0], e_t[0].rearrange("p a b -> p (a b)")[:, 0:640]], [eB[1], e_t[1].rearrange("p a b -> p (a b)")[:, 0:640]]]
            B_eBd = [[B_eBa[0], B_e[0]], [B_eBa[1], B_e[1]]]

            def b_qk(qt):
                d0 = max(0, 4 - qt)
                for d in range(d0, 5):
                    kb = qt + d - 4
                    for a in range(2):
                        sc.op("pe", lambda e, a=a, d=d, kb=kb, qt=qt: e.matmul(
                            psB[a][:, d * 128:(d + 1) * 128], lhsT=kbT[64 * a:64 * a + 64, kb * 128:(kb + 1) * 128],
                            rhs=qbT[64 * a:64 * a + 64, qt * 128:(qt + 1) * 128], start=True, stop=True),
                            reads=[B_kbT, B_qbT], writes=B_psS[a])

            def b_exp(qt, a):
                d0 = max(0, 4 - qt)
                eb, Beb = eBd[a][qt % 2], B_eBd[a][qt % 2]
                sc.op("act", lambda e, a=a, d0=d0, eb=eb: e.activation(out=eb[:, d0 * 128:640], in_=psB[a][:, d0 * 128:640], func=AF.Exp),
                      reads=B_psS[a], writes=[Beb])
                sc.op("dve", lambda e, a=a, d0=d0, eb=eb: e.tensor_tensor(out=eb[:, d0 * 128:640], in0=eb[:, d0 * 128:640],
                                                                          in1=EBB[a][:, d0 * 128:640], op=ALU.mult),
                      reads=[Beb, B_EBB], writes=[Beb])

            def b_pv(qt, a):
                d0 = max(0, 4 - qt)
                pi = qt % 2
                eb, Beb = eBd[a][qt % 2], B_eBd[a][qt % 2]
                for d in range(d0, 5):
                    kb = qt + d - 4
                    sc.op("pe", lambda e, a=a, d=d, kb=kb, d0=d0, pi=pi, eb=eb: e.matmul(
                        poB2[pi][64 * a:64 * a + 64, 0:128], lhsT=vb[:, kb, 64 * a:64 * a + 64], rhs=eb[:, d * 128:(d + 1) * 128],
                        start=(d == d0), stop=(d == 4)), reads=[B_vb, Beb], writes=[B_poB2[pi]])
                    sc.op("pe", lambda e, a=a, d=d, d0=d0, pi=pi, eb=eb: e.matmul(
                        pzB2[pi][64 * a:64 * a + 64, 0:128], lhsT=onesb[:, 0:64], rhs=eb[:, d * 128:(d + 1) * 128],
                        start=(d == d0), stop=(d == 4)), reads=[B_onesb, Beb], writes=[B_pzB2[pi]])

            def b_fin(qt):
                finalize_B(128, (qt % 4) * 128, qt * 128, qt % 2)
                if qt % 4 == 3:
                    write_slots(lambda e, o_ap, m_ap: e.tensor_scalar(out=o_ap, in0=gB[:, 0:512], scalar1=m_ap, scalar2=None,
                                                                      op0=ALU.mult), 512, (qt - 3) * 128, 4, [B_sq])

            b_qk(0)
            for qt in range(64):
                for a in range(2):
                    b_exp(qt, a)
                if qt + 1 < 64:
                    b_qk(qt + 1)
                if qt >= 1:
                    b_fin(qt - 1)
                for a in range(2):
                    b_pv(qt, a)
            b_fin(63)
            poB = poB2[0]; pzB = pzB2[0]
            B_eB = fence(B_eB, B_eBa)

            for a in range(2):
                sc.op("pe", lambda e, a=a: e.matmul(psB[a][:, 0:128], lhsT=kbT[64 * a:64 * a + 64, TS0:TS0 + 128],
                                                    rhs=qbT[64 * a:64 * a + 64, TS0:TS0 + 128], start=True, stop=True),
                      reads=[B_kbT, B_qbT], writes=B_psS[a])
                sc.op("dve", lambda e, a=a: e.tensor_tensor(out=tmpB[a][:, 0:128], in0=psB[a][:, 0:128], in1=bBsn_s[:, a, :], op=ALU.add),
                      reads=B_psS[a] + [B_bBsn], writes=[B_tmpB[a]])
                sc.op("act", lambda e, a=a: e.activation(out=eB[a][:, 0:128], in_=tmpB[a][:, 0:128], func=AF.Exp),
                      reads=[B_tmpB[a]], writes=[B_eB])
            for a in range(2):
                sc.op("pe", lambda e, a=a: e.matmul(poB[64 * a:64 * a + 64, 0:128], lhsT=vb[:, 64, 64 * a:64 * a + 64], rhs=eB[a][:, 0:128],
                                                    start=True, stop=False), reads=[B_vb, B_eB], writes=[B_pn])
                sc.op("pe", lambda e, a=a: e.matmul(pzB[64 * a:64 * a + 64, 0:128], lhsT=onesb[:, 0:64], rhs=eB[a][:, 0:128],
                                                    start=True, stop=False), reads=[B_onesb, B_eB], writes=[B_pv[0]])
            for i in range(4):
                sc.op("sp", lambda e, i=i: e.dma_start(out=xsl[0][:, 0:512].rearrange("p (t c) -> p t c", c=128),
                                                       in_=cbk[i].rearrange("(t p) c -> p t c", p=128)), writes=[B_xsl[0]], dma=True)
                sc.op("sp", lambda e, i=i: e.dma_start(out=xsl[1][:, 0:512].rearrange("p (t c) -> p t c", c=128),
                                                       in_=cbv[i].rearrange("(t p) c -> p t c", p=128)), writes=[B_xsl[1]], dma=True)
                for t in range(4):
                    sc.op("pe", lambda e, t=t: e.transpose(out=pt[:, t, :], in_=xsl[0][:, t * 128:(t + 1) * 128], identity=ident_s[:]),
                          reads=[B_xsl[0], B_ident], writes=[B_pt])
                sc.op("act", lambda e: e.copy(out=kcT[:, 0:512], in_=PA[:, 0, :]), reads=[B_pt], writes=[B_kcT])
                sc.op("dve", lambda e: e.tensor_copy(out=vc[:, 0:4, :], in_=xsl[1][:, 0:512].rearrange("p (t c) -> p t c", c=128)),
                      reads=[B_xsl[1]], writes=[B_vc])
                pbs = [PB[:, a, 0:128].rearrange("p (k q) -> p k q", k=4) for a in range(2)]
                for kb in range(4):
                    for a in range(2):
                        sc.op("pe", lambda e, kb=kb, a=a, i=i: e.matmul(
                            pbs[a][:, kb, :], lhsT=kcT[64 * a:64 * a + 64, kb * 128:(kb + 1) * 128],
                            rhs=qbT[64 * a:64 * a + 64, TS0 + 32 * i:TS0 + 32 * i + 32], start=True, stop=True),
                            reads=[B_kcT, B_qbT], writes=[B_pf[a]])
                stb = kf32[:, 0:256].rearrange("p (a k q) -> p a k q", a=2, k=4)
                esb = es_s[:].rearrange("p k m q -> p (k m q)")[:, 0:256].rearrange("p (a k q) -> p a k q", a=2, k=4)
                for a in range(2):
                    sc.op("dve", lambda e, a=a: e.tensor_tensor(out=stb[:, a, :, :], in0=pbs[a], in1=bBs_s[:, a, :, :], op=ALU.add),
                          reads=[B_pf[a], B_bBs], writes=[B_kf32])
                sc.op("act", lambda e: e.activation(out=esb, in_=stb, func=AF.Exp), reads=[B_kf32], writes=[B_es])
                for kb in range(4):
                    for a in range(2):
                        sc.op("pe", lambda e, kb=kb, a=a, i=i: e.matmul(
                            poB[64 * a:64 * a + 64, 32 * i:32 * i + 32], lhsT=vc[:, kb, 64 * a:64 * a + 64], rhs=esb[:, a, kb, :],
                            start=False, stop=(i == 3 and kb == 3)), reads=[B_vc, B_es], writes=[B_pn])
                        sc.op("pe", lambda e, kb=kb, a=a, i=i: e.matmul(
                            pzB[64 * a:64 * a + 64, 32 * i:32 * i + 32], lhsT=onesb[:, 0:64], rhs=esb[:, a, kb, :],
                            start=False, stop=(i == 3 and kb == 3)), reads=[B_onesb, B_es], writes=[B_pv[0]])
            finalize_B(128, 0, TS0)
            write_slots(lambda e, o_ap, m_ap: e.tensor_scalar(out=o_ap, in0=gB[:, 0:128], scalar1=m_ap, scalar2=None, op0=ALU.mult),
                        128, TS0, 4, [B_sq])

            do_rs(rs_srcB2, rs_dstB2, B_srcB, B_dstB)
            B_wout = fence(sc.buf("wout"), [B_qbT]); wout = qbT[:, 0:8192].rearrange("p (k n) -> p k n", k=8)
            B_wmg = fence(sc.buf("wmg"), [B_kbT, B_gbT])
            wmg = [kbT[:, 0:8192].rearrange("p (k n) -> p k n", k=4), gbT[:, 0:8192].rearrange("p (k n) -> p k n", k=4)]
            wpieces = []
            for kc in range(8):
                wpieces.append(lambda kc=kc: sc.op("pool", lambda e: e.dma_start(out=wout[:, kc, :], in_=w_out[kc * 128:(kc + 1) * 128, :]),
                                                   writes=[B_wout], dma=True))
            for kc in range(8):
                for half in range(2):
                    wpieces.append(lambda kc=kc, half=half: sc.op("pool", lambda e: e.dma_start(
                        out=wmg[kc // 4][:, kc % 4, half * 1024:(half + 1) * 1024],
                        in_=w_mg[kc * 128:(kc + 1) * 128, half * 1024:(half + 1) * 1024]), writes=[B_wmg], dma=True))

            units = [(g, kb) for g in range(16) for kb in range(4 * g + 4)]
            zacc = xn[0][:].rearrange("p (a b) -> p a b", a=2); B_zacc = B_xn[0]

            def emit_qk(u):
                g, kb = units[u]
                si = u % 2
                for m_ in range(2):
                    sc.op("pe", lambda e, m_=m_, si=si, kb=kb, g=g: e.matmul(
                        psS[si][:, m_, :], lhsT=kaT[64 * m_:64 * m_ + 64, kb * 128:(kb + 1) * 128],
                        rhs=qaT[64 * m_:64 * m_ + 64, g * 512:(g + 1) * 512], start=True, stop=True),
                        reads=[B_kaT, B_qaT], writes=B_psS[si])

            POOLEVERY = int(os.environ.get("POOLEVERY", "4"))
            zacc2 = xn[1][:].rearrange("p (a b) -> p a b", a=2); B_zacc2 = B_xn[1]
            zb2 = hT[0][:, 4:6, :]; B_zb2 = B_kcT
            pending = []
            zb = hT[1][:, 6:8, :]; B_zb = B_es

            def emit_exp(u):
                g, kb = units[u]
                si = u % 2
                ei = u % 3
                iv = kb - 4 * g
                if iv >= -1:
                    sc.op("act", lambda e, si=si, ei=ei: e.activation(out=e_t[ei], in_=psS[si][:], func=AF.Exp),
                          reads=B_psS[si], writes=[B_e[ei]])
                    for m_ in range(2):
                        sc.op("dve", lambda e, m_=m_, ei=ei, iv=iv: e.tensor_tensor(out=e_t[ei][:, m_, :], in0=e_t[ei][:, m_, :],
                                                                                    in1=EBA[:, iv + 1, :], op=ALU.mult),
                              reads=[B_e[ei], B_EBA], writes=[B_e[ei]])
                else:
                    sc.op("act", lambda e, si=si, ei=ei: e.activation(out=e_t[ei], in_=psS[si][:], func=AF.Exp, bias=c15_s[:, 0:1]),
                          reads=B_psS[si] + [B_c15], writes=[B_e[ei]])

            def emit_pv(u):
                g, kb = units[u]
                ei = u % 3
                nkb = 4 * g + 4
                for m_ in range(2):
                    sc.op("pe", lambda e, m_=m_, ei=ei, kb=kb, nkb=nkb: e.matmul(
                        po[:, m_, :], lhsT=va[:, kb, :], rhs=e_t[ei][:, m_, :], start=(kb == 0), stop=(kb == nkb - 1)),
                        reads=[B_va, B_e[ei]], writes=B_po)
                pool_units = [i for i in range(nkb - 1) if i % POOLEVERY == POOLEVERY - 2 and i > 0] if POOLEVERY else []
                if kb in pool_units:
                    first, last = (kb == pool_units[0]), (kb == pool_units[-1])
                    dst_ap, Bdst = (zb2, B_zb2) if last else (zacc2, B_zacc2)
                    if first:
                        sc.op("pool", lambda e, ei=ei, dst_ap=dst_ap: e.tensor_copy(out=dst_ap, in_=e_t[ei]), reads=[B_e[ei]], writes=[Bdst])
                    else:
                        sc.op("pool", lambda e, ei=ei, dst_ap=dst_ap: e.tensor_tensor(out=dst_ap, in0=zacc2, in1=e_t[ei], op=ALU.add),
                              reads=[B_e[ei], B_zacc2], writes=[Bdst])
                elif kb == 0:
                    sc.op("dve", lambda e, ei=ei: e.tensor_copy(out=zacc, in_=e_t[ei]), reads=[B_e[ei]], writes=[B_zacc])
                elif kb == nkb - 1:
                    sc.op("dve", lambda e, ei=ei: e.tensor_tensor(out=zb, in0=zacc, in1=e_t[ei], op=ALU.add),
                          reads=[B_e[ei], B_zacc], writes=[B_zb])
                else:
                    sc.op("dve", lambda e, ei=ei: e.tensor_tensor(out=zacc, in0=zacc, in1=e_t[ei], op=ALU.add),
                          reads=[B_e[ei], B_zacc], writes=[B_zacc])
                if kb == nkb - 1:
                    while pending:
                        pending.pop(0)()
                    def pz_mm():
                        for m_ in range(2):
                            sc.op("pe", lambda e, m_=m_: e.matmul(pz[:, m_, :], lhsT=onesb[:], rhs=zb[:, m_, :], start=True, stop=not POOLEVERY),
                                  reads=[B_onesb, B_zb], writes=B_pz)
                            if POOLEVERY:
                                sc.op("pe", lambda e, m_=m_: e.matmul(pz[:, m_, :], lhsT=onesb[:], rhs=zb2[:, m_, :], start=False, stop=True),
                                      reads=[B_onesb, B_zb2], writes=B_pz)
                    hold["pz"] = pz_mm
                    steps = finalize_A_steps(512, g * 512, pz[:, 0, :], [B_pz[0]])
                    hold["sa"] = steps[0]
                    hold["n"] = 2
                    for st_ in steps[1:]:
                        pending.append(st_)
                        pending.append(lambda: None)
                elif pending:
                    pending.pop(0)()

            hold = {"n": 0, "sa": None, "pz": None}
            held = []
            emit_qk(0)
            for u in range(len(units)):
                if u + 1 < len(units):
                    emit_qk(u + 1)
                emit_exp(u)
                if u >= 60 and u % 16 == 0 and wpieces:
                    wpieces.pop(0)()
                if hold["pz"] is not None:
                    hold["pz"]()
                    hold["pz"] = None
                if hold["n"] > 0:
                    held.append(u)
                    hold["n"] -= 1
                    if hold["n"] == 0:
                        hold["sa"]()
                        hold["sa"] = None
                        for v in held:
                            emit_pv(v)
                        held = []
                else:
                    emit_pv(u)
            if hold["pz"] is not None:
                hold["pz"]()
            if hold["sa"] is not None:
                hold["sa"]()
            while pending:
                pending.pop(0)()

            while wpieces:
                wpieces.pop(0)()
            TS0 = 8192
            for m_ in range(2):
                sc.op("pe", lambda e, m_=m_: e.matmul(psS[0][:, m_, 0:128], lhsT=kaT[64 * m_:64 * m_ + 64, TS0:TS0 + 128],
                                                      rhs=qaT[64 * m_:64 * m_ + 64, TS0:TS0 + 128], start=True, stop=True),
                      reads=[B_kaT, B_qaT], writes=B_psS[0])
                sc.op("dve", lambda e, m_=m_: e.tensor_tensor(out=dtmp[:, m_, 0:128], in0=psS[0][:, m_, 0:128], in1=bAsn_s[:],
                                                              op=ALU.add), reads=B_psS[0] + [B_bAsn], writes=[B_dtmp])
            sc.op("act", lambda e: e.activation(out=e_t[0][:, :, 0:128], in_=dtmp[:, :, 0:128], func=AF.Exp),
                  reads=[B_dtmp], writes=[B_e[0]])
            for m_ in range(2):
                sc.op("pe", lambda e, m_=m_: e.matmul(po[:, m_, 0:128], lhsT=va[:, 64, :], rhs=e_t[0][:, m_, 0:128],
                                                      start=True, stop=False), reads=[B_va, B_e[0]], writes=B_po)
                sc.op("pe", lambda e, m_=m_: e.matmul(pz[:, m_, 0:128], lhsT=onesb[:], rhs=e_t[0][:, m_, 0:128],
                                                      start=True, stop=False), reads=[B_onesb, B_e[0]], writes=B_pz)
            for i in range(4):
                sc.op("sp", lambda e, i=i: e.dma_start(out=xsl[0][:].rearrange("p (t c) -> p t c", c=128),
                                                       in_=cak[i].rearrange("(t p) c -> p t c", p=128)), writes=[B_xsl[0]], dma=True)
                sc.op("sp", lambda e, i=i: e.dma_start(out=xsl[1][:].rearrange("p (t c) -> p t c", c=128),
                                                       in_=cav[i].rearrange("(t p) c -> p t c", p=128)), writes=[B_xsl[1]], dma=True)
                for t in range(8):
                    sc.op("pe", lambda e, t=t: e.transpose(out=pt[:, t, :], in_=xsl[0][:, t * 128:(t + 1) * 128], identity=ident_s[:]),
                          reads=[B_xsl[0], B_ident], writes=[B_pt])
                sc.op("act", lambda e: e.copy(out=kcT, in_=PA[:].rearrange("p a b -> p (a b)")), reads=[B_pt], writes=[B_kcT])
                sc.op("dve", lambda e: e.tensor_copy(out=vc, in_=xsl[1][:].rearrange("p (t c) -> p t c", c=128)),
                      reads=[B_xsl[1]], writes=[B_vc])
                pss = [PB[:, m2, 0:256].rearrange("p (k q) -> p k q", k=8) for m2 in range(2)]
                esv = es_s[:].rearrange("p k m q -> p (k m q)").rearrange("p (m k q) -> p m k q", m=2, k=8)
                for kb in range(8):
                    for m_ in range(2):
                        sc.op("pe", lambda e, kb=kb, m_=m_, i=i: e.matmul(
                            pss[m_][:, kb, :], lhsT=kcT[64 * m_:64 * m_ + 64, kb * 128:(kb + 1) * 128],
                            rhs=qaT[64 * m_:64 * m_ + 64, TS0 + 32 * i:TS0 + 32 * i + 32], start=True, stop=True),
                            reads=[B_kcT, B_qaT], writes=[B_pf[m_]])
                stmp = kf32[:, 0:512].rearrange("p (m k q) -> p m k q", m=2, k=8)
                for m_ in range(2):
                    sc.op("dve", lambda e, m_=m_: e.tensor_tensor(out=stmp[:, m_, :, :], in0=pss[m_], in1=bAs_s[:],
                                                                  op=ALU.add), reads=[B_pf[m_], B_bAs], writes=[B_kf32])
                sc.op("act", lambda e: e.activation(out=esv, in_=stmp, func=AF.Exp), reads=[B_kf32], writes=[B_es])
                for kb in range(8):
                    for m_ in range(2):
                        sc.op("pe", lambda e, kb=kb, m_=m_, i=i: e.matmul(
                            po[:, m_, 32 * i:32 * i + 32], lhsT=vc[:, kb, :], rhs=esv[:, m_, kb, :], start=False, stop=(i == 3 and kb == 7)),
                            reads=[B_vc, B_es], writes=B_po)
                        sc.op("pe", lambda e, kb=kb, m_=m_, i=i: e.matmul(
                            pz[:, m_, 32 * i:32 * i + 32], lhsT=onesb[:], rhs=esv[:, m_, kb, :], start=False, stop=(i == 3 and kb == 7)),
                            reads=[B_onesb, B_es], writes=B_pz)
            finalize_A(128, TS0, pz[:, 0, :], [B_pz[0]])

            B_woa = fence(sc.buf("woa"), [B_gaT]); woa = gaT[:, 0:4096].rearrange("p (k n) -> p k n", k=4)
            B_wob = fence(sc.buf("wob"), [B_gaT]); wob = gaT[:, 4096:8192].rearrange("p (k n) -> p k n", k=4)
            sc.op("pool", lambda e: e.dma_start(out=woa, in_=w_oa.rearrange("(k p) n -> p k n", p=128)), writes=[B_woa], dma=True)
            sc.op("pool", lambda e: e.dma_start(out=wob, in_=w_ob.rearrange("(k p) n -> p k n", p=128)), writes=[B_wob], dma=True)
            do_rs(rs_srcA2_2, rs_dstA2_2, B_srcA2, B_dstA2)
            B_GA1 = fence(sc.buf("GA1"), [B_qaT]); B_GA2 = fence(sc.buf("GA2"), [B_qaT])
            GA = qaT[:, 0:4 * MT].rearrange("p (s c) -> p s c", s=4)
            B_GB = fence(sc.buf("GB"), [B_kaT]); GB = kaT[:, 0:4 * MT].rearrange("p (s c) -> p s c", s=4)
            sc.op("sp", lambda e: e.dma_start(out=GA[:, :, 0:1024], in_=rs_dstA1_2.rearrange("(s p) c -> p s c", s=4)),
                  reads=[B_dstA1], writes=[B_GA1], dma=True)
            sc.op("pool", lambda e: e.dma_start(out=GA[:, :, 1024:MT], in_=rs_dstA2_2.rearrange("(s p) c -> p s c", s=4)),
                  reads=[B_dstA2], writes=[B_GA2], dma=True)
            sc.op("sp", lambda e: e.dma_start(out=GB, in_=rs_dstB.rearrange("s p c -> p s c")), reads=[B_dstB], writes=[B_GB], dma=True)
            mT = hT[0]; B_mT = B_hT[0]
            for bb in B_e + [B_kcT, B_vc]:
                fence(B_mT, [bb])
            hTm = hT[1]; B_hTm = B_hT[1]
            fence(B_hTm, [B_gst])
            B_y = sc.buf("y_out")
            out_bufs.append(B_y)
            zT = kf32; B_zT = B_kf32

            def p5_group(tok0, ncol, ctx):
                ntile = (ncol + 127) // 128
                for t in range(ntile):
                    rows = min(128, ncol - t * 128)
                    sl = t % 2
                    sc.op("sp", lambda e, sl=sl, t=t, rows=rows: e.dma_start(out=xsl[sl][0:rows, :],
                                                                            in_=xm[tok0 + t * 128:tok0 + t * 128 + rows, :]),
                          writes=[B_xsl[sl]], dma=True)
                    sc.op("act", lambda e, sl=sl, rows=rows: e.activation(out=xn[sl][0:rows, :], in_=xsl[sl][0:rows, :], func=AF.Square,
                                                                          accum_out=ssq[0:rows, sl:sl + 1]),
                          reads=[B_xsl[sl]], writes=[B_xn[sl], B_ssq[sl]])
                    sc.op("act", lambda e, sl=sl, rows=rows: e.activation(out=ssq[0:rows, sl:sl + 1], in_=ssq[0:rows, sl:sl + 1],
                                                                          func=AF.Ln, bias=epsc[0:rows, 0:1]),
                          reads=[B_ssq[sl], B_epsc], writes=[B_ssq[sl]])
                    sc.op("act", lambda e, sl=sl, rows=rows: e.activation(out=rstd[0:rows, sl:sl + 1], in_=ssq[0:rows, sl:sl + 1],
                                                                          func=AF.Exp, scale=-0.5),
                          reads=[B_ssq[sl]], writes=[B_rstd[sl]])
                    sc.op("dve", lambda e, sl=sl, rows=rows: e.tensor_scalar(out=xnb[sl][0:rows, :], in0=xsl[sl][0:rows, :],
                                                                             scalar1=rstd[0:rows, sl:sl + 1], scalar2=None, op0=ALU.mult),
                          reads=[B_xsl[sl], B_rstd[sl]], writes=[B_xnb[sl]])
                    for kc in range(8):
                        sc.op("pe", lambda e, sl=sl, kc=kc, rows=rows: e.matmul(pt[:, kc, 0:rows], lhsT=xnb[sl][0:rows, kc * 128:(kc + 1) * 128],
                                                                                rhs=identb[0:rows, 0:rows], start=True, stop=True),
                              reads=[B_xnb[sl], B_identb], writes=[B_pt])
                    for kc in range(8):
                        sc.op("dve", lambda e, kc=kc, t=t, rows=rows: e.tensor_scalar(
                            out=hTm[:, kc, t * 128:t * 128 + rows], in0=pt[:, kc, 0:rows],
                            scalar1=Amod[:, kc, ctx:ctx + 1], scalar2=modT[:, kc, ctx:ctx + 1], op0=ALU.mult, op1=ALU.add),
                            reads=[B_pt, B_Amod, B_modT], writes=[B_hTm])
                for fo in range(8):
                    par = fo % 2
                    Pab = PB if par == 0 else PA
                    B_ab = [B_pf[0], B_pf[1]] if par == 0 else [B_pt, B_pt]
                    Pmg = PC if par == 0 else PD
                    B_mg = [B_pn, B_pk] if par == 0 else [B_pv[0], B_pv[1]]
                    sqx, B_sqx, rsx, B_rsx = sq2[par], B_sq2[par], rs2[par], B_rs2[par]
                    for kc in range(4):
                        sc.op("pe", lambda e, fo=fo, kc=kc, Pab=Pab: e.matmul(Pab[:, 0, 0:ncol], lhsT=woa[:, kc, fo * 128:(fo + 1) * 128],
                                                                              rhs=GA[:, kc, tok0:tok0 + ncol], start=(kc == 0), stop=(kc == 3)),
                              reads=[B_woa, B_GA1 if tok0 < 1024 else B_GA2], writes=[B_ab[0]])
                    for kc in range(4):
                        sc.op("pe", lambda e, fo=fo, kc=kc, Pab=Pab: e.matmul(Pab[:, 1, 0:ncol], lhsT=wob[:, kc, fo * 128:(fo + 1) * 128],
                                                                              rhs=GB[:, kc, tok0:tok0 + ncol], start=(kc == 0), stop=(kc == 3)),
                              reads=[B_wob, B_GB], writes=[B_ab[1]])
                    for half in range(2):
                        for kc in range(8):
                            sc.op("pe", lambda e, fo=fo, kc=kc, half=half, Pmg=Pmg: e.matmul(
                                Pmg[:, half, 0:ncol], lhsT=wmg[kc // 4][:, kc % 4, half * 1024 + fo * 128:half * 1024 + (fo + 1) * 128],
                                rhs=hTm[:, kc, 0:ncol], start=(kc == 0), stop=(kc == 7)),
                                reads=[B_wmg, B_hTm], writes=[B_mg[half]])
                    sc.op("act", lambda e, Pmg=Pmg, sqx=sqx: e.activation(out=sqx[:, 0:ncol], in_=Pmg[:, 0, 0:ncol], func=AF.Sigmoid),
                          reads=[B_mg[0]], writes=[B_sqx])
                    sc.op("act", lambda e, Pmg=Pmg, rsx=rsx: e.activation(out=rsx[:, 0:ncol], in_=Pmg[:, 1, 0:ncol], func=AF.Sigmoid),
                          reads=[B_mg[1]], writes=[B_rsx])
                    sc.op("dve", lambda e, Pab=Pab, sqx=sqx: e.tensor_tensor(out=sqx[:, 0:ncol], in0=Pab[:, 0, 0:ncol], in1=sqx[:, 0:ncol], op=ALU.mult),
                          reads=[B_ab[0], B_sqx], writes=[B_sqx])
                    sc.op("dve", lambda e, Pab=Pab, rsx=rsx: e.tensor_tensor(out=rsx[:, 0:ncol], in0=Pab[:, 1, 0:ncol], in1=rsx[:, 0:ncol], op=ALU.mult),
                          reads=[B_ab[1], B_rsx], writes=[B_rsx])
                    sc.op("dve", lambda e, fo=fo, sqx=sqx, rsx=rsx: e.tensor_tensor(out=mT[:, fo, 0:ncol], in0=sqx[:, 0:ncol], in1=rsx[:, 0:ncol], op=ALU.add),
                          reads=[B_sqx, B_rsx], writes=[B_mT])
                if gate_ctx[0] != ctx:
                    gate_ctx[0] = ctx
                    for fo in range(8):
                        sc.op("dve", lambda e, fo=fo: e.tensor_scalar(out=vst[fo % 2][:, 0:128], in0=ident_s[:], scalar1=modT[:, 16 + fo, ctx:ctx + 1],
                                                                      scalar2=None, op0=ALU.mult), reads=[B_ident, B_modT], writes=[B_vst[fo % 2]])
                        sc.op("pe", lambda e, fo=fo: e.matmul(PD[:, fo // 4, (fo % 4) * 128:(fo % 4 + 1) * 128], lhsT=onesf_s[:], rhs=vst[fo % 2][:, 0:128],
                                                              start=True, stop=True), reads=[B_onesf, B_vst[fo % 2]], writes=[B_pv[fo // 4]])
                    for hf in range(2):
                        sc.op("dve", lambda e, hf=hf: e.tensor_copy(out=gate_bc[hf][:], in_=PD[:, hf, :]), reads=[B_pv[hf]], writes=[B_gbc[hf]])
                for t in range(ntile):
                    rows = min(128, ncol - t * 128)
                    sl = t % 2
                    sc.op("sp", lambda e, sl=sl, t=t, rows=rows: e.dma_start(out=xsl[sl][0:rows, :],
                                                                            in_=xm[tok0 + t * 128:tok0 + t * 128 + rows, :]),
                          writes=[B_xsl[sl]], dma=True)
                    for hf in range(2):
                        Pz = PD[:, hf, :]
                        B_Pz = B_pv[hf]
                        for kc in range(8):
                            sc.op("pe", lambda e, hf=hf, kc=kc, t=t, rows=rows, Pz=Pz: e.matmul(
                                Pz[0:rows, :], lhsT=mT[:, kc, t * 128:t * 128 + rows], rhs=wout[:, kc, hf * 512:(hf + 1) * 512],
                                start=(kc == 0), stop=(kc == 7)), reads=[B_wout, B_mT], writes=[B_Pz])
                        sc.op("dve", lambda e, hf=hf, sl=sl, rows=rows, Pz=Pz: e.tensor_tensor(
                            out=xn[sl][0:rows, hf * 512:(hf + 1) * 512], in0=Pz[0:rows, :], in1=gate_bc[hf][0:rows, :], op=ALU.mult),
                            reads=[B_Pz, B_gbc[hf]], writes=[B_xn[sl]])
                        sc.op("dve", lambda e, hf=hf, sl=sl, rows=rows: e.tensor_tensor(
                            out=xn[sl][0:rows, hf * 512:(hf + 1) * 512], in0=xn[sl][0:rows, hf * 512:(hf + 1) * 512],
                            in1=xsl[sl][0:rows, hf * 512:(hf + 1) * 512], op=ALU.add),
                            reads=[B_xsl[sl], B_xn[sl]], writes=[B_xn[sl]])
                    sc.op("pool", lambda e, sl=sl, t=t, rows=rows: e.dma_start(out=y[tok0 + t * 128:tok0 + t * 128 + rows, :],
                                                                              in_=xn[sl][0:rows, :]),
                          reads=[B_xn[sl]], writes=[B_y], dma=True)

            gate_ctx = [None]
            gate_bc = [tmpn, kout[1][:].rearrange("p a b -> p (a b)")]; B_gbc = [B_tmpn, fence(sc.buf("gbc1"), [B_kout[1]])]
            for gq in range(4):
                p5_group(gq * 512, 512, 0)
            p5_group(2048, 32, 5)


        except _Stop:
            pass
        sc.op("sp", lambda e: e.nop(), reads=out_bufs)
        with nc.Block() as block:
            sc.emit(block)
    return nc


_NC_CACHE = {}


def kernel(x_prompt, x_sample, cache_a_k, cache_a_v, cache_b_k, cache_b_v, c_prompt, c_sample, g_norm, w_ada, b_ada,
           w_in, g_qa, g_ka, lam_q1, lam_k1, lam_q2, lam_k2, g_subln, t5_bias, g_qb, g_kb, rel_bias_b, w_oa, w_ob, w_out):
    f32 = np.float32
    A = lambda a: np.ascontiguousarray(np.asarray(a, dtype=f32))
    x_prompt, x_sample = A(x_prompt), A(x_sample)
    w_in0 = A(w_in)[0]
    if "nc" not in _NC_CACHE:
        _NC_CACHE["nc"] = build_program()
    nc = _NC_CACHE["nc"]

    ident = np.eye(128, dtype=f32)
    bones = np.zeros((128, 128), f32); bones[:64, :64] = 1; bones[64:, 64:] = 1
    p = np.arange(128)[:, None, None]; iv = np.arange(-1, 4)[None, :, None]; jj = np.arange(512)[None, None, :]
    relA = 128 * iv + p - jj
    bktA = _t5_bucket_np(relA)
    visA = (2 * iv + p // 64) <= (jj // 64)
    mA = np.where(visA, 0.0, NEGM).astype(f32)

    p1 = np.arange(128)
    relAs = 128 * np.arange(8)[None, :, None] + p1[:, None, None] - 1024 - np.arange(32)[None, None, :]
    bktAs = _t5_bucket_np(relAs)
    same = (p1[:, None] // 32) == (p1[None, :] // 32)
    relN = (p1[:, None] % 32) - (p1[None, :] % 32)
    bktN = _t5_bucket_np(relN)
    dd = np.arange(5)[None, :, None]; jq = np.arange(128)[None, None, :]; pp = p1[:, None, None]
    relB = 128 * (dd - 4) + pp - jq
    dlt = 2 * (dd - 4) + pp // 64 - jq // 64
    visB = (dlt <= 0) & (dlt >= -8)
    idxB = np.clip(relB, -128, 128) + 128
    relBs = 128 * (np.arange(4)[None, :, None] - 4) + p1[:, None, None] - np.arange(32)[None, None, :]
    idxBs = np.clip(relBs, -128, 128) + 128
    idxBn = np.clip(relN, -128, 128) + 128
    cak_, cav_, cbk_, cbv_ = A(cache_a_k)[0], A(cache_a_v)[0], A(cache_b_k)[0], A(cache_b_v)[0]
    rbb = A(rel_bias_b)[0]

    in_maps = []
    for c in range(8):
        g, h = c // 4, c % 4
        t5h = A(t5_bias)[:, h]
        gv = np.zeros((128, 8), f32)
        gv[:, 0] = np.tile(A(g_qa)[0], 2); gv[:, 1] = np.tile(A(g_ka)[0], 2)
        gv[:, 2] = np.tile(A(g_qb)[0], 2); gv[:, 3] = np.tile(A(g_kb)[0], 2)
        gv[:, 4] = A(g_subln)[0]
        cols = lambda base, w, idx: w_in0[:, base + idx * w: base + (idx + 1) * w]
        wc = np.concatenate([cols(0, 128, h), cols(512, 128, h), cols(1536, 128, h),
                             cols(2048, 128, h), cols(2560, 128, h), cols(3584, 128, h),
                             cols(1024, 128, h), cols(3072, 128, h)], axis=1)
        cv = np.stack([A(c_prompt)[g]] + [A(c_sample)[4 * g + i] for i in range(4)] + [A(c_sample)[c]], axis=1)
        lam = np.stack([A(lam_q1)[0], A(lam_k1)[0], A(lam_q2)[0], A(lam_k2)[0]], axis=0)
        m = {
            "xp": x_prompt[g],
            "xs4": x_sample[4 * g:4 * g + 4].reshape(128, D),
            "xm": np.concatenate([x_prompt[g, 2048 * h:2048 * h + 2048], x_sample[c]], axis=0),
            "cvec": np.ascontiguousarray(cv),
            "w_ada": A(w_ada)[0],
            "b_ada": np.ascontiguousarray(A(b_ada)[0].reshape(24, 128).T),
            "g_norm": np.ascontiguousarray(A(g_norm)[0].reshape(8, 128).T),
            "w_c": np.ascontiguousarray(wc),
            "w_mg": np.ascontiguousarray(w_in0[:, 4096:6144]),
            "w_oa": A(w_oa)[0], "w_ob": A(w_ob)[0], "w_out": A(w_out)[0],
            "gvec": gv,
            "lamv": np.ascontiguousarray(np.broadcast_to(lam[None], (128, 4, 64))),
            "bA": np.ascontiguousarray(np.where(visA, t5h[bktA], NEGM).astype(f32)),
            "mA": mA,
            "msk": np.ascontiguousarray(np.broadcast_to(np.eye(4, dtype=f32)[h][None], (128, 4))),
            "cak": np.ascontiguousarray(cak_[4 * g:4 * g + 4, :, h, :]),
            "cav": np.ascontiguousarray(cav_[4 * g:4 * g + 4, :, h, :]),
            "cbk": np.ascontiguousarray(cbk_[4 * g:4 * g + 4, :, 2 * h:2 * h + 2, :].reshape(4, 512, 128)),
            "cbv": np.ascontiguousarray(cbv_[4 * g:4 * g + 4, :, 2 * h:2 * h + 2, :].reshape(4, 512, 128)),
            "bAs": np.ascontiguousarray(t5h[bktAs].astype(f32)),
            "bAsn": np.ascontiguousarray(np.where(same, t5h[bktN], NEGM).astype(f32)),
            "bB": np.ascontiguousarray(np.stack([np.where(visB, rbb[idxB, 2 * h + a], NEGM).reshape(128, 640) for a in range(2)], axis=1).astype(f32)),
            "bBs": np.ascontiguousarray(np.stack([rbb[idxBs, 2 * h + a] for a in range(2)], axis=1).astype(f32)),
            "bBsn": np.ascontiguousarray(np.stack([np.where(same, rbb[idxBn, 2 * h + a], NEGM) for a in range(2)], axis=1).astype(f32)),
            "onesf": np.ones((128, 128), f32),
            "c15": np.full((128, 1), t5h[15], f32),
            "ident": ident, "bones": bones,
        }
        in_maps.append(m)
    res = run_bass_kernel_spmd(nc, in_maps, core_ids=list(range(8)))
    R = res.results
    yp = np.zeros((2, S, D), f32); ys = np.zeros((8, 32, D), f32)
    akp = np.zeros((1, 2, S, 4, 128), f32); avp = np.zeros((1, 2, S, 4, 128), f32)
    bkp = np.zeros((1, 2, 512, 8, 64), f32); bvp = np.zeros((1, 2, 512, 8, 64), f32)
    aks = np.zeros((1, 8, 32, 4, 128), f32); avs = np.zeros((1, 8, 32, 4, 128), f32)
    bks = np.zeros((1, 8, 32, 8, 64), f32); bvs = np.zeros((1, 8, 32, 8, 64), f32)
    for c in range(8):
        g, h = c // 4, c % 4
        r = R[c]
        yy = np.asarray(r["y"], f32)
        yp[g, 2048 * h:2048 * h + 2048] = yy[:2048]
        ys[c] = yy[2048:]
        akp[0, g, :, h, :] = np.asarray(r["ak"], f32)
        avp[0, g, :, h, :] = np.asarray(r["av"], f32)
        bkp[0, g, :, 2 * h:2 * h + 2, :] = np.asarray(r["bk"], f32).reshape(512, 2, 64)
        bvp[0, g, :, 2 * h:2 * h + 2, :] = np.asarray(r["bv"], f32).reshape(512, 2, 64)
        aks[0, 4 * g:4 * g + 4, :, h, :] = np.asarray(r["aks"], f32).reshape(4, 32, 128)
        avs[0, 4 * g:4 * g + 4, :, h, :] = np.asarray(r["avs"], f32).reshape(4, 32, 128)
        bks[0, 4 * g:4 * g + 4, :, 2 * h:2 * h + 2, :] = np.asarray(r["bks"], f32).reshape(4, 32, 2, 64)
        bvs[0, 4 * g:4 * g + 4, :, 2 * h:2 * h + 2, :] = np.asarray(r["bvs"], f32).reshape(4, 32, 2, 64)
    return (yp, ys, akp, avp, bkp, bvp, aks, avs, bks, bvs)
```

```python
import math
from contextlib import ExitStack

import numpy as np
import concourse.bass as bass
import concourse.mybir as mybir
from concourse.bass_utils import run_bass_kernel_spmd

F32 = mybir.dt.float32
BF16 = mybir.dt.bfloat16
AF = mybir.ActivationFunctionType
ALU = mybir.AluOpType
AX = mybir.AxisListType

D = 1024
S = 8192
NT = 65
TOK = NT * 128
EPS = 1e-6
NEGM = -30000.0
LAM_INIT = 0.8 - 0.6 * math.exp(0.0)
MT = 2048 + 32
STOP_AFTER = None


class Buf:
    def __init__(self, name):
        self.name = name
        self.w = None
        self.r = []
        self.dsem = None
        self.dcnt = 0


class Sched:
    ENGS = ["pe", "act", "dve", "pool", "sp"]

    def __init__(self, nc, stack):
        self.nc = nc
        self.stack = stack
        self.ops = {e: [] for e in self.ENGS}
        self.esem = {e: stack.enter_context(nc.semaphore("es_" + e)) for e in self.ENGS}
        self.nsem = 5

    def buf(self, name):
        return Buf(name)

    def _dsem(self, b):
        if b.dsem is None:
            b.dsem = self.stack.enter_context(self.nc.semaphore("ds_" + b.name))
            self.nsem += 1
        return b.dsem

    def op(self, eng, fn, reads=(), writes=(), dma=False):
        deps = []
        for b in reads:
            if b.w is not None:
                deps.append(b.w)
        for b in writes:
            if b.w is not None:
                deps.append(b.w)
            deps.extend(b.r)
        k = len(self.ops[eng])
        if dma:
            tgt = writes[0] if writes else reads[0]
            sem = self._dsem(tgt)
            tgt.dcnt += 16
            ev = ("d", sem, tgt.dcnt)
        else:
            ev = ("c", eng, k)
        self.ops[eng].append(dict(fn=fn, deps=deps, ev=ev, dma=dma))
        for b in writes:
            b.w = ev
            b.r = []
        for b in reads:
            if b not in writes:
                b.r.append(ev)
        return ev

    def barrier_all(self, bufs):
        evs = []
        for b in bufs:
            if b.w is not None:
                evs.append(b.w)
            evs.extend(b.r)
        return evs

    def emit(self, block):
        need = {e: set() for e in self.ENGS}
        for e in self.ENGS:
            for o in self.ops[e]:
                for d in o["deps"]:
                    if d[0] == "c" and d[1] != e:
                        need[d[1]].add(d[2])
                    elif d[0] == "c" and d[1] == e and e != "pe":
                        need[e].add(d[2])
        rank = {}
        for e in self.ENGS:
            for i, k in enumerate(sorted(need[e])):
                rank[(e, k)] = i + 1
        esem = self.esem

        def run(e, handle):
            seen = {}
            for k, o in enumerate(self.ops[e]):
                for d in o["deps"]:
                    if d[0] == "c":
                        if d[1] == e and e == "pe":
                            continue
                        sem, val = esem[d[1]], rank[(d[1], d[2])]
                    else:
                        sem, val = d[1], d[2]
                    key = id(sem)
                    if seen.get(key, 0) >= val:
                        continue
                    seen[key] = val
                    handle.wait_ge(sem, val)
                ins = o["fn"](handle)
                if o["dma"]:
                    ins.then_inc(o["ev"][1], 16)
                elif (e, k) in rank:
                    ins.then_inc(esem[e], 1)

        @block.tensor
        def _(h):
            run("pe", h)

        @block.scalar
        def _(h):
            run("act", h)

        @block.vector
        def _(h):
            run("dve", h)

        @block.gpsimd
        def _(h):
            run("pool", h)

        @block.sync
        def _(h):
            run("sp", h)


def _t5_bucket_np(rel):
    rel = np.asarray(rel, np.int64)
    half = 16
    ret = np.where(rel > 0, half, 0)
    n = np.abs(rel)
    nf = np.maximum(n, 1).astype(np.float32)
    large = 8 + (np.log(nf / np.float32(8)) / np.float32(math.log(128 / 8)) * np.float32(8)).astype(np.int32)
    large = np.minimum(large, half - 1)
    return ret + np.where(n < 8, n, large)


def build_program():
    nc = bass.Bass("TRN2", target_bir_lowering=False)
    try:
        nc.allow_low_precision("bf16 matmul operands with fp32 accumulation (reference tolerance is bf16-level)")
    except Exception:
        pass
    try:
        nc.allow_non_contiguous_dma("small strided parameter loads")
    except Exception:
        pass

    def din(name, shape, dt=F32):
        return nc.dram_tensor(name, list(shape), dt, kind="ExternalInput").ap()

    def dout(name, shape, dt=F32):
        return nc.dram_tensor(name, list(shape), dt, kind="ExternalOutput").ap()

    xp = din("xp", [S, D])
    xs4 = din("xs4", [128, D])
    xm = din("xm", [MT, D])
    cvec = din("cvec", [D, 6])
    w_ada = din("w_ada", [D, 3 * D])
    b_ada = din("b_ada", [128, 24])
    g_norm = din("g_norm", [128, 8])
    w_c = din("w_c", [D, 1024])
    w_mg = din("w_mg", [D, 2048])
    w_oa = din("w_oa", [512, D])
    w_ob = din("w_ob", [512, D])
    w_out = din("w_out", [D, D])
    gvec = din("gvec", [128, 8])
    lamv = din("lamv", [128, 4, 64])
    bA = din("bA", [128, 5, 512])
    mA = din("mA", [128, 5, 512])
    c15 = din("c15", [128, 1])
    msk = din("msk", [128, 4])
    cak = din("cak", [4, 1024, 128]); cav = din("cav", [4, 1024, 128])
    cbk = din("cbk", [4, 512, 128]); cbv = din("cbv", [4, 512, 128])
    bAs = din("bAs", [128, 8, 32]); bAsn = din("bAsn", [128, 128])
    bB = din("bB", [128, 2, 640]); bBs = din("bBs", [128, 2, 4, 32]); bBsn = din("bBsn", [128, 2, 128])
    onesf = din("onesf", [128, 128])
    ident = din("ident", [128, 128])
    bones = din("bones", [128, 128])
    y = dout("y", [MT, D])
    ak = dout("ak", [S, 128])
    av = dout("av", [S, 128])
    bk = dout("bk", [512, 128])
    bv = dout("bv", [512, 128])
    aks = dout("aks", [128, 128])
    avs = dout("avs", [128, 128])
    bks = dout("bks", [128, 128])
    bvs = dout("bvs", [128, 128])

    with ExitStack() as st:
        sc = Sched(nc, st)

        def sb(name, shape, dt=F32):
            return st.enter_context(nc.sbuf_tensor(name, list(shape), dt))

        def ps(name, shape, dt=F32):
            return st.enter_context(nc.psum_tensor(name, list(shape), dt))

        qaT = sb("qaT", [128, TOK], BF16); B_qaT = sc.buf("qaT")
        kaT = sb("kaT", [128, TOK], BF16); B_kaT = sc.buf("kaT")
        gaT = sb("gaT", [128, TOK], BF16); B_gaT = sc.buf("gaT")
        qbT = sb("qbT", [128, TOK], BF16); B_qbT = sc.buf("qbT")
        kbT = sb("kbT", [128, TOK], BF16); B_kbT = sc.buf("kbT")
        gbT = sb("gbT", [128, TOK], BF16); B_gbT = sc.buf("gbT")
        va = sb("va", [128, NT, 128], BF16); B_va = sc.buf("va")
        vb = sb("vb", [128, NT, 128], BF16); B_vb = sc.buf("vb")
        feat_sb = [qaT, kaT, gaT, qbT, kbT, gbT]
        feat_B = [B_qaT, B_kaT, B_gaT, B_qbT, B_kbT, B_gbT]

        ident_s = sb("ident_s", [128, 128]); B_ident = sc.buf("ident")
        bones_s = sb("bones_s", [128, 128]); B_bones = sc.buf("bones")
        gvec_s = sb("gvec_s", [128, 8]); B_gvec = sc.buf("gvec")
        gsc = sb("gsc", [128, 8]); B_gsc = sc.buf("gsc")
        gn_s = sb("gn_s", [128, 8]); B_gn = sc.buf("gn")
        bada_s = sb("bada_s", [128, 24]); B_bada = sc.buf("bada")
        cv_s = sb("cv_s", [128, 8, 6]); B_cv = sc.buf("cv")
        scv = sb("scv", [128, 8, 6]); B_scv = sc.buf("scv")
        modT = sb("modT", [128, 24, 6]); B_modT = sc.buf("modT")
        Amod = sb("Amod", [128, 8, 6]); B_Amod = sc.buf("Amod")
        wc_s = sb("wc_s", [128, 8, 1024], BF16); B_wc = sc.buf("wc")

        sc.op("sp", lambda e: e.dma_start(out=ident_s[:], in_=ident), writes=[B_ident], dma=True)
        sc.op("sp", lambda e: e.dma_start(out=bones_s[:], in_=bones), writes=[B_bones], dma=True)
        identb = sb("identb", [128, 128], BF16); B_identb = sc.buf("identb")
        bonesb = sb("bonesb", [128, 128], BF16); B_bonesb = sc.buf("bonesb")
        sc.op("pool", lambda e: e.dma_start(out=identb[:], in_=ident), writes=[B_identb], dma=True)
        sc.op("pool", lambda e: e.dma_start(out=bonesb[:], in_=bones), writes=[B_bonesb], dma=True)
        sc.op("sp", lambda e: e.dma_start(out=gvec_s[:], in_=gvec), writes=[B_gvec], dma=True)
        sc.op("sp", lambda e: e.dma_start(out=gn_s[:], in_=g_norm), writes=[B_gn], dma=True)
        sc.op("sp", lambda e: e.dma_start(out=bada_s[:], in_=b_ada), writes=[B_bada], dma=True)
        sc.op("sp", lambda e: e.dma_start(out=cv_s[:], in_=cvec.rearrange("(kc p) n -> p kc n", p=128)),
              writes=[B_cv], dma=True)
        sc.op("pool", lambda e: e.dma_start(out=wc_s[:], in_=w_c.rearrange("(kc p) n -> p kc n", p=128)),
              writes=[B_wc], dma=True)

        sc.op("act", lambda e: e.activation(out=scv[:], in_=cv_s[:], func=AF.Exp, scale=-1.0), reads=[B_cv], writes=[B_scv])
        sc.op("dve", lambda e: e.tensor_scalar(out=scv[:], in0=scv[:], scalar1=1.0, scalar2=None, op0=ALU.add),
              reads=[B_scv], writes=[B_scv])
        sc.op("dve", lambda e: e.reciprocal(out=scv[:], in_=scv[:]), reads=[B_scv], writes=[B_scv])
        sc.op("dve", lambda e: e.tensor_tensor(out=scv[:], in0=scv[:], in1=cv_s[:], op=ALU.mult),
              reads=[B_scv, B_cv], writes=[B_scv])
        xsl = [sb("xsl%d" % i, [128, D]) for i in range(2)]; B_xsl = [sc.buf("xsl%d" % i) for i in range(2)]
        xnb = [sb("xnb%d" % i, [128, D], BF16) for i in range(2)]; B_xnb = [sc.buf("xnb%d" % i) for i in range(2)]
        wada_s = [xnb[i][:].rearrange("p (k n) -> p k n", k=8) for i in range(2)]
        B_wada = B_xnb
        scvb = sb("scvb", [128, 8, 6], BF16); B_scvb = sc.buf("scvb")
        sc.op("dve", lambda e: e.tensor_copy(out=scvb[:], in_=scv[:]), reads=[B_scv], writes=[B_scvb])
        PA = ps("PA", [128, 2, 512]); PB = ps("PB", [128, 2, 512]); PC = ps("PC", [128, 2, 512]); PD = ps("PD", [128, 2, 512])
        pf = [PB[:, i, :] for i in range(2)]; B_pf = [sc.buf("pf%d" % i) for i in range(2)]
        pmod = pf[0][:, 0:192].rearrange("p (m n) -> p m n", n=8); B_pmod = B_pf[0]
        wada_v = w_ada.rearrange("(kc p) n -> p kc n", p=128)
        for m in range(24):
            sl = m % 2
            sc.op("pool", lambda e, m=m, sl=sl: e.dma_start(out=wada_s[sl], in_=wada_v[:, :, m * 128:(m + 1) * 128]),
                  writes=[B_wada[sl]], dma=True)
            for kc in range(8):
                sc.op("pe", lambda e, m=m, sl=sl, kc=kc: e.matmul(pmod[:, m, 0:6], lhsT=wada_s[sl][:, kc, :],
                                                                  rhs=scvb[:, kc, :], start=(kc == 0), stop=(kc == 7)),
                      reads=[B_wada[sl], B_scvb], writes=[B_pmod])
        for m in range(24):
            sc.op("dve", lambda e, m=m: e.tensor_scalar(out=modT[:, m, :], in0=pmod[:, m, 0:6],
                                                        scalar1=bada_s[:, m:m + 1], scalar2=None, op0=ALU.add),
                  reads=[B_pmod, B_bada], writes=[B_modT])
        sc.op("dve", lambda e: e.tensor_scalar(out=Amod[:], in0=modT[:, 8:16, :], scalar1=1.0, scalar2=32.0,
                                               op0=ALU.add, op1=ALU.mult), reads=[B_modT], writes=[B_Amod])
        for kc in range(8):
            sc.op("dve", lambda e, kc=kc: e.tensor_scalar(out=Amod[:, kc, :], in0=Amod[:, kc, :],
                                                          scalar1=gn_s[:, kc:kc + 1], scalar2=None, op0=ALU.mult),
                  reads=[B_gn], writes=[B_Amod])
        sc.op("dve", lambda e: e.tensor_scalar(out=gsc[:, 0:4], in0=gvec_s[:, 0:4], scalar1=1.0, scalar2=None,
                                               op0=ALU.mult), reads=[B_gvec], writes=[B_gsc])
        sc.op("dve", lambda e: e.tensor_scalar(out=gsc[:, 1:2], in0=gvec_s[:, 1:2], scalar1=8.0, scalar2=None,
                                               op0=ALU.mult), reads=[B_gvec], writes=[B_gsc])
        sc.op("dve", lambda e: e.tensor_scalar(out=gsc[:, 3:4], in0=gvec_s[:, 3:4], scalar1=8.0, scalar2=None,
                                               op0=ALU.mult), reads=[B_gvec], writes=[B_gsc])

        epsc = sb("epsc", [128, 4]); B_epsc = sc.buf("epsc")
        sc.op("pool", lambda e: e.memset(epsc[:, 0:1], float(D * EPS)), writes=[B_epsc])
        sc.op("pool", lambda e: e.memset(epsc[:, 1:2], float(64 * EPS)), writes=[B_epsc])
        sc.op("pool", lambda e: e.memset(epsc[:, 2:3], float(128 * EPS)), writes=[B_epsc])
        sc.op("pool", lambda e: e.memset(epsc[:, 3:4], 1.0), writes=[B_epsc])
        xn = [sb("xn%d" % i, [128, D]) for i in range(2)]; B_xn = [sc.buf("xn%d" % i) for i in range(2)]
        ssq = sb("ssq", [128, 2]); B_ssq = [sc.buf("ssq0"), sc.buf("ssq1")]
        rstd = sb("rstd", [128, 2]); B_rstd = [sc.buf("rstd0"), sc.buf("rstd1")]
        hT = [sb("hT%d" % i, [128, 8, 512], BF16) for i in range(2)]; B_hT = [sc.buf("hT%d" % i) for i in range(2)]
        pt = PA[:].rearrange("p a (b c) -> p (a b) c", c=128); B_pt = sc.buf("pt")
        pn = PC[:, 0, :]; B_pn = sc.buf("pn")
        pk = PC[:, 1, :].rearrange("p (t c) -> p t c", c=128); B_pk = sc.buf("pk")
        pv = [PD[:, i, :] for i in range(2)]; B_pv = [sc.buf("pv0"), sc.buf("pv1")]
        sq = sb("sq", [128, 512]); B_sq = sc.buf("sq")
        rs = sb("rs", [128, 512]); B_rs = sc.buf("rs")
        tmpn = sb("tmpn", [128, 512]); B_tmpn = sc.buf("tmpn")
        kf32 = sb("kf32", [128, 512]); B_kf32 = sc.buf("kf32")
        kout = [sb("kout%d" % i, [128, 4, 128]) for i in range(2)]; B_kout = [sc.buf("kout%d" % i) for i in range(2)]
        vst = [sb("vst%d" % i, [128, 256]) for i in range(2)]; B_vst = [sc.buf("vst%d" % i) for i in range(2)]
        B_ak = sc.buf("ak_out"); B_av = sc.buf("av_out"); B_bk = sc.buf("bk_out"); B_bv = sc.buf("bv_out")
        out_bufs = [B_ak, B_av, B_bk, B_bv]

        tile_ctr = [0]
        kout_ctr = [0]

        sq2 = [sq, kf32]; B_sq2 = [B_sq, B_kf32]
        sqb = [sb("sqb%d" % i, [128, 512], BF16) for i in range(2)]; B_sqb = [sc.buf("sqb%d" % i) for i in range(2)]
        rs2 = [rs, sb("rs_b", [128, 512])]; B_rs2 = [B_rs, sc.buf("rs_b")]
        nrm_ctr = [0]
        later = []
        later2 = []

        def grp_ntile(gi):
            return 4 if gi < 16 else 1

        def grp_ctx(gi):
            return [(0, 128, 0)] if gi < 16 else [(32 * i, 32 * i + 32, 1 + i) for i in range(4)]

        tile_slot = {}

        def prep_a(gi, t):
            tg = gi * 4 + t
            sl = tile_ctr[0] % 2
            tile_ctr[0] += 1
            tile_slot[(gi, t)] = sl
            src = xp[tg * 128:(tg + 1) * 128, :] if tg < 64 else xs4
            sc.op("sp", lambda e, sl=sl, src=src: e.dma_start(out=xsl[sl][:], in_=src), writes=[B_xsl[sl]], dma=True)
            sc.op("act", lambda e, sl=sl: e.activation(out=xnb[sl][:], in_=xsl[sl][:], func=AF.Square,
                                                       accum_out=ssq[:, sl:sl + 1]),
                  reads=[B_xsl[sl]], writes=[B_xnb[sl], B_ssq[sl]])
            sc.op("act", lambda e, sl=sl: e.activation(out=ssq[:, sl:sl + 1], in_=ssq[:, sl:sl + 1], func=AF.Ln,
                                                       bias=epsc[:, 0:1]),
                  reads=[B_ssq[sl], B_epsc], writes=[B_ssq[sl]])
            sc.op("act", lambda e, sl=sl: e.activation(out=rstd[:, sl:sl + 1], in_=ssq[:, sl:sl + 1], func=AF.Exp, scale=-0.5),
                  reads=[B_ssq[sl]], writes=[B_rstd[sl]])
            sc.op("act", lambda e, sl=sl: e.mul(out=xnb[sl][:], in_=xsl[sl][:], mul=rstd[:, sl:sl + 1]),
                  reads=[B_xsl[sl], B_rstd[sl]], writes=[B_xnb[sl]])

        def prep_b(gi, t):
            hs = gi % 2
            sl = tile_slot[(gi, t)]
            for kc in range(8):
                sc.op("pe", lambda e, sl=sl, kc=kc: e.matmul(pt[:, kc, :], lhsT=xnb[sl][:, kc * 128:(kc + 1) * 128], rhs=identb[:],
                                                             start=True, stop=True),
                      reads=[B_xnb[sl], B_identb], writes=[B_pt])
            if gi < 16:
                xv = xn[sl][:].rearrange("p (k n) -> p k n", k=8)
                sc.op("dve", lambda e, xv=xv: e.tensor_tensor(out=xv, in0=pt, in1=Amod[:, :, 0:1].to_broadcast([128, 8, 128]), op=ALU.mult),
                      reads=[B_pt, B_Amod], writes=[B_xn[sl]])
                sc.op("dve", lambda e, xv=xv, t=t: e.tensor_tensor(out=hT[hs][:, :, t * 128:(t + 1) * 128], in0=xv,
                                                                   in1=modT[:, 0:8, 0:1].to_broadcast([128, 8, 128]), op=ALU.add),
                      reads=[B_xn[sl], B_modT], writes=[B_hT[hs]])
            for (c0, c1, ctx) in (grp_ctx(gi) if gi >= 16 else []):
                for kc in range(8):
                    sc.op("dve", lambda e, kc=kc, c0=c0, c1=c1, ctx=ctx, t=t: e.tensor_scalar(
                        out=hT[hs][:, kc, t * 128 + c0:t * 128 + c1], in0=pt[:, kc, c0:c1],
                        scalar1=Amod[:, kc, ctx:ctx + 1], scalar2=modT[:, kc, ctx:ctx + 1],
                        op0=ALU.mult, op1=ALU.add),
                        reads=[B_pt, B_Amod, B_modT], writes=[B_hT[hs]])

        def feat(gi, f):
            hs = gi % 2
            ntile = grp_ntile(gi)
            ncol = ntile * 128
            tok0 = gi * 512
            pfi = f % 2
            for kc in range(8):
                sc.op("pe", lambda e, f=f, kc=kc, pfi=pfi: e.matmul(pf[pfi][:, 0:ncol], lhsT=wc_s[:, kc, f * 128:(f + 1) * 128],
                                                                    rhs=hT[hs][:, kc, 0:ncol], start=(kc == 0), stop=(kc == 7)),
                      reads=[B_wc, B_hT[hs]], writes=[B_pf[pfi]])
            dst = feat_sb[f]; Bd = feat_B[f]
            if f in (2, 5):
                sc.op("act", lambda e, pfi=pfi: e.activation(out=tmpn[:, 0:ncol], in_=pf[pfi][:, 0:ncol], func=AF.Exp, scale=-1.0),
                      reads=[B_pf[pfi]], writes=[B_tmpn])
                sc.op("act", lambda e: e.activation(out=tmpn[:, 0:ncol], in_=tmpn[:, 0:ncol], func=AF.Ln, bias=epsc[:, 3:4]),
                      reads=[B_tmpn, B_epsc], writes=[B_tmpn])
                sc.op("act", lambda e: e.activation(out=tmpn[:, 0:ncol], in_=tmpn[:, 0:ncol], func=AF.Exp, scale=-1.0),
                      reads=[B_tmpn], writes=[B_tmpn])
                sc.op("dve", lambda e, pfi=pfi, dst=dst: e.tensor_tensor(out=dst[:, tok0:tok0 + ncol], in0=pf[pfi][:, 0:ncol],
                                                                         in1=tmpn[:, 0:ncol], op=ALU.mult),
                      reads=[B_pf[pfi], B_tmpn], writes=[Bd])
                return
            gcol = {0: 0, 1: 1, 3: 2, 4: 3}[f]
            ni = nrm_ctr[0] % 2
            nrm_ctr[0] += 1
            sqx, B_sqx, rsx, B_rsx = sqb[ni], B_sqb[ni], rs2[ni], B_rs2[ni]
            sc.op("act", lambda e, pfi=pfi: e.activation(out=sqx[:, 0:ncol], in_=pf[pfi][:, 0:ncol], func=AF.Square),
                  reads=[B_pf[pfi]], writes=[B_sqx])
            later.append(lambda: feat_b(gi, f, ni))

        def feat_b(gi, f, ni):
            hs = gi % 2
            ntile = grp_ntile(gi)
            ncol = ntile * 128
            tok0 = gi * 512
            pfi = f % 2
            dst = feat_sb[f]; Bd = feat_B[f]
            gcol = {0: 0, 1: 1, 3: 2, 4: 3}[f]
            sqx, B_sqx, rsx, B_rsx = sqb[ni], B_sqb[ni], rs2[ni], B_rs2[ni]
            sc.op("pe", lambda e: e.matmul(pn[:, 0:ncol], lhsT=bonesb[:], rhs=sqx[:, 0:ncol], start=True, stop=True),
                  reads=[B_bonesb, B_sqx], writes=[B_pn])
            sc.op("act", lambda e: e.activation(out=rsx[:, 0:ncol], in_=pn[:, 0:ncol], func=AF.Ln, bias=epsc[:, 1:2]),
                  reads=[B_pn, B_epsc], writes=[B_rsx])
            sc.op("act", lambda e: e.activation(out=rsx[:, 0:ncol], in_=rsx[:, 0:ncol], func=AF.Exp, scale=-0.5),
                  reads=[B_rsx], writes=[B_rsx])
            if f in (0, 3):
                sc.op("dve", lambda e, pfi=pfi, dst=dst, gcol=gcol: e.scalar_tensor_tensor(
                    out=dst[:, tok0:tok0 + ncol], in0=pf[pfi][:, 0:ncol], scalar=gsc[:, gcol:gcol + 1], in1=rsx[:, 0:ncol],
                    op0=ALU.mult, op1=ALU.mult), reads=[B_pf[pfi], B_rsx, B_gsc], writes=[Bd])
                return
            sc.op("dve", lambda e, pfi=pfi, gcol=gcol: e.scalar_tensor_tensor(
                out=kf32[:, 0:ncol], in0=pf[pfi][:, 0:ncol], scalar=gsc[:, gcol:gcol + 1], in1=rsx[:, 0:ncol],
                op0=ALU.mult, op1=ALU.mult), reads=[B_pf[pfi], B_rsx, B_gsc], writes=[B_kf32])
            sc.op("pool", lambda e, dst=dst: e.tensor_copy(out=dst[:, tok0:tok0 + ncol], in_=kf32[:, 0:ncol]),
                  reads=[B_kf32], writes=[Bd])
            need_out = (f == 1) or (gi >= 15)
            if need_out:
                later2.append(lambda: feat_c(gi, f))

        def feat_c(gi, f):
            ntile = grp_ntile(gi)
            tok0 = gi * 512
            if True:
                ko = kout_ctr[0] % 2
                kout_ctr[0] += 1
                for t in range(ntile):
                    sc.op("pe", lambda e, t=t: e.transpose(out=pk[:, t, :], in_=kf32[:, t * 128:(t + 1) * 128],
                                                           identity=ident_s[:]),
                          reads=[B_kf32, B_ident], writes=[B_pk])
                sc.op("dve", lambda e, ko=ko: e.tensor_copy(out=kout[ko][:, 0:ntile, :], in_=pk[:, 0:ntile, :]),
                      reads=[B_pk], writes=[B_kout[ko]])
                if gi < 16:
                    if f == 1:
                        dstd = ak[tok0:tok0 + 512, :].rearrange("(t p) e -> p t e", p=128); Bo = B_ak
                    else:
                        dstd = bk.rearrange("(t p) e -> p t e", p=128); Bo = B_bk
                    sc.op("pool", lambda e, ko=ko, dstd=dstd: e.dma_start(out=dstd, in_=kout[ko][:]),
                          reads=[B_kout[ko]], writes=[Bo], dma=True)
                else:
                    dstd = aks if f == 1 else bks
                    Bo = B_ak if f == 1 else B_bk
                    sc.op("pool", lambda e, ko=ko, dstd=dstd: e.dma_start(out=dstd, in_=kout[ko][:, 0, :]),
                          reads=[B_kout[ko]], writes=[Bo], dma=True)

        def vtile(gi, t):
            hs = gi % 2
            tg = gi * 4 + t
            pvi = tg % 2
            for kc in range(8):
                sc.op("pe", lambda e, t=t, kc=kc, pvi=pvi: e.matmul(pv[pvi][:, 0:256], lhsT=hT[hs][:, kc, t * 128:(t + 1) * 128],
                                                                    rhs=wc_s[:, kc, 768:1024], start=(kc == 0), stop=(kc == 7)),
                      reads=[B_wc, B_hT[hs]], writes=[B_pv[pvi]])
            sc.op("dve", lambda e, pvi=pvi: e.tensor_copy(out=vst[pvi][:], in_=pv[pvi][:, 0:256]),
                  reads=[B_pv[pvi]], writes=[B_vst[pvi]])
            sc.op("pool", lambda e, pvi=pvi, tg=tg: e.tensor_copy(out=va[:, tg, :], in_=vst[pvi][:, 0:128]),
                  reads=[B_vst[pvi]], writes=[B_va])
            sc.op("pool", lambda e, pvi=pvi, tg=tg: e.tensor_copy(out=vb[:, tg, :], in_=vst[pvi][:, 128:256]),
                  reads=[B_vst[pvi]], writes=[B_vb])
            if tg < 64:
                sc.op("pool", lambda e, pvi=pvi, tg=tg: e.dma_start(out=av[tg * 128:(tg + 1) * 128, :], in_=vst[pvi][:, 0:128]),
                      reads=[B_vst[pvi]], writes=[B_av], dma=True)
                if tg >= 60:
                    sc.op("pool", lambda e, pvi=pvi, tg=tg: e.dma_start(out=bv[(tg - 60) * 128:(tg - 59) * 128, :],
                                                                        in_=vst[pvi][:, 128:256]),
                          reads=[B_vst[pvi]], writes=[B_bv], dma=True)
            else:
                sc.op("pool", lambda e, pvi=pvi: e.dma_start(out=avs, in_=vst[pvi][:, 0:128]),
                      reads=[B_vst[pvi]], writes=[B_av], dma=True)
                sc.op("pool", lambda e, pvi=pvi: e.dma_start(out=bvs, in_=vst[pvi][:, 128:256]),
                      reads=[B_vst[pvi]], writes=[B_bv], dma=True)

        import os
        NG = int(os.environ.get("DBG_NG", "16"))
        NGRP = NG + 1 if NG == 16 else NG
        tiles = [(gi, t) for gi in range(NGRP) for t in range(grp_ntile(gi))]
        tidx = {tl: i for i, tl in enumerate(tiles)}
        a_done = [0]

        def ensure_a(n):
            while a_done[0] < min(n, len(tiles)):
                prep_a(*tiles[a_done[0]])
                a_done[0] += 1

        for t in range(grp_ntile(0)):
            ensure_a(tidx[(0, t)] + 2)
            prep_b(0, t)
        for gi in range(NGRP):
            n_next = grp_ntile(gi + 1) if gi + 1 < NGRP else 0
            for i in range(6):
                if i < n_next:
                    ensure_a(tidx[(gi + 1, i)] + 2)
                feat(gi, i)
                while later2:
                    later2.pop(0)()
                if i < grp_ntile(gi):
                    vtile(gi, i)
                if i < n_next:
                    prep_b(gi + 1, i)
                while later:
                    later.pop(0)()
        while later2:
            later2.pop(0)()

        class _Stop(Exception):
            pass
        STAGE = int(os.environ.get("DBG_STAGE", "99"))

        def stage(k):
            if STAGE <= k:
                raise _Stop()
        try:
          if NG == 16:
            stage(1)
            def fence(newb, oldbs):
                for ob in oldbs:
                    if ob.w is not None:
                        newb.r.append(ob.w)
                    newb.r.extend(ob.r)
                return newb

            rs_srcA1_2 = nc.dram_tensor("rs_srcA1", [4 * 4 * 128, 1024], BF16).ap()
            rs_dstA1_2 = nc.dram_tensor("rs_dstA1", [4 * 128, 1024], BF16).ap()
            rs_srcA2_2 = nc.dram_tensor("rs_srcA2", [4 * 4 * 128, 1056], BF16).ap()
            rs_dstA2_2 = nc.dram_tensor("rs_dstA2", [4 * 128, 1056], BF16).ap()
            rs_srcA1 = rs_srcA1_2.rearrange("(j s p) c -> j s p c", j=4, s=4)
            rs_srcA2 = rs_srcA2_2.rearrange("(j s p) c -> j s p c", j=4, s=4)
            B_srcA1 = sc.buf("rs_srcA1"); B_dstA1 = sc.buf("rs_dstA1"); B_srcA2 = sc.buf("rs_srcA2"); B_dstA2 = sc.buf("rs_dstA2")
            rs_srcB2 = nc.dram_tensor("rs_srcB", [4 * 4 * 128, MT], BF16).ap()
            rs_dstB2 = nc.dram_tensor("rs_dstB", [4 * 128, MT], BF16).ap()
            rs_srcs = {4: rs_srcB2.rearrange("(j s p) c -> j s p c", j=4, s=4)}
            rs_dstB = rs_dstB2.rearrange("(s p) c -> s p c", s=4)
            B_srcB = sc.buf("rs_srcB"); B_dstB = sc.buf("rs_dstB")
            B_srcs = {4: B_srcB}

            def do_rs(src2, dst2, Bs, Bd):
                if os.environ.get("DBG_NOCC"):
                    sc.op("pool", lambda e: e.dma_start(out=dst2, in_=src2[0:4 * 128, :]), reads=[Bs], writes=[Bd], dma=True)
                else:
                    sc.op("pool", lambda e: e.collective_compute("ReduceScatter", ALU.add, replica_groups=[[0, 1, 2, 3], [4, 5, 6, 7]],
                                                                 ins=[src2], outs=[dst2]), reads=[Bs], writes=[Bd], dma=False)

            msk_s = sb("msk_s", [128, 4]); B_msk = sc.buf("msk")
            c15_s = sb("c15_s", [128, 1]); B_c15 = sc.buf("c15")
            lam_s = kout[0][:].rearrange("p a b -> p (a b)")[:, 0:256].rearrange("p (a b) -> p a b", a=4)
            B_lam = fence(sc.buf("lam"), [B_kout[0]])
            lamt = sb("lamt", [128, 8]); B_lamt = sc.buf("lamt")
            onesf_s = sb("onesf_s", [128, 128]); B_onesf = sc.buf("onesf")
            onesb = sb("onesb", [128, 128], BF16); B_onesb = sc.buf("onesb")
            bAs_s = sb("bAs_s", [128, 8, 32]); B_bAs = sc.buf("bAs")
            bAsn_s = sb("bAsn_s", [128, 128]); B_bAsn = sc.buf("bAsn")

            bBs_s = sb("bBs_s", [128, 2, 4, 32]); B_bBs = sc.buf("bBs")
            bBsn_s = sb("bBsn_s", [128, 2, 128]); B_bBsn = sc.buf("bBsn")
            es_s = hT[1][:, 6, :].rearrange("p (k m q) -> p k m q", k=8, m=2)
            B_es = fence(sc.buf("es"), [B_hT[1]])
            for (dst_t, src_t, Bb) in [(msk_s, msk, B_msk), (c15_s, c15, B_c15), (lam_s, lamv, B_lam), (onesf_s, onesf, B_onesf),
                                       (bAs_s, bAs, B_bAs), (bAsn_s, bAsn, B_bAsn), (bBs_s, bBs, B_bBs),
                                       (bBsn_s, bBsn, B_bBsn)]:
                sc.op("sp", lambda e, d=dst_t, s_=src_t: e.dma_start(out=d[:], in_=s_), writes=[Bb], dma=True)
            sc.op("pool", lambda e: e.memset(onesb[:], 1.0), writes=[B_onesb])
            sc.op("dve", lambda e: e.tensor_tensor(out=lam_s[:, 0, :], in0=lam_s[:, 0, :], in1=lam_s[:, 1, :], op=ALU.mult),
                  reads=[B_lam], writes=[B_lam])
            sc.op("dve", lambda e: e.tensor_tensor(out=lam_s[:, 2, :], in0=lam_s[:, 2, :], in1=lam_s[:, 3, :], op=ALU.mult),
                  reads=[B_lam], writes=[B_lam])
            sc.op("dve", lambda e: e.reduce_sum(out=lamt[:, 0:1], in_=lam_s[:, 0, :], axis=AX.X), reads=[B_lam], writes=[B_lamt])
            sc.op("dve", lambda e: e.reduce_sum(out=lamt[:, 1:2], in_=lam_s[:, 2, :], axis=AX.X), reads=[B_lam], writes=[B_lamt])
            sc.op("act", lambda e: e.activation(out=lamt[:, 2:4], in_=lamt[:, 0:2], func=AF.Exp), reads=[B_lamt], writes=[B_lamt])
            sc.op("dve", lambda e: e.tensor_tensor(out=lamt[:, 4:5], in0=lamt[:, 3:4], in1=lamt[:, 2:3], op=ALU.subtract),
                  reads=[B_lamt], writes=[B_lamt])
            sc.op("dve", lambda e: e.tensor_scalar(out=lamt[:, 4:5], in0=lamt[:, 4:5], scalar1=-float(LAM_INIT), scalar2=None,
                                                   op0=ALU.add), reads=[B_lamt], writes=[B_lamt])
            sc.op("dve", lambda e: e.tensor_scalar(out=lamt[:, 5:6], in0=gvec_s[:, 4:5], scalar1=float(1.0 - LAM_INIT),
                                                   scalar2=None, op0=ALU.mult), reads=[B_gvec, B_lamt], writes=[B_lamt])

            B_e = [fence(sc.buf("e0"), [B_hT[0]]), fence(sc.buf("e1"), [B_hT[0]]), fence(sc.buf("e2"), [B_hT[1]])]
            e_t = [hT[0][:, 0:2, :], hT[0][:, 2:4, :], hT[1][:, 4:6, :]]
            B_EBA = fence(sc.buf("EBA"), [B_wc])
            EBA = wc_s[:, 2:5, :].rearrange("p a b -> p (a b)")[:, 0:2560].rearrange("p (v c) -> p v c", v=5)
            B_kcT = fence(sc.buf("kcT"), [B_hT[0]]); kcT = hT[0][:, 4:6, :].rearrange("p a b -> p (a b)")
            B_vc = fence(sc.buf("vc"), [B_hT[0]]); vc = hT[0][:, 6:8, :].rearrange("p a (b c) -> p (a b) c", c=128)
            B_gst = fence(sc.buf("gst"), [B_hT[1]]); gst = hT[1][:, 0:4, :]
            B_eB = fence(sc.buf("eB"), [B_wc]); eB = [wc_s[:, 0, 0:640], wc_s[:, 1, 0:640]]
            bA_v = [xsl[0][:, 0:512], xsl[0][:, 512:1024], xsl[1][:, 0:512], xsl[1][:, 512:1024], kf32[:, 0:512]]
            bA_B = [B_xsl[0], B_xsl[0], B_xsl[1], B_xsl[1], B_kf32]
            sc.op("sp", lambda e: e.dma_start(out=xsl[0][:].rearrange("p (a b) -> p a b", a=2), in_=bA[:, 0:2, :]),
                  writes=[B_xsl[0]], dma=True)
            sc.op("sp", lambda e: e.dma_start(out=xsl[1][:].rearrange("p (a b) -> p a b", a=2), in_=bA[:, 2:4, :]),
                  writes=[B_xsl[1]], dma=True)
            sc.op("sp", lambda e: e.dma_start(out=kf32[:], in_=bA[:, 4, :]), writes=[B_kf32], dma=True)
            for i5 in range(5):
                sc.op("act", lambda e, i5=i5: e.activation(out=EBA[:, i5, :], in_=bA_v[i5], func=AF.Exp),
                      reads=[bA_B[i5]], writes=[B_EBA])
            rz = xn[0][:].rearrange("p (a b) -> p a b", a=2); B_rz = B_xn[0]
            dtmp = xn[1][:].rearrange("p (a b) -> p a b", a=2); B_dtmp = B_xn[1]
            psS = [PA, PB]; B_psS = [[B_pt], [B_pf[0], B_pf[1]]]
            po = PC; B_po = [B_pn, B_pk]
            pz = PD; B_pz = [B_pv[0], B_pv[1]]
            o_b = tmpn

            def write_slots(gated_ap_fn, ncol, tok0, slot_base, Bsrcs):
                for s_ in range(4):
                    sc.op("dve", lambda e, s_=s_: gated_ap_fn(e, gst[:, s_, 0:ncol], msk_s[:, s_:s_ + 1]),
                          reads=Bsrcs + [B_msk], writes=[B_gst])
                if tok0 < S:
                    j, c0 = tok0 // 2048, tok0 % 2048
                    if slot_base == 4:
                        dsrc, Bsrc_ = rs_srcs[4][j, :, :, c0:c0 + ncol], B_srcB
                    elif c0 < 1024:
                        dsrc, Bsrc_ = rs_srcA1[j, :, :, c0:c0 + ncol], B_srcA1
                    else:
                        dsrc, Bsrc_ = rs_srcA2[j, :, :, c0 - 1024:c0 - 1024 + ncol], B_srcA2
                    sc.op("pool", lambda e: e.dma_start(out=dsrc.rearrange("s p c -> p s c"), in_=gst[:, :, 0:ncol]),
                          reads=[B_gst], writes=[Bsrc_], dma=True)
                    if slot_base == 0 and tok0 == 13 * 512:
                        do_rs(rs_srcA1_2, rs_dstA1_2, B_srcA1, B_dstA1)
                else:
                    for i in range(4):
                        if slot_base == 4:
                            dsrc, Bsrc_ = rs_srcs[4][i, :, :, 2048:2080], B_srcB
                        else:
                            dsrc, Bsrc_ = rs_srcA2[i, :, :, 1024:1056], B_srcA2
                        sc.op("pool", lambda e, i=i, dsrc=dsrc: e.dma_start(out=dsrc.rearrange("s p c -> p s c"),
                                                                         in_=gst[:, :, 32 * i:32 * i + 32]),
                              reads=[B_gst], writes=[Bsrc_], dma=True)

            def finalize_A_steps(ncol, tok0, ssq_ps, B_ssq_ps):
                def s_a():
                    sc.op("act", lambda e: e.activation(out=dtmp[:, :, 0:ncol], in_=pz[:, :, 0:ncol], func=AF.Ln), reads=B_pz, writes=[B_dtmp])
                    sc.op("act", lambda e: e.activation(out=dtmp[:, :, 0:ncol], in_=dtmp[:, :, 0:ncol], func=AF.Exp, scale=-1.0),
                          reads=[B_dtmp], writes=[B_dtmp])
                    sc.op("dve", lambda e: e.tensor_tensor(out=sq[:, 0:ncol], in0=po[:, 0, 0:ncol], in1=dtmp[:, 0, 0:ncol], op=ALU.mult),
                          reads=B_po + [B_dtmp], writes=[B_sq])
                    sc.op("dve", lambda e: e.tensor_tensor(out=rs[:, 0:ncol], in0=po[:, 1, 0:ncol], in1=dtmp[:, 1, 0:ncol], op=ALU.mult),
                          reads=B_po + [B_dtmp], writes=[B_rs])

                def s_b():
                    sc.op("dve", lambda e: e.scalar_tensor_tensor(out=o_b[:, 0:ncol], in0=rs[:, 0:ncol], scalar=lamt[:, 4:5],
                                                                  in1=sq[:, 0:ncol], op0=ALU.mult, op1=ALU.add),
                          reads=[B_rs, B_sq, B_lamt], writes=[B_tmpn])

                def s_c():
                    sc.op("act", lambda e: e.activation(out=sq[:, 0:ncol], in_=o_b[:, 0:ncol], func=AF.Square),
                          reads=[B_tmpn], writes=[B_sq])

                def s_d():
                    sc.op("pe", lambda e: e.matmul(ssq_ps[:, 0:ncol], lhsT=onesf_s[:], rhs=sq[:, 0:ncol], start=True, stop=True),
                          reads=[B_onesf, B_sq], writes=B_ssq_ps)

                def s_e():
                    sc.op("act", lambda e: e.activation(out=rs[:, 0:ncol], in_=ssq_ps[:, 0:ncol], func=AF.Ln, bias=epsc[:, 2:3]),
                          reads=B_ssq_ps + [B_epsc], writes=[B_rs])
                    sc.op("act", lambda e: e.activation(out=rs[:, 0:ncol], in_=rs[:, 0:ncol], func=AF.Exp, scale=-0.5),
                          reads=[B_rs], writes=[B_rs])

                def s_f():
                    sc.op("dve", lambda e: e.tensor_tensor(out=o_b[:, 0:ncol], in0=o_b[:, 0:ncol], in1=rs[:, 0:ncol], op=ALU.mult),
                          reads=[B_tmpn, B_rs], writes=[B_tmpn])
                    sc.op("dve", lambda e: e.scalar_tensor_tensor(out=sq[:, 0:ncol], in0=o_b[:, 0:ncol], scalar=lamt[:, 5:6],
                                                                  in1=gaT[:, tok0:tok0 + ncol], op0=ALU.mult, op1=ALU.mult),
                          reads=[B_tmpn, B_lamt, B_gaT], writes=[B_sq])

                def s_g():
                    write_slots(lambda e, o_ap, m_ap: e.tensor_scalar(out=o_ap, in0=sq[:, 0:ncol], scalar1=m_ap, scalar2=float(math.sqrt(128.0)),
                                                                      op0=ALU.mult, op1=ALU.mult), ncol, tok0, 0, [B_sq])
                return [s_a, s_b, s_c, s_d, s_e, s_f, s_g]

            def finalize_A(ncol, tok0, ssq_ps, B_ssq_ps):
                for st_ in finalize_A_steps(ncol, tok0, ssq_ps, B_ssq_ps):
                    st_()

            TS0 = 8192
            tmpB = [xn[0], xn[1]]; B_tmpB = [B_xn[0], B_xn[1]]
            psB = [PA[:].rearrange("p a b -> p (a b)"), PB[:].rearrange("p a b -> p (a b)")]
            gB = sq

            poB2 = [PC[:, 0, :], PC[:, 1, :]]; B_poB2 = [B_pn, B_pk]
            pzB2 = [PD[:, 0, :], PD[:, 1, :]]; B_pzB2 = [B_pv[0], B_pv[1]]
            EBB = [wc_s[:, 5, 0:640], wc_s[:, 6, 0:640]]; B_EBB = fence(sc.buf("EBB"), [B_wc])
            B_eBa = [fence(sc.buf("eB0"), [B_wc, B_eB]), fence(sc.buf("eB1"), [B_wc, B_eB])]
            for a in range(2):
                sc.op("sp", lambda e, a=a: e.dma_start(out=xsl[a][:, 0:640], in_=bB[:, a, :]), writes=[B_xsl[a]], dma=True)
                sc.op("act", lambda e, a=a: e.activation(out=EBB[a], in_=xsl[a][:, 0:640], func=AF.Exp), reads=[B_xsl[a]], writes=[B_EBB])

            def finalize_B(ncol, c0, tok0, pi=0):
                sc.op("dve", lambda e: e.reciprocal(out=rs[:, 0:ncol], in_=pzB2[pi][:, 0:ncol]), reads=[B_pzB2[pi]], writes=[B_rs])
                sc.op("dve", lambda e: e.tensor_tensor(out=rs[:, 0:ncol], in0=poB2[pi][:, 0:ncol], in1=rs[:, 0:ncol], op=ALU.mult),
                      reads=[B_poB2[pi], B_rs], writes=[B_rs])
                sc.op("dve", lambda e: e.tensor_tensor(out=gB[:, c0:c0 + ncol], in0=rs[:, 0:ncol], in1=gbT[:, tok0:tok0 + ncol],
                                                       op=ALU.mult), reads=[B_rs, B_gbT], writes=[B_sq])

            eBd = [[eB[0], e_t[0].rearrange("p a b -> p (a b)")[:, 0:640]], [eB[1], e_t[1].rearrange("p a b -> p (a b)")[:, 0:640]]]
            B_eBd = [[B_eBa[0], B_e[0]], [B_eBa[1], B_e[1]]]

            def b_qk(qt):
                d0 = max(0, 4 - qt)
                for d in range(d0, 5):
                    kb = qt + d - 4
                    for a in range(2):
                        sc.op("pe", lambda e, a=a, d=d, kb=kb, qt=qt: e.matmul(
                            psB[a][:, d * 128:(d + 1) * 128], lhsT=kbT[64 * a:64 * a + 64, kb * 128:(kb + 1) * 128],
                            rhs=qbT[64 * a:64 * a + 64, qt * 128:(qt + 1) * 128], start=True, stop=True),
                            reads=[B_kbT, B_qbT], writes=B_psS[a])

            def b_exp(qt, a):
                d0 = max(0, 4 - qt)
                eb, Beb = eBd[a][qt % 2], B_eBd[a][qt % 2]
                sc.op("act", lambda e, a=a, d0=d0, eb=eb: e.activation(out=eb[:, d0 * 128:640], in_=psB[a][:, d0 * 128:640], func=AF.Exp),
                      reads=B_psS[a], writes=[Beb])
                sc.op("dve", lambda e, a=a, d0=d0, eb=eb: e.tensor_tensor(out=eb[:, d0 * 128:640], in0=eb[:, d0 * 128:640],
                                                                          in1=EBB[a][:, d0 * 128:640], op=ALU.mult),
                      reads=[Beb, B_EBB], writes=[Beb])

            def b_pv(qt, a):
                d0 = max(0, 4 - qt)
                pi = qt % 2
                eb, Beb = eBd[a][qt % 2], B_eBd[a][qt % 2]
                for d in range(d0, 5):
                    kb = qt + d - 4
                    sc.op("pe", lambda e, a=a, d=d, kb=kb, d0=d0, pi=pi, eb=eb: e.matmul(
                        poB2[pi][64 * a:64 * a + 64, 0:128], lhsT=vb[:, kb, 64 * a:64 * a + 64], rhs=eb[:, d * 128:(d + 1) * 128],
                        start=(d == d0), stop=(d == 4)), reads=[B_vb, Beb], writes=[B_poB2[pi]])
                    sc.op("pe", lambda e, a=a, d=d, d0=d0, pi=pi, eb=eb: e.matmul(
                        pzB2[pi][64 * a:64 * a + 64, 0:128], lhsT=onesb[:, 0:64], rhs=eb[:, d * 128:(d + 1) * 128],
                        start=(d == d0), stop=(d == 4)), reads=[B_onesb, Beb], writes=[B_pzB2[pi]])

            def b_fin(qt):
                finalize_B(128, (qt % 4) * 128, qt * 128, qt % 2)
                if qt % 4 == 3:
                    write_slots(lambda e, o_ap, m_ap: e.tensor_scalar(out=o_ap, in0=gB[:, 0:512], scalar1=m_ap, scalar2=None,
                                                                      op0=ALU.mult), 512, (qt - 3) * 128, 4, [B_sq])

            b_qk(0)
            for qt in range(64):
                for a in range(2):
                    b_exp(qt, a)
                if qt + 1 < 64:
                    b_qk(qt + 1)
                if qt >= 1:
                    b_fin(qt - 1)
                for a in range(2):
                    b_pv(qt, a)
            b_fin(63)
            poB = poB2[0]; pzB = pzB2[0]
            B_eB = fence(B_eB, B_eBa)

            for a in range(2):
                sc.op("pe", lambda e, a=a: e.matmul(psB[a][:, 0:128], lhsT=kbT[64 * a:64 * a + 64, TS0:TS0 + 128],
                                                    rhs=qbT[64 * a:64 * a + 64, TS0:TS0 + 128], start=True, stop=True),
                      reads=[B_kbT, B_qbT], writes=B_psS[a])
                sc.op("dve", lambda e, a=a: e.tensor_tensor(out=tmpB[a][:, 0:128], in0=psB[a][:, 0:128], in1=bBsn_s[:, a, :], op=ALU.add),
                      reads=B_psS[a] + [B_bBsn], writes=[B_tmpB[a]])
                sc.op("act", lambda e, a=a: e.activation(out=eB[a][:, 0:128], in_=tmpB[a][:, 0:128], func=AF.Exp),
                      reads=[B_tmpB[a]], writes=[B_eB])
            for a in range(2):
                sc.op("pe", lambda e, a=a: e.matmul(poB[64 * a:64 * a + 64, 0:128], lhsT=vb[:, 64, 64 * a:64 * a + 64], rhs=eB[a][:, 0:128],
                                                    start=True, stop=False), reads=[B_vb, B_eB], writes=[B_pn])
                sc.op("pe", lambda e, a=a: e.matmul(pzB[64 * a:64 * a + 64, 0:128], lhsT=onesb[:, 0:64], rhs=eB[a][:, 0:128],
                                                    start=True, stop=False), reads=[B_onesb, B_eB], writes=[B_pv[0]])
            for i in range(4):
                sc.op("sp", lambda e, i=i: e.dma_start(out=xsl[0][:, 0:512].rearrange("p (t c) -> p t c", c=128),
                                                       in_=cbk[i].rearrange("(t p) c -> p t c", p=128)), writes=[B_xsl[0]], dma=True)
                sc.op("sp", lambda e, i=i: e.dma_start(out=xsl[1][:, 0:512].rearrange("p (t c) -> p t c", c=128),
                                                       in_=cbv[i].rearrange("(t p) c -> p t c", p=128)), writes=[B_xsl[1]], dma=True)
                for t in range(4):
                    sc.op("pe", lambda e, t=t: e.transpose(out=pt[:, t, :], in_=xsl[0][:, t * 128:(t + 1) * 128], identity=ident_s[:]),
                          reads=[B_xsl[0], B_ident], writes=[B_pt])
                sc.op("act", lambda e: e.copy(out=kcT[:, 0:512], in_=PA[:, 0, :]), reads=[B_pt], writes=[B_kcT])
                sc.op("dve", lambda e: e.tensor_copy(out=vc[:, 0:4, :], in_=xsl[1][:, 0:512].rearrange("p (t c) -> p t c", c=128)),
                      reads=[B_xsl[1]], writes=[B_vc])
                pbs = [PB[:, a, 0:128].rearrange("p (k q) -> p k q", k=4) for a in range(2)]
                for kb in range(4):
                    for a in range(2):
                        sc.op("pe", lambda e, kb=kb, a=a, i=i: e.matmul(
                            pbs[a][:, kb, :], lhsT=kcT[64 * a:64 * a + 64, kb * 128:(kb + 1) * 128],
                            rhs=qbT[64 * a:64 * a + 64, TS0 + 32 * i:TS0 + 32 * i + 32], start=True, stop=True),
                            reads=[B_kcT, B_qbT], writes=[B_pf[a]])
                stb = kf32[:, 0:256].rearrange("p (a k q) -> p a k q", a=2, k=4)
                esb = es_s[:].rearrange("p k m q -> p (k m q)")[:, 0:256].rearrange("p (a k q) -> p a k q", a=2, k=4)
                for a in range(2):
                    sc.op("dve", lambda e, a=a: e.tensor_tensor(out=stb[:, a, :, :], in0=pbs[a], in1=bBs_s[:, a, :, :], op=ALU.add),
                          reads=[B_pf[a], B_bBs], writes=[B_kf32])
                sc.op("act", lambda e: e.activation(out=esb, in_=stb, func=AF.Exp), reads=[B_kf32], writes=[B_es])
                for kb in range(4):
                    for a in range(2):
                        sc.op("pe", lambda e, kb=kb, a=a, i=i: e.matmul(
                            poB[64 * a:64 * a + 64, 32 * i:32 * i + 32], lhsT=vc[:, kb, 64 * a:64 * a + 64], rhs=esb[:, a, kb, :],
                            start=False, stop=(i == 3 and kb == 3)), reads=[B_vc, B_es], writes=[B_pn])
                        sc.op("pe", lambda e, kb=kb, a=a, i=i: e.matmul(
                            pzB[64 * a:64 * a + 64, 32 * i:32 * i + 32], lhsT=onesb[:, 0:64], rhs=esb[:, a, kb, :],
                            start=False, stop=(i == 3 and kb == 3)), reads=[B_onesb, B_es], writes=[B_pv[0]])
            finalize_B(128, 0, TS0)
            write_slots(lambda e, o_ap, m_ap: e.tensor_scalar(out=o_ap, in0=gB[:, 0:128], scalar1=m_ap, scalar2=None, op0=ALU.mult),
                        128, TS0, 4, [B_sq])

            do_rs(rs_srcB2, rs_dstB2, B_srcB, B_dstB)
            B_wout = fence(sc.buf("wout"), [B_qbT]); wout = qbT[:, 0:8192].rearrange("p (k n) -> p k n", k=8)
            B_wmg = fence(sc.buf("wmg"), [B_kbT, B_gbT])
            wmg = [kbT[:, 0:8192].rearrange("p (k n) -> p k n", k=4), gbT[:, 0:8192].rearrange("p (k n) -> p k n", k=4)]
            vbf = vb[:].rearrange("p t e -> p (t e)")
            B_woa = fence(sc.buf("woa"), [B_vb]); woa = vbf[:, 0:4096].rearrange("p (k n) -> p k n", k=4)
            B_wob = fence(sc.buf("wob"), [B_vb]); wob = vbf[:, 4096:8192].rearrange("p (k n) -> p k n", k=4)
            wpieces = []
            for kc in range(4):
                wpieces.append(lambda kc=kc: sc.op("pool", lambda e: e.dma_start(out=woa[:, kc, :], in_=w_oa[kc * 128:(kc + 1) * 128, :]),
                                                   writes=[B_woa], dma=True))
                wpieces.append(lambda kc=kc: sc.op("pool", lambda e: e.dma_start(out=wob[:, kc, :], in_=w_ob[kc * 128:(kc + 1) * 128, :]),
                                                   writes=[B_wob], dma=True))
            for kc in range(8):
                wpieces.append(lambda kc=kc: sc.op("pool", lambda e: e.dma_start(out=wout[:, kc, :], in_=w_out[kc * 128:(kc + 1) * 128, :]),
                                                   writes=[B_wout], dma=True))
            for kc in range(8):
                for half in range(2):
                    wpieces.append(lambda kc=kc, half=half: sc.op("pool", lambda e: e.dma_start(
                        out=wmg[kc // 4][:, kc % 4, half * 1024:(half + 1) * 1024],
                        in_=w_mg[kc * 128:(kc + 1) * 128, half * 1024:(half + 1) * 1024]), writes=[B_wmg], dma=True))

            units = [(g, kb) for g in range(16) for kb in range(4 * g + 4)]
            zacc = xn[0][:].rearrange("p (a b) -> p a b", a=2); B_zacc = B_xn[0]

            def emit_qk(u):
                g, kb = units[u]
                si = u % 2
                for m_ in range(2):
                    sc.op("pe", lambda e, m_=m_, si=si, kb=kb, g=g: e.matmul(
                        psS[si][:, m_, :], lhsT=kaT[64 * m_:64 * m_ + 64, kb * 128:(kb + 1) * 128],
                        rhs=qaT[64 * m_:64 * m_ + 64, g * 512:(g + 1) * 512], start=True, stop=True),
                        reads=[B_kaT, B_qaT], writes=B_psS[si])

            pending = []
            zb = hT[1][:, 6:8, :]; B_zb = B_es

            def emit_exp(u):
                g, kb = units[u]
                si = u % 2
                ei = u % 3
                iv = kb - 4 * g
                if iv >= -1:
                    sc.op("act", lambda e, si=si, ei=ei: e.activation(out=e_t[ei], in_=psS[si][:], func=AF.Exp),
                          reads=B_psS[si], writes=[B_e[ei]])
                    for m_ in range(2):
                        sc.op("dve", lambda e, m_=m_, ei=ei, iv=iv: e.tensor_tensor(out=e_t[ei][:, m_, :], in0=e_t[ei][:, m_, :],
                                                                                    in1=EBA[:, iv + 1, :], op=ALU.mult),
                              reads=[B_e[ei], B_EBA], writes=[B_e[ei]])
                else:
                    sc.op("act", lambda e, si=si, ei=ei: e.activation(out=e_t[ei], in_=psS[si][:], func=AF.Exp, bias=c15_s[:, 0:1]),
                          reads=B_psS[si] + [B_c15], writes=[B_e[ei]])

            def emit_pv(u):
                g, kb = units[u]
                ei = u % 3
                nkb = 4 * g + 4
                for m_ in range(2):
                    sc.op("pe", lambda e, m_=m_, ei=ei, kb=kb, nkb=nkb: e.matmul(
                        po[:, m_, :], lhsT=va[:, kb, :], rhs=e_t[ei][:, m_, :], start=(kb == 0), stop=(kb == nkb - 1)),
                        reads=[B_va, B_e[ei]], writes=B_po)
                if kb == 0:
                    sc.op("dve", lambda e, ei=ei: e.tensor_copy(out=zacc, in_=e_t[ei]), reads=[B_e[ei]], writes=[B_zacc])
                elif kb == nkb - 1:
                    sc.op("dve", lambda e, ei=ei: e.tensor_tensor(out=zb, in0=zacc, in1=e_t[ei], op=ALU.add),
                          reads=[B_e[ei], B_zacc], writes=[B_zb])
                else:
                    sc.op("dve", lambda e, ei=ei: e.tensor_tensor(out=zacc, in0=zacc, in1=e_t[ei], op=ALU.add),
                          reads=[B_e[ei], B_zacc], writes=[B_zacc])
                if kb == nkb - 1:
                    while pending:
                        pending.pop(0)()
                    def pz_mm():
                        for m_ in range(2):
                            sc.op("pe", lambda e, m_=m_: e.matmul(pz[:, m_, :], lhsT=onesb[:], rhs=zb[:, m_, :], start=True, stop=True),
                                  reads=[B_onesb, B_zb], writes=B_pz)
                    hold["pz"] = pz_mm
                    steps = finalize_A_steps(512, g * 512, pz[:, 0, :], [B_pz[0]])
                    hold["sa"] = steps[0]
                    hold["n"] = 2
                    for st_ in steps[1:]:
                        pending.append(st_)
                        pending.append(lambda: None)
                elif pending:
                    pending.pop(0)()

            hold = {"n": 0, "sa": None, "pz": None}
            held = []
            emit_qk(0)
            for u in range(len(units)):
                if u + 1 < len(units):
                    emit_qk(u + 1)
                emit_exp(u)
                if u >= 60 and u % 12 == 0 and wpieces:
                    wpieces.pop(0)()
                if hold["pz"] is not None:
                    hold["pz"]()
                    hold["pz"] = None
                if hold["n"] > 0:
                    held.append(u)
                    hold["n"] -= 1
                    if hold["n"] == 0:
                        hold["sa"]()
                        hold["sa"] = None
                        for v in held:
                            emit_pv(v)
                        held = []
                else:
                    emit_pv(u)
            if hold["pz"] is not None:
                hold["pz"]()
            if hold["sa"] is not None:
                hold["sa"]()
            while pending:
                pending.pop(0)()

            while wpieces:
                wpieces.pop(0)()
            TS0 = 8192
            for m_ in range(2):
                sc.op("pe", lambda e, m_=m_: e.matmul(psS[0][:, m_, 0:128], lhsT=kaT[64 * m_:64 * m_ + 64, TS0:TS0 + 128],
                                                      rhs=qaT[64 * m_:64 * m_ + 64, TS0:TS0 + 128], start=True, stop=True),
                      reads=[B_kaT, B_qaT], writes=B_psS[0])
                sc.op("dve", lambda e, m_=m_: e.tensor_tensor(out=dtmp[:, m_, 0:128], in0=psS[0][:, m_, 0:128], in1=bAsn_s[:],
                                                              op=ALU.add), reads=B_psS[0] + [B_bAsn], writes=[B_dtmp])
            sc.op("act", lambda e: e.activation(out=e_t[0][:, :, 0:128], in_=dtmp[:, :, 0:128], func=AF.Exp),
                  reads=[B_dtmp], writes=[B_e[0]])
            for m_ in range(2):
                sc.op("pe", lambda e, m_=m_: e.matmul(po[:, m_, 0:128], lhsT=va[:, 64, :], rhs=e_t[0][:, m_, 0:128],
                                                      start=True, stop=False), reads=[B_va, B_e[0]], writes=B_po)
                sc.op("pe", lambda e, m_=m_: e.matmul(pz[:, m_, 0:128], lhsT=onesb[:], rhs=e_t[0][:, m_, 0:128],
                                                      start=True, stop=False), reads=[B_onesb, B_e[0]], writes=B_pz)
            for i in range(4):
                sc.op("sp", lambda e, i=i: e.dma_start(out=xsl[0][:].rearrange("p (t c) -> p t c", c=128),
                                                       in_=cak[i].rearrange("(t p) c -> p t c", p=128)), writes=[B_xsl[0]], dma=True)
                sc.op("sp", lambda e, i=i: e.dma_start(out=xsl[1][:].rearrange("p (t c) -> p t c", c=128),
                                                       in_=cav[i].rearrange("(t p) c -> p t c", p=128)), writes=[B_xsl[1]], dma=True)
                for t in range(8):
                    sc.op("pe", lambda e, t=t: e.transpose(out=pt[:, t, :], in_=xsl[0][:, t * 128:(t + 1) * 128], identity=ident_s[:]),
                          reads=[B_xsl[0], B_ident], writes=[B_pt])
                sc.op("act", lambda e: e.copy(out=kcT, in_=PA[:].rearrange("p a b -> p (a b)")), reads=[B_pt], writes=[B_kcT])
                sc.op("dve", lambda e: e.tensor_copy(out=vc, in_=xsl[1][:].rearrange("p (t c) -> p t c", c=128)),
                      reads=[B_xsl[1]], writes=[B_vc])
                pss = [PB[:, m2, 0:256].rearrange("p (k q) -> p k q", k=8) for m2 in range(2)]
                esv = es_s[:].rearrange("p k m q -> p (k m q)").rearrange("p (m k q) -> p m k q", m=2, k=8)
                for kb in range(8):
                    for m_ in range(2):
                        sc.op("pe", lambda e, kb=kb, m_=m_, i=i: e.matmul(
                            pss[m_][:, kb, :], lhsT=kcT[64 * m_:64 * m_ + 64, kb * 128:(kb + 1) * 128],
                            rhs=qaT[64 * m_:64 * m_ + 64, TS0 + 32 * i:TS0 + 32 * i + 32], start=True, stop=True),
                            reads=[B_kcT, B_qaT], writes=[B_pf[m_]])
                stmp = kf32[:, 0:512].rearrange("p (m k q) -> p m k q", m=2, k=8)
                for m_ in range(2):
                    sc.op("dve", lambda e, m_=m_: e.tensor_tensor(out=stmp[:, m_, :, :], in0=pss[m_], in1=bAs_s[:],
                                                                  op=ALU.add), reads=[B_pf[m_], B_bAs], writes=[B_kf32])
                sc.op("act", lambda e: e.activation(out=esv, in_=stmp, func=AF.Exp), reads=[B_kf32], writes=[B_es])
                for kb in range(8):
                    for m_ in range(2):
                        sc.op("pe", lambda e, kb=kb, m_=m_, i=i: e.matmul(
                            po[:, m_, 32 * i:32 * i + 32], lhsT=vc[:, kb, :], rhs=esv[:, m_, kb, :], start=False, stop=(i == 3 and kb == 7)),
                            reads=[B_vc, B_es], writes=B_po)
                        sc.op("pe", lambda e, kb=kb, m_=m_, i=i: e.matmul(
                            pz[:, m_, 32 * i:32 * i + 32], lhsT=onesb[:], rhs=esv[:, m_, kb, :], start=False, stop=(i == 3 and kb == 7)),
                            reads=[B_onesb, B_es], writes=B_pz)
            finalize_A(128, TS0, pz[:, 0, :], [B_pz[0]])

            do_rs(rs_srcA2_2, rs_dstA2_2, B_srcA2, B_dstA2)
            B_GA1 = fence(sc.buf("GA1"), [B_qaT]); B_GA2 = fence(sc.buf("GA2"), [B_qaT])
            GA = qaT[:, 0:4 * MT].rearrange("p (s c) -> p s c", s=4)
            B_GB = fence(sc.buf("GB"), [B_kaT]); GB = kaT[:, 0:4 * MT].rearrange("p (s c) -> p s c", s=4)
            sc.op("sp", lambda e: e.dma_start(out=GA[:, :, 0:1024], in_=rs_dstA1_2.rearrange("(s p) c -> p s c", s=4)),
                  reads=[B_dstA1], writes=[B_GA1], dma=True)
            sc.op("pool", lambda e: e.dma_start(out=GA[:, :, 1024:MT], in_=rs_dstA2_2.rearrange("(s p) c -> p s c", s=4)),
                  reads=[B_dstA2], writes=[B_GA2], dma=True)
            sc.op("sp", lambda e: e.dma_start(out=GB, in_=rs_dstB.rearrange("s p c -> p s c")), reads=[B_dstB], writes=[B_GB], dma=True)
            mT = hT[0]; B_mT = B_hT[0]
            for bb in B_e + [B_kcT, B_vc]:
                fence(B_mT, [bb])
            hTm = hT[1]; B_hTm = B_hT[1]
            fence(B_hTm, [B_gst])
            B_y = sc.buf("y_out")
            out_bufs.append(B_y)
            zT = kf32; B_zT = B_kf32

            def p5_group(tok0, ncol, ctx):
                ntile = (ncol + 127) // 128
                for t in range(ntile):
                    rows = min(128, ncol - t * 128)
                    sl = t % 2
                    sc.op("sp", lambda e, sl=sl, t=t, rows=rows: e.dma_start(out=xsl[sl][0:rows, :],
                                                                            in_=xm[tok0 + t * 128:tok0 + t * 128 + rows, :]),
                          writes=[B_xsl[sl]], dma=True)
                    sc.op("act", lambda e, sl=sl, rows=rows: e.activation(out=xn[sl][0:rows, :], in_=xsl[sl][0:rows, :], func=AF.Square,
                                                                          accum_out=ssq[0:rows, sl:sl + 1]),
                          reads=[B_xsl[sl]], writes=[B_xn[sl], B_ssq[sl]])
                    sc.op("act", lambda e, sl=sl, rows=rows: e.activation(out=ssq[0:rows, sl:sl + 1], in_=ssq[0:rows, sl:sl + 1],
                                                                          func=AF.Ln, bias=epsc[0:rows, 0:1]),
                          reads=[B_ssq[sl], B_epsc], writes=[B_ssq[sl]])
                    sc.op("act", lambda e, sl=sl, rows=rows: e.activation(out=rstd[0:rows, sl:sl + 1], in_=ssq[0:rows, sl:sl + 1],
                                                                          func=AF.Exp, scale=-0.5),
                          reads=[B_ssq[sl]], writes=[B_rstd[sl]])
                    sc.op("dve", lambda e, sl=sl, rows=rows: e.tensor_scalar(out=xnb[sl][0:rows, :], in0=xsl[sl][0:rows, :],
                                                                             scalar1=rstd[0:rows, sl:sl + 1], scalar2=None, op0=ALU.mult),
                          reads=[B_xsl[sl], B_rstd[sl]], writes=[B_xnb[sl]])
                    for kc in range(8):
                        sc.op("pe", lambda e, sl=sl, kc=kc, rows=rows: e.matmul(pt[:, kc, 0:rows], lhsT=xnb[sl][0:rows, kc * 128:(kc + 1) * 128],
                                                                                rhs=identb[0:rows, 0:rows], start=True, stop=True),
                              reads=[B_xnb[sl], B_identb], writes=[B_pt])
                    for kc in range(8):
                        sc.op("dve", lambda e, kc=kc, t=t, rows=rows: e.tensor_scalar(
                            out=hTm[:, kc, t * 128:t * 128 + rows], in0=pt[:, kc, 0:rows],
                            scalar1=Amod[:, kc, ctx:ctx + 1], scalar2=modT[:, kc, ctx:ctx + 1], op0=ALU.mult, op1=ALU.add),
                            reads=[B_pt, B_Amod, B_modT], writes=[B_hTm])
                for fo in range(8):
                    par = fo % 2
                    Pab = PB if par == 0 else PA
                    B_ab = [B_pf[0], B_pf[1]] if par == 0 else [B_pt, B_pt]
                    Pmg = PC if par == 0 else PD
                    B_mg = [B_pn, B_pk] if par == 0 else [B_pv[0], B_pv[1]]
                    sqx, B_sqx, rsx, B_rsx = sq2[par], B_sq2[par], rs2[par], B_rs2[par]
                    for kc in range(4):
                        sc.op("pe", lambda e, fo=fo, kc=kc, Pab=Pab: e.matmul(Pab[:, 0, 0:ncol], lhsT=woa[:, kc, fo * 128:(fo + 1) * 128],
                                                                              rhs=GA[:, kc, tok0:tok0 + ncol], start=(kc == 0), stop=(kc == 3)),
                              reads=[B_woa, B_GA1 if tok0 < 1024 else B_GA2], writes=[B_ab[0]])
                    for kc in range(4):
                        sc.op("pe", lambda e, fo=fo, kc=kc, Pab=Pab: e.matmul(Pab[:, 1, 0:ncol], lhsT=wob[:, kc, fo * 128:(fo + 1) * 128],
                                                                              rhs=GB[:, kc, tok0:tok0 + ncol], start=(kc == 0), stop=(kc == 3)),
                              reads=[B_wob, B_GB], writes=[B_ab[1]])
                    for half in range(2):
                        for kc in range(8):
                            sc.op("pe", lambda e, fo=fo, kc=kc, half=half, Pmg=Pmg: e.matmul(
                                Pmg[:, half, 0:ncol], lhsT=wmg[kc // 4][:, kc % 4, half * 1024 + fo * 128:half * 1024 + (fo + 1) * 128],
                                rhs=hTm[:, kc, 0:ncol], start=(kc == 0), stop=(kc == 7)),
                                reads=[B_wmg, B_hTm], writes=[B_mg[half]])
                    sc.op("act", lambda e, Pmg=Pmg, sqx=sqx: e.activation(out=sqx[:, 0:ncol], in_=Pmg[:, 0, 0:ncol], func=AF.Sigmoid),
                          reads=[B_mg[0]], writes=[B_sqx])
                    sc.op("act", lambda e, Pmg=Pmg, rsx=rsx: e.activation(out=rsx[:, 0:ncol], in_=Pmg[:, 1, 0:ncol], func=AF.Sigmoid),
                          reads=[B_mg[1]], writes=[B_rsx])
                    sc.op("dve", lambda e, Pab=Pab, sqx=sqx: e.tensor_tensor(out=sqx[:, 0:ncol], in0=Pab[:, 0, 0:ncol], in1=sqx[:, 0:ncol], op=ALU.mult),
                          reads=[B_ab[0], B_sqx], writes=[B_sqx])
                    sc.op("dve", lambda e, Pab=Pab, rsx=rsx: e.tensor_tensor(out=rsx[:, 0:ncol], in0=Pab[:, 1, 0:ncol], in1=rsx[:, 0:ncol], op=ALU.mult),
                          reads=[B_ab[1], B_rsx], writes=[B_rsx])
                    sc.op("dve", lambda e, fo=fo, sqx=sqx, rsx=rsx: e.tensor_tensor(out=mT[:, fo, 0:ncol], in0=sqx[:, 0:ncol], in1=rsx[:, 0:ncol], op=ALU.add),
                          reads=[B_sqx, B_rsx], writes=[B_mT])
                if gate_ctx[0] != ctx:
                    gate_ctx[0] = ctx
                    for fo in range(8):
                        sc.op("dve", lambda e, fo=fo: e.tensor_scalar(out=vst[fo % 2][:, 0:128], in0=ident_s[:], scalar1=modT[:, 16 + fo, ctx:ctx + 1],
                                                                      scalar2=None, op0=ALU.mult), reads=[B_ident, B_modT], writes=[B_vst[fo % 2]])
                        sc.op("pe", lambda e, fo=fo: e.matmul(PD[:, fo // 4, (fo % 4) * 128:(fo % 4 + 1) * 128], lhsT=onesf_s[:], rhs=vst[fo % 2][:, 0:128],
                                                              start=True, stop=True), reads=[B_onesf, B_vst[fo % 2]], writes=[B_pv[fo // 4]])
                    for hf in range(2):
                        sc.op("dve", lambda e, hf=hf: e.tensor_copy(out=gate_bc[hf][:], in_=PD[:, hf, :]), reads=[B_pv[hf]], writes=[B_gbc[hf]])
                for t in range(ntile):
                    rows = min(128, ncol - t * 128)
                    sl = t % 2
                    sc.op("sp", lambda e, sl=sl, t=t, rows=rows: e.dma_start(out=xsl[sl][0:rows, :],
                                                                            in_=xm[tok0 + t * 128:tok0 + t * 128 + rows, :]),
                          writes=[B_xsl[sl]], dma=True)
                    for hf in range(2):
                        Pz = PD[:, hf, :]
                        B_Pz = B_pv[hf]
                        for kc in range(8):
                            sc.op("pe", lambda e, hf=hf, kc=kc, t=t, rows=rows, Pz=Pz: e.matmul(
                                Pz[0:rows, :], lhsT=mT[:, kc, t * 128:t * 128 + rows], rhs=wout[:, kc, hf * 512:(hf + 1) * 512],
                                start=(kc == 0), stop=(kc == 7)), reads=[B_wout, B_mT], writes=[B_Pz])
                        sc.op("dve", lambda e, hf=hf, sl=sl, rows=rows, Pz=Pz: e.tensor_tensor(
                            out=xn[sl][0:rows, hf * 512:(hf + 1) * 512], in0=Pz[0:rows, :], in1=gate_bc[hf][0:rows, :], op=ALU.mult),
                            reads=[B_Pz, B_gbc[hf]], writes=[B_xn[sl]])
                        sc.op("dve", lambda e, hf=hf, sl=sl, rows=rows: e.tensor_tensor(
                            out=xn[sl][0:rows, hf * 512:(hf + 1) * 512], in0=xn[sl][0:rows, hf * 512:(hf + 1) * 512],
                            in1=xsl[sl][0:rows, hf * 512:(hf + 1) * 512], op=ALU.add),
                            reads=[B_xsl[sl], B_xn[sl]], writes=[B_xn[sl]])
                    sc.op("pool", lambda e, sl=sl, t=t, rows=rows: e.dma_start(out=y[tok0 + t * 128:tok0 + t * 128 + rows, :],
                                                                              in_=xn[sl][0:rows, :]),
                          reads=[B_xn[sl]], writes=[B_y], dma=True)

            gate_ctx = [None]
            gate_bc = [tmpn, kout[1][:].rearrange("p a b -> p (a b)")]; B_gbc = [B_tmpn, fence(sc.buf("gbc1"), [B_kout[1]])]
            for gq in range(4):
                p5_group(gq * 512, 512, 0)
            p5_group(2048, 32, 5)


        except _Stop:
            pass
        sc.op("sp", lambda e: e.nop(), reads=out_bufs)
        with nc.Block() as block:
            sc.emit(block)
    return nc


_NC_CACHE = {}


def kernel(x_prompt, x_sample, cache_a_k, cache_a_v, cache_b_k, cache_b_v, c_prompt, c_sample, g_norm, w_ada, b_ada,
           w_in, g_qa, g_ka, lam_q1, lam_k1, lam_q2, lam_k2, g_subln, t5_bias, g_qb, g_kb, rel_bias_b, w_oa, w_ob, w_out):
    f32 = np.float32
    A = lambda a: np.ascontiguousarray(np.asarray(a, dtype=f32))
    x_prompt, x_sample = A(x_prompt), A(x_sample)
    w_in0 = A(w_in)[0]
    if "nc" not in _NC_CACHE:
        _NC_CACHE["nc"] = build_program()
    nc = _NC_CACHE["nc"]

    ident = np.eye(128, dtype=f32)
    bones = np.zeros((128, 128), f32); bones[:64, :64] = 1; bones[64:, 64:] = 1
    p = np.arange(128)[:, None, None]; iv = np.arange(-1, 4)[None, :, None]; jj = np.arange(512)[None, None, :]
    relA = 128 * iv + p - jj
    bktA = _t5_bucket_np(relA)
    visA = (2 * iv + p // 64) <= (jj // 64)
    mA = np.where(visA, 0.0, NEGM).astype(f32)

    p1 = np.arange(128)
    relAs = 128 * np.arange(8)[None, :, None] + p1[:, None, None] - 1024 - np.arange(32)[None, None, :]
    bktAs = _t5_bucket_np(relAs)
    same = (p1[:, None] // 32) == (p1[None, :] // 32)
    relN = (p1[:, None] % 32) - (p1[None, :] % 32)
    bktN = _t5_bucket_np(relN)
    dd = np.arange(5)[None, :, None]; jq = np.arange(128)[None, None, :]; pp = p1[:, None, None]
    relB = 128 * (dd - 4) + pp - jq
    dlt = 2 * (dd - 4) + pp // 64 - jq // 64
    visB = (dlt <= 0) & (dlt >= -8)
    idxB = np.clip(relB, -128, 128) + 128
    relBs = 128 * (np.arange(4)[None, :, None] - 4) + p1[:, None, None] - np.arange(32)[None, None, :]
    idxBs = np.clip(relBs, -128, 128) + 128
    idxBn = np.clip(relN, -128, 128) + 128
    cak_, cav_, cbk_, cbv_ = A(cache_a_k)[0], A(cache_a_v)[0], A(cache_b_k)[0], A(cache_b_v)[0]
    rbb = A(rel_bias_b)[0]

    in_maps = []
    for c in range(8):
        g, h = c // 4, c % 4
        t5h = A(t5_bias)[:, h]
        gv = np.zeros((128, 8), f32)
        gv[:, 0] = np.tile(A(g_qa)[0], 2); gv[:, 1] = np.tile(A(g_ka)[0], 2)
        gv[:, 2] = np.tile(A(g_qb)[0], 2); gv[:, 3] = np.tile(A(g_kb)[0], 2)
        gv[:, 4] = A(g_subln)[0]
        cols = lambda base, w, idx: w_in0[:, base + idx * w: base + (idx + 1) * w]
        wc = np.concatenate([cols(0, 128, h), cols(512, 128, h), cols(1536, 128, h),
                             cols(2048, 128, h), cols(2560, 128, h), cols(3584, 128, h),
                             cols(1024, 128, h), cols(3072, 128, h)], axis=1)
        cv = np.stack([A(c_prompt)[g]] + [A(c_sample)[4 * g + i] for i in range(4)] + [A(c_sample)[c]], axis=1)
        lam = np.stack([A(lam_q1)[0], A(lam_k1)[0], A(lam_q2)[0], A(lam_k2)[0]], axis=0)
        m = {
            "xp": x_prompt[g],
            "xs4": x_sample[4 * g:4 * g + 4].reshape(128, D),
            "xm": np.concatenate([x_prompt[g, 2048 * h:2048 * h + 2048], x_sample[c]], axis=0),
            "cvec": np.ascontiguousarray(cv),
            "w_ada": A(w_ada)[0],
            "b_ada": np.ascontiguousarray(A(b_ada)[0].reshape(24, 128).T),
            "g_norm": np.ascontiguousarray(A(g_norm)[0].reshape(8, 128).T),
            "w_c": np.ascontiguousarray(wc),
            "w_mg": np.ascontiguousarray(w_in0[:, 4096:6144]),
            "w_oa": A(w_oa)[0], "w_ob": A(w_ob)[0], "w_out": A(w_out)[0],
            "gvec": gv,
            "lamv": np.ascontiguousarray(np.broadcast_to(lam[None], (128, 4, 64))),
            "bA": np.ascontiguousarray(np.where(visA, t5h[bktA], NEGM).astype(f32)),
            "mA": mA,
            "msk": np.ascontiguousarray(np.broadcast_to(np.eye(4, dtype=f32)[h][None], (128, 4))),
            "cak": np.ascontiguousarray(cak_[4 * g:4 * g + 4, :, h, :]),
            "cav": np.ascontiguousarray(cav_[4 * g:4 * g + 4, :, h, :]),
            "cbk": np.ascontiguousarray(cbk_[4 * g:4 * g + 4, :, 2 * h:2 * h + 2, :].reshape(4, 512, 128)),
            "cbv": np.ascontiguousarray(cbv_[4 * g:4 * g + 4, :, 2 * h:2 * h + 2, :].reshape(4, 512, 128)),
            "bAs": np.ascontiguousarray(t5h[bktAs].astype(f32)),
            "bAsn": np.ascontiguousarray(np.where(same, t5h[bktN], NEGM).astype(f32)),
            "bB": np.ascontiguousarray(np.stack([np.where(visB, rbb[idxB, 2 * h + a], NEGM).reshape(128, 640) for a in range(2)], axis=1).astype(f32)),
            "bBs": np.ascontiguousarray(np.stack([rbb[idxBs, 2 * h + a] for a in range(2)], axis=1).astype(f32)),
            "bBsn": np.ascontiguousarray(np.stack([np.where(same, rbb[idxBn, 2 * h + a], NEGM) for a in range(2)], axis=1).astype(f32)),
            "onesf": np.ones((128, 128), f32),
            "c15": np.full((128, 1), t5h[15], f32),
            "ident": ident, "bones": bones,
        }
        in_maps.append(m)
    res = run_bass_kernel_spmd(nc, in_maps, core_ids=list(range(8)))
    R = res.results
    yp = np.zeros((2, S, D), f32); ys = np.zeros((8, 32, D), f32)
    akp = np.zeros((1, 2, S, 4, 128), f32); avp = np.zeros((1, 2, S, 4, 128), f32)
    bkp = np.zeros((1, 2, 512, 8, 64), f32); bvp = np.zeros((1, 2, 512, 8, 64), f32)
    aks = np.zeros((1, 8, 32, 4, 128), f32); avs = np.zeros((1, 8, 32, 4, 128), f32)
    bks = np.zeros((1, 8, 32, 8, 64), f32); bvs = np.zeros((1, 8, 32, 8, 64), f32)
    for c in range(8):
        g, h = c // 4, c % 4
        r = R[c]
        yy = np.asarray(r["y"], f32)
        yp[g, 2048 * h:2048 * h + 2048] = yy[:2048]
        ys[c] = yy[2048:]
        akp[0, g, :, h, :] = np.asarray(r["ak"], f32)
        avp[0, g, :, h, :] = np.asarray(r["av"], f32)
        bkp[0, g, :, 2 * h:2 * h + 2, :] = np.asarray(r["bk"], f32).reshape(512, 2, 64)
        bvp[0, g, :, 2 * h:2 * h + 2, :] = np.asarray(r["bv"], f32).reshape(512, 2, 64)
        aks[0, 4 * g:4 * g + 4, :, h, :] = np.asarray(r["aks"], f32).reshape(4, 32, 128)
        avs[0, 4 * g:4 * g + 4, :, h, :] = np.asarray(r["avs"], f32).reshape(4, 32, 128)
        bks[0, 4 * g:4 * g + 4, :, 2 * h:2 * h + 2, :] = np.asarray(r["bks"], f32).reshape(4, 32, 2, 64)
        bvs[0, 4 * g:4 * g + 4, :, 2 * h:2 * h + 2, :] = np.asarray(r["bvs"], f32).reshape(4, 32, 2, 64)
    return (yp, ys, akp, avp, bkp, bvp, aks, avs, bks, bvs)
```

```python
import math
from contextlib import ExitStack

import numpy as np
import concourse.bass as bass
import concourse.mybir as mybir
from concourse.bass_utils import run_bass_kernel_spmd

F32 = mybir.dt.float32
BF16 = mybir.dt.bfloat16
AF = mybir.ActivationFunctionType
ALU = mybir.AluOpType
AX = mybir.AxisListType

D = 1024
S = 8192
NT = 65
TOK = NT * 128
EPS = 1e-6
NEGM = -30000.0
LAM_INIT = 0.8 - 0.6 * math.exp(0.0)
MT = 2048 + 32
STOP_AFTER = None


class Buf:
    def __init__(self, name):
        self.name = name
        self.w = None
        self.r = []
        self.dsem = None
        self.dcnt = 0


class Sched:
    ENGS = ["pe", "act", "dve", "pool", "sp"]

    def __init__(self, nc, stack):
        self.nc = nc
        self.stack = stack
        self.ops = {e: [] for e in self.ENGS}
        self.esem = {e: stack.enter_context(nc.semaphore("es_" + e)) for e in self.ENGS}
        self.nsem = 5

    def buf(self, name):
        return Buf(name)

    def _dsem(self, b):
        if b.dsem is None:
            b.dsem = self.stack.enter_context(self.nc.semaphore("ds_" + b.name))
            self.nsem += 1
        return b.dsem

    def op(self, eng, fn, reads=(), writes=(), dma=False):
        deps = []
        for b in reads:
            if b.w is not None:
                deps.append(b.w)
        for b in writes:
            if b.w is not None:
                deps.append(b.w)
            deps.extend(b.r)
        k = len(self.ops[eng])
        if dma:
            tgt = writes[0] if writes else reads[0]
            sem = self._dsem(tgt)
            tgt.dcnt += 16
            ev = ("d", sem, tgt.dcnt)
        else:
            ev = ("c", eng, k)
        self.ops[eng].append(dict(fn=fn, deps=deps, ev=ev, dma=dma))
        for b in writes:
            b.w = ev
            b.r = []
        for b in reads:
            if b not in writes:
                b.r.append(ev)
        return ev

    def barrier_all(self, bufs):
        evs = []
        for b in bufs:
            if b.w is not None:
                evs.append(b.w)
            evs.extend(b.r)
        return evs

    def emit(self, block):
        need = {e: set() for e in self.ENGS}
        for e in self.ENGS:
            for o in self.ops[e]:
                for d in o["deps"]:
                    if d[0] == "c" and d[1] != e:
                        need[d[1]].add(d[2])
                    elif d[0] == "c" and d[1] == e and e != "pe":
                        need[e].add(d[2])
        rank = {}
        for e in self.ENGS:
            for i, k in enumerate(sorted(need[e])):
                rank[(e, k)] = i + 1
        esem = self.esem

        def run(e, handle):
            seen = {}
            for k, o in enumerate(self.ops[e]):
                for d in o["deps"]:
                    if d[0] == "c":
                        if d[1] == e and e == "pe":
                            continue
                        sem, val = esem[d[1]], rank[(d[1], d[2])]
                    else:
                        sem, val = d[1], d[2]
                    key = id(sem)
                    if seen.get(key, 0) >= val:
                        continue
                    seen[key] = val
                    handle.wait_ge(sem, val)
                ins = o["fn"](handle)
                if o["dma"]:
                    ins.then_inc(o["ev"][1], 16)
                elif (e, k) in rank:
                    ins.then_inc(esem[e], 1)

        @block.tensor
        def _(h):
            run("pe", h)

        @block.scalar
        def _(h):
            run("act", h)

        @block.vector
        def _(h):
            run("dve", h)

        @block.gpsimd
        def _(h):
            run("pool", h)

        @block.sync
        def _(h):
            run("sp", h)


def _t5_bucket_np(rel):
    rel = np.asarray(rel, np.int64)
    half = 16
    ret = np.where(rel > 0, half, 0)
    n = np.abs(rel)
    nf = np.maximum(n, 1).astype(np.float32)
    large = 8 + (np.log(nf / np.float32(8)) / np.float32(math.log(128 / 8)) * np.float32(8)).astype(np.int32)
    large = np.minimum(large, half - 1)
    return ret + np.where(n < 8, n, large)


def build_program():
    nc = bass.Bass("TRN2", target_bir_lowering=False)
    try:
        nc.allow_low_precision("bf16 matmul operands with fp32 accumulation (reference tolerance is bf16-level)")
    except Exception:
        pass
    try:
        nc.allow_non_contiguous_dma("small strided parameter loads")
    except Exception:
        pass

    def din(name, shape, dt=F32):
        return nc.dram_tensor(name, list(shape), dt, kind="ExternalInput").ap()

    def dout(name, shape, dt=F32):
        return nc.dram_tensor(name, list(shape), dt, kind="ExternalOutput").ap()

    xp = din("xp", [S, D])
    xs4 = din("xs4", [128, D])
    xm = din("xm", [MT, D])
    cvec = din("cvec", [D, 6])
    w_ada = din("w_ada", [D, 3 * D])
    b_ada = din("b_ada", [128, 24])
    g_norm = din("g_norm", [128, 8])
    w_c = din("w_c", [D, 1024])
    w_mg = din("w_mg", [D, 2048])
    w_oa = din("w_oa", [512, D])
    w_ob = din("w_ob", [512, D])
    w_out = din("w_out", [D, D])
    gvec = din("gvec", [128, 8])
    lamv = din("lamv", [128, 4, 64])
    bA = din("bA", [128, 5, 512])
    mA = din("mA", [128, 5, 512])
    c15 = din("c15", [128, 1])
    msk = din("msk", [128, 4])
    cak = din("cak", [4, 1024, 128]); cav = din("cav", [4, 1024, 128])
    cbk = din("cbk", [4, 512, 128]); cbv = din("cbv", [4, 512, 128])
    bAs = din("bAs", [128, 8, 32]); bAsn = din("bAsn", [128, 128])
    bB = din("bB", [128, 2, 640]); bBs = din("bBs", [128, 2, 4, 32]); bBsn = din("bBsn", [128, 2, 128])
    onesf = din("onesf", [128, 128])
    ident = din("ident", [128, 128])
    bones = din("bones", [128, 128])
    y = dout("y", [MT, D])
    ak = dout("ak", [S, 128])
    av = dout("av", [S, 128])
    bk = dout("bk", [512, 128])
    bv = dout("bv", [512, 128])
    aks = dout("aks", [128, 128])
    avs = dout("avs", [128, 128])
    bks = dout("bks", [128, 128])
    bvs = dout("bvs", [128, 128])

    with ExitStack() as st:
        sc = Sched(nc, st)

        def sb(name, shape, dt=F32):
            return st.enter_context(nc.sbuf_tensor(name, list(shape), dt))

        def ps(name, shape, dt=F32):
            return st.enter_context(nc.psum_tensor(name, list(shape), dt))

        qaT = sb("qaT", [128, TOK], BF16); B_qaT = sc.buf("qaT")
        kaT = sb("kaT", [128, TOK], BF16); B_kaT = sc.buf("kaT")
        gaT = sb("gaT", [128, TOK], BF16); B_gaT = sc.buf("gaT")
        qbT = sb("qbT", [128, TOK], BF16); B_qbT = sc.buf("qbT")
        kbT = sb("kbT", [128, TOK], BF16); B_kbT = sc.buf("kbT")
        gbT = sb("gbT", [128, TOK], BF16); B_gbT = sc.buf("gbT")
        va = sb("va", [128, NT, 128], BF16); B_va = sc.buf("va")
        vb = sb("vb", [128, NT, 128], BF16); B_vb = sc.buf("vb")
        feat_sb = [qaT, kaT, gaT, qbT, kbT, gbT]
        feat_B = [B_qaT, B_kaT, B_gaT, B_qbT, B_kbT, B_gbT]

        ident_s = sb("ident_s", [128, 128]); B_ident = sc.buf("ident")
        bones_s = sb("bones_s", [128, 128]); B_bones = sc.buf("bones")
        gvec_s = sb("gvec_s", [128, 8]); B_gvec = sc.buf("gvec")
        gsc = sb("gsc", [128, 8]); B_gsc = sc.buf("gsc")
        gn_s = sb("gn_s", [128, 8]); B_gn = sc.buf("gn")
        bada_s = sb("bada_s", [128, 24]); B_bada = sc.buf("bada")
        cv_s = sb("cv_s", [128, 8, 6]); B_cv = sc.buf("cv")
        scv = sb("scv", [128, 8, 6]); B_scv = sc.buf("scv")
        modT = sb("modT", [128, 24, 6]); B_modT = sc.buf("modT")
        Amod = sb("Amod", [128, 8, 6]); B_Amod = sc.buf("Amod")
        wc_s = sb("wc_s", [128, 8, 1024], BF16); B_wc = sc.buf("wc")

        sc.op("sp", lambda e: e.dma_start(out=ident_s[:], in_=ident), writes=[B_ident], dma=True)
        sc.op("sp", lambda e: e.dma_start(out=bones_s[:], in_=bones), writes=[B_bones], dma=True)
        identb = sb("identb", [128, 128], BF16); B_identb = sc.buf("identb")
        bonesb = sb("bonesb", [128, 128], BF16); B_bonesb = sc.buf("bonesb")
        sc.op("pool", lambda e: e.dma_start(out=identb[:], in_=ident), writes=[B_identb], dma=True)
        sc.op("pool", lambda e: e.dma_start(out=bonesb[:], in_=bones), writes=[B_bonesb], dma=True)
        sc.op("sp", lambda e: e.dma_start(out=gvec_s[:], in_=gvec), writes=[B_gvec], dma=True)
        sc.op("sp", lambda e: e.dma_start(out=gn_s[:], in_=g_norm), writes=[B_gn], dma=True)
        sc.op("sp", lambda e: e.dma_start(out=bada_s[:], in_=b_ada), writes=[B_bada], dma=True)
        sc.op("sp", lambda e: e.dma_start(out=cv_s[:], in_=cvec.rearrange("(kc p) n -> p kc n", p=128)),
              writes=[B_cv], dma=True)
        sc.op("pool", lambda e: e.dma_start(out=wc_s[:], in_=w_c.rearrange("(kc p) n -> p kc n", p=128)),
              writes=[B_wc], dma=True)

        sc.op("act", lambda e: e.activation(out=scv[:], in_=cv_s[:], func=AF.Exp, scale=-1.0), reads=[B_cv], writes=[B_scv])
        sc.op("dve", lambda e: e.tensor_scalar(out=scv[:], in0=scv[:], scalar1=1.0, scalar2=None, op0=ALU.add),
              reads=[B_scv], writes=[B_scv])
        sc.op("dve", lambda e: e.reciprocal(out=scv[:], in_=scv[:]), reads=[B_scv], writes=[B_scv])
        sc.op("dve", lambda e: e.tensor_tensor(out=scv[:], in0=scv[:], in1=cv_s[:], op=ALU.mult),
              reads=[B_scv, B_cv], writes=[B_scv])
        xsl = [sb("xsl%d" % i, [128, D]) for i in range(2)]; B_xsl = [sc.buf("xsl%d" % i) for i in range(2)]
        xnb = [sb("xnb%d" % i, [128, D], BF16) for i in range(2)]; B_xnb = [sc.buf("xnb%d" % i) for i in range(2)]
        wada_s = [xnb[i][:].rearrange("p (k n) -> p k n", k=8) for i in range(2)]
        B_wada = B_xnb
        scvb = sb("scvb", [128, 8, 6], BF16); B_scvb = sc.buf("scvb")
        sc.op("dve", lambda e: e.tensor_copy(out=scvb[:], in_=scv[:]), reads=[B_scv], writes=[B_scvb])
        PA = ps("PA", [128, 2, 512]); PB = ps("PB", [128, 2, 512]); PC = ps("PC", [128, 2, 512]); PD = ps("PD", [128, 2, 512])
        pf = [PB[:, i, :] for i in range(2)]; B_pf = [sc.buf("pf%d" % i) for i in range(2)]
        pmod = pf[0][:, 0:192].rearrange("p (m n) -> p m n", n=8); B_pmod = B_pf[0]
        wada_v = w_ada.rearrange("(kc p) n -> p kc n", p=128)
        for m in range(24):
            sl = m % 2
            sc.op("pool", lambda e, m=m, sl=sl: e.dma_start(out=wada_s[sl], in_=wada_v[:, :, m * 128:(m + 1) * 128]),
                  writes=[B_wada[sl]], dma=True)
            for kc in range(8):
                sc.op("pe", lambda e, m=m, sl=sl, kc=kc: e.matmul(pmod[:, m, 0:6], lhsT=wada_s[sl][:, kc, :],
                                                                  rhs=scvb[:, kc, :], start=(kc == 0), stop=(kc == 7)),
                      reads=[B_wada[sl], B_scvb], writes=[B_pmod])
        for m in range(24):
            sc.op("dve", lambda e, m=m: e.tensor_scalar(out=modT[:, m, :], in0=pmod[:, m, 0:6],
                                                        scalar1=bada_s[:, m:m + 1], scalar2=None, op0=ALU.add),
                  reads=[B_pmod, B_bada], writes=[B_modT])
        sc.op("dve", lambda e: e.tensor_scalar(out=Amod[:], in0=modT[:, 8:16, :], scalar1=1.0, scalar2=32.0,
                                               op0=ALU.add, op1=ALU.mult), reads=[B_modT], writes=[B_Amod])
        for kc in range(8):
            sc.op("dve", lambda e, kc=kc: e.tensor_scalar(out=Amod[:, kc, :], in0=Amod[:, kc, :],
                                                          scalar1=gn_s[:, kc:kc + 1], scalar2=None, op0=ALU.mult),
                  reads=[B_gn], writes=[B_Amod])
        sc.op("dve", lambda e: e.tensor_scalar(out=gsc[:, 0:4], in0=gvec_s[:, 0:4], scalar1=1.0, scalar2=None,
                                               op0=ALU.mult), reads=[B_gvec], writes=[B_gsc])
        sc.op("dve", lambda e: e.tensor_scalar(out=gsc[:, 1:2], in0=gvec_s[:, 1:2], scalar1=8.0, scalar2=None,
                                               op0=ALU.mult), reads=[B_gvec], writes=[B_gsc])
        sc.op("dve", lambda e: e.tensor_scalar(out=gsc[:, 3:4], in0=gvec_s[:, 3:4], scalar1=8.0, scalar2=None,
                                               op0=ALU.mult), reads=[B_gvec], writes=[B_gsc])

        epsc = sb("epsc", [128, 4]); B_epsc = sc.buf("epsc")
        sc.op("pool", lambda e: e.memset(epsc[:, 0:1], float(D * EPS)), writes=[B_epsc])
        sc.op("pool", lambda e: e.memset(epsc[:, 1:2], float(64 * EPS)), writes=[B_epsc])
        sc.op("pool", lambda e: e.memset(epsc[:, 2:3], float(128 * EPS)), writes=[B_epsc])
        sc.op("pool", lambda e: e.memset(epsc[:, 3:4], 1.0), writes=[B_epsc])
        xn = [sb("xn%d" % i, [128, D]) for i in range(2)]; B_xn = [sc.buf("xn%d" % i) for i in range(2)]
        ssq = sb("ssq", [128, 2]); B_ssq = [sc.buf("ssq0"), sc.buf("ssq1")]
        rstd = sb("rstd", [128, 2]); B_rstd = [sc.buf("rstd0"), sc.buf("rstd1")]
        hT = [sb("hT%d" % i, [128, 8, 512], BF16) for i in range(2)]; B_hT = [sc.buf("hT%d" % i) for i in range(2)]
        pt = PA[:].rearrange("p a (b c) -> p (a b) c", c=128); B_pt = sc.buf("pt")
        pn = PC[:, 0, :]; B_pn = sc.buf("pn")
        pk = PC[:, 1, :].rearrange("p (t c) -> p t c", c=128); B_pk = sc.buf("pk")
        pv = [PD[:, i, :] for i in range(2)]; B_pv = [sc.buf("pv0"), sc.buf("pv1")]
        sq = sb("sq", [128, 512]); B_sq = sc.buf("sq")
        rs = sb("rs", [128, 512]); B_rs = sc.buf("rs")
        tmpn = sb("tmpn", [128, 512]); B_tmpn = sc.buf("tmpn")
        kf32 = sb("kf32", [128, 512]); B_kf32 = sc.buf("kf32")
        kout = [sb("kout%d" % i, [128, 4, 128]) for i in range(2)]; B_kout = [sc.buf("kout%d" % i) for i in range(2)]
        vst = [sb("vst%d" % i, [128, 256]) for i in range(2)]; B_vst = [sc.buf("vst%d" % i) for i in range(2)]
        B_ak = sc.buf("ak_out"); B_av = sc.buf("av_out"); B_bk = sc.buf("bk_out"); B_bv = sc.buf("bv_out")
        out_bufs = [B_ak, B_av, B_bk, B_bv]

        tile_ctr = [0]
        kout_ctr = [0]

        sq2 = [sq, kf32]; B_sq2 = [B_sq, B_kf32]
        sqb = [sb("sqb%d" % i, [128, 512], BF16) for i in range(2)]; B_sqb = [sc.buf("sqb%d" % i) for i in range(2)]
        rs2 = [rs, sb("rs_b", [128, 512])]; B_rs2 = [B_rs, sc.buf("rs_b")]
        nrm_ctr = [0]
        later = []
        later2 = []

        def grp_ntile(gi):
            return 4 if gi < 16 else 1

        def grp_ctx(gi):
            return [(0, 128, 0)] if gi < 16 else [(32 * i, 32 * i + 32, 1 + i) for i in range(4)]

        tile_slot = {}

        def prep_a(gi, t):
            tg = gi * 4 + t
            sl = tile_ctr[0] % 2
            tile_ctr[0] += 1
            tile_slot[(gi, t)] = sl
            src = xp[tg * 128:(tg + 1) * 128, :] if tg < 64 else xs4
            sc.op("sp", lambda e, sl=sl, src=src: e.dma_start(out=xsl[sl][:], in_=src), writes=[B_xsl[sl]], dma=True)
            sc.op("act", lambda e, sl=sl: e.activation(out=xnb[sl][:], in_=xsl[sl][:], func=AF.Square,
                                                       accum_out=ssq[:, sl:sl + 1]),
                  reads=[B_xsl[sl]], writes=[B_xnb[sl], B_ssq[sl]])
            sc.op("act", lambda e, sl=sl: e.activation(out=ssq[:, sl:sl + 1], in_=ssq[:, sl:sl + 1], func=AF.Ln,
                                                       bias=epsc[:, 0:1]),
                  reads=[B_ssq[sl], B_epsc], writes=[B_ssq[sl]])
            sc.op("act", lambda e, sl=sl: e.activation(out=rstd[:, sl:sl + 1], in_=ssq[:, sl:sl + 1], func=AF.Exp, scale=-0.5),
                  reads=[B_ssq[sl]], writes=[B_rstd[sl]])
            sc.op("act", lambda e, sl=sl: e.mul(out=xnb[sl][:], in_=xsl[sl][:], mul=rstd[:, sl:sl + 1]),
                  reads=[B_xsl[sl], B_rstd[sl]], writes=[B_xnb[sl]])

        def prep_b(gi, t):
            hs = gi % 2
            sl = tile_slot[(gi, t)]
            for kc in range(8):
                sc.op("pe", lambda e, sl=sl, kc=kc: e.matmul(pt[:, kc, :], lhsT=xnb[sl][:, kc * 128:(kc + 1) * 128], rhs=identb[:],
                                                             start=True, stop=True),
                      reads=[B_xnb[sl], B_identb], writes=[B_pt])
            if gi < 16:
                xv = xn[sl][:].rearrange("p (k n) -> p k n", k=8)
                sc.op("dve", lambda e, xv=xv: e.tensor_tensor(out=xv, in0=pt, in1=Amod[:, :, 0:1].to_broadcast([128, 8, 128]), op=ALU.mult),
                      reads=[B_pt, B_Amod], writes=[B_xn[sl]])
                sc.op("dve", lambda e, xv=xv, t=t: e.tensor_tensor(out=hT[hs][:, :, t * 128:(t + 1) * 128], in0=xv,
                                                                   in1=modT[:, 0:8, 0:1].to_broadcast([128, 8, 128]), op=ALU.add),
                      reads=[B_xn[sl], B_modT], writes=[B_hT[hs]])
            for (c0, c1, ctx) in (grp_ctx(gi) if gi >= 16 else []):
                for kc in range(8):
                    sc.op("dve", lambda e, kc=kc, c0=c0, c1=c1, ctx=ctx, t=t: e.tensor_scalar(
                        out=hT[hs][:, kc, t * 128 + c0:t * 128 + c1], in0=pt[:, kc, c0:c1],
                        scalar1=Amod[:, kc, ctx:ctx + 1], scalar2=modT[:, kc, ctx:ctx + 1],
                        op0=ALU.mult, op1=ALU.add),
                        reads=[B_pt, B_Amod, B_modT], writes=[B_hT[hs]])

        def feat(gi, f):
            hs = gi % 2
            ntile = grp_ntile(gi)
            ncol = ntile * 128
            tok0 = gi * 512
            pfi = f % 2
            for kc in range(8):
                sc.op("pe", lambda e, f=f, kc=kc, pfi=pfi: e.matmul(pf[pfi][:, 0:ncol], lhsT=wc_s[:, kc, f * 128:(f + 1) * 128],
                                                                    rhs=hT[hs][:, kc, 0:ncol], start=(kc == 0), stop=(kc == 7)),
                      reads=[B_wc, B_hT[hs]], writes=[B_pf[pfi]])
            dst = feat_sb[f]; Bd = feat_B[f]
            if f in (2, 5):
                sc.op("act", lambda e, pfi=pfi: e.activation(out=tmpn[:, 0:ncol], in_=pf[pfi][:, 0:ncol], func=AF.Exp, scale=-1.0),
                      reads=[B_pf[pfi]], writes=[B_tmpn])
                sc.op("act", lambda e: e.activation(out=tmpn[:, 0:ncol], in_=tmpn[:, 0:ncol], func=AF.Ln, bias=epsc[:, 3:4]),
                      reads=[B_tmpn, B_epsc], writes=[B_tmpn])
                sc.op("act", lambda e: e.activation(out=tmpn[:, 0:ncol], in_=tmpn[:, 0:ncol], func=AF.Exp, scale=-1.0),
                      reads=[B_tmpn], writes=[B_tmpn])
                sc.op("dve", lambda e, pfi=pfi, dst=dst: e.tensor_tensor(out=dst[:, tok0:tok0 + ncol], in0=pf[pfi][:, 0:ncol],
                                                                         in1=tmpn[:, 0:ncol], op=ALU.mult),
                      reads=[B_pf[pfi], B_tmpn], writes=[Bd])
                return
            gcol = {0: 0, 1: 1, 3: 2, 4: 3}[f]
            ni = nrm_ctr[0] % 2
            nrm_ctr[0] += 1
            sqx, B_sqx, rsx, B_rsx = sqb[ni], B_sqb[ni], rs2[ni], B_rs2[ni]
            sc.op("act", lambda e, pfi=pfi: e.activation(out=sqx[:, 0:ncol], in_=pf[pfi][:, 0:ncol], func=AF.Square),
                  reads=[B_pf[pfi]], writes=[B_sqx])
            later.append(lambda: feat_b(gi, f, ni))

        def feat_b(gi, f, ni):
            hs = gi % 2
            ntile = grp_ntile(gi)
            ncol = ntile * 128
            tok0 = gi * 512
            pfi = f % 2
            dst = feat_sb[f]; Bd = feat_B[f]
            gcol = {0: 0, 1: 1, 3: 2, 4: 3}[f]
            sqx, B_sqx, rsx, B_rsx = sqb[ni], B_sqb[ni], rs2[ni], B_rs2[ni]
            sc.op("pe", lambda e: e.matmul(pn[:, 0:ncol], lhsT=bonesb[:], rhs=sqx[:, 0:ncol], start=True, stop=True),
                  reads=[B_bonesb, B_sqx], writes=[B_pn])
            sc.op("act", lambda e: e.activation(out=rsx[:, 0:ncol], in_=pn[:, 0:ncol], func=AF.Ln, bias=epsc[:, 1:2]),
                  reads=[B_pn, B_epsc], writes=[B_rsx])
            sc.op("act", lambda e: e.activation(out=rsx[:, 0:ncol], in_=rsx[:, 0:ncol], func=AF.Exp, scale=-0.5),
                  reads=[B_rsx], writes=[B_rsx])
            if f in (0, 3):
                sc.op("dve", lambda e, pfi=pfi, dst=dst, gcol=gcol: e.scalar_tensor_tensor(
                    out=dst[:, tok0:tok0 + ncol], in0=pf[pfi][:, 0:ncol], scalar=gsc[:, gcol:gcol + 1], in1=rsx[:, 0:ncol],
                    op0=ALU.mult, op1=ALU.mult), reads=[B_pf[pfi], B_rsx, B_gsc], writes=[Bd])
                return
            sc.op("dve", lambda e, pfi=pfi, gcol=gcol: e.scalar_tensor_tensor(
                out=kf32[:, 0:ncol], in0=pf[pfi][:, 0:ncol], scalar=gsc[:, gcol:gcol + 1], in1=rsx[:, 0:ncol],
                op0=ALU.mult, op1=ALU.mult), reads=[B_pf[pfi], B_rsx, B_gsc], writes=[B_kf32])
            sc.op("pool", lambda e, dst=dst: e.tensor_copy(out=dst[:, tok0:tok0 + ncol], in_=kf32[:, 0:ncol]),
                  reads=[B_kf32], writes=[Bd])
            need_out = (f == 1) or (gi >= 15)
            if need_out:
                later2.append(lambda: feat_c(gi, f))

        def feat_c(gi, f):
            ntile = grp_ntile(gi)
            tok0 = gi * 512
            if True:
                ko = kout_ctr[0] % 2
                kout_ctr[0] += 1
                for t in range(ntile):
                    sc.op("pe", lambda e, t=t: e.transpose(out=pk[:, t, :], in_=kf32[:, t * 128:(t + 1) * 128],
                                                           identity=ident_s[:]),
                          reads=[B_kf32, B_ident], writes=[B_pk])
                sc.op("dve", lambda e, ko=ko: e.tensor_copy(out=kout[ko][:, 0:ntile, :], in_=pk[:, 0:ntile, :]),
                      reads=[B_pk], writes=[B_kout[ko]])
                if gi < 16:
                    if f == 1:
                        dstd = ak[tok0:tok0 + 512, :].rearrange("(t p) e -> p t e", p=128); Bo = B_ak
                    else:
                        dstd = bk.rearrange("(t p) e -> p t e", p=128); Bo = B_bk
                    sc.op("pool", lambda e, ko=ko, dstd=dstd: e.dma_start(out=dstd, in_=kout[ko][:]),
                          reads=[B_kout[ko]], writes=[Bo], dma=True)
                else:
                    dstd = aks if f == 1 else bks
                    Bo = B_ak if f == 1 else B_bk
                    sc.op("pool", lambda e, ko=ko, dstd=dstd: e.dma_start(out=dstd, in_=kout[ko][:, 0, :]),
                          reads=[B_kout[ko]], writes=[Bo], dma=True)

        def vtile(gi, t):
            hs = gi % 2
            tg = gi * 4 + t
            pvi = tg % 2
            for kc in range(8):
                sc.op("pe", lambda e, t=t, kc=kc, pvi=pvi: e.matmul(pv[pvi][:, 0:256], lhsT=hT[hs][:, kc, t * 128:(t + 1) * 128],
                                                                    rhs=wc_s[:, kc, 768:1024], start=(kc == 0), stop=(kc == 7)),
                      reads=[B_wc, B_hT[hs]], writes=[B_pv[pvi]])
            sc.op("dve", lambda e, pvi=pvi: e.tensor_copy(out=vst[pvi][:], in_=pv[pvi][:, 0:256]),
                  reads=[B_pv[pvi]], writes=[B_vst[pvi]])
            sc.op("pool", lambda e, pvi=pvi, tg=tg: e.tensor_copy(out=va[:, tg, :], in_=vst[pvi][:, 0:128]),
                  reads=[B_vst[pvi]], writes=[B_va])
            sc.op("pool", lambda e, pvi=pvi, tg=tg: e.tensor_copy(out=vb[:, tg, :], in_=vst[pvi][:, 128:256]),
                  reads=[B_vst[pvi]], writes=[B_vb])
            if tg < 64:
                sc.op("pool", lambda e, pvi=pvi, tg=tg: e.dma_start(out=av[tg * 128:(tg + 1) * 128, :], in_=vst[pvi][:, 0:128]),
                      reads=[B_vst[pvi]], writes=[B_av], dma=True)
                if tg >= 60:
                    sc.op("pool", lambda e, pvi=pvi, tg=tg: e.dma_start(out=bv[(tg - 60) * 128:(tg - 59) * 128, :],
                                                                        in_=vst[pvi][:, 128:256]),
                          reads=[B_vst[pvi]], writes=[B_bv], dma=True)
            else:
                sc.op("pool", lambda e, pvi=pvi: e.dma_start(out=avs, in_=vst[pvi][:, 0:128]),
                      reads=[B_vst[pvi]], writes=[B_av], dma=True)
                sc.op("pool", lambda e, pvi=pvi: e.dma_start(out=bvs, in_=vst[pvi][:, 128:256]),
                      reads=[B_vst[pvi]], writes=[B_bv], dma=True)

        import os
        NG = int(os.environ.get("DBG_NG", "16"))
        NGRP = NG + 1 if NG == 16 else NG
        tiles = [(gi, t) for gi in range(NGRP) for t in range(grp_ntile(gi))]
        tidx = {tl: i for i, tl in enumerate(tiles)}
        a_done = [0]

        def ensure_a(n):
            while a_done[0] < min(n, len(tiles)):
                prep_a(*tiles[a_done[0]])
                a_done[0] += 1

        for t in range(grp_ntile(0)):
            ensure_a(tidx[(0, t)] + 2)
            prep_b(0, t)
        for gi in range(NGRP):
            n_next = grp_ntile(gi + 1) if gi + 1 < NGRP else 0
            for i in range(6):
                if i < n_next:
                    ensure_a(tidx[(gi + 1, i)] + 2)
                feat(gi, i)
                while later2:
                    later2.pop(0)()
                if i < grp_ntile(gi):
                    vtile(gi, i)
                if i < n_next:
                    prep_b(gi + 1, i)
                while later:
                    later.pop(0)()
        while later2:
            later2.pop(0)()

        class _Stop(Exception):
            pass
        STAGE = int(os.environ.get("DBG_STAGE", "99"))

        def stage(k):
            if STAGE <= k:
                raise _Stop()
        try:
          if NG == 16:
            stage(1)
            def fence(newb, oldbs):
                for ob in oldbs:
                    if ob.w is not None:
                        newb.r.append(ob.w)
                    newb.r.extend(ob.r)
                return newb

            rs_srcA1_2 = nc.dram_tensor("rs_srcA1", [4 * 4 * 128, 1024], BF16).ap()
            rs_dstA1_2 = nc.dram_tensor("rs_dstA1", [4 * 128, 1024], BF16).ap()
            rs_srcA2_2 = nc.dram_tensor("rs_srcA2", [4 * 4 * 128, 1056], BF16).ap()
            rs_dstA2_2 = nc.dram_tensor("rs_dstA2", [4 * 128, 1056], BF16).ap()
            rs_srcA1 = rs_srcA1_2.rearrange("(j s p) c -> j s p c", j=4, s=4)
            rs_srcA2 = rs_srcA2_2.rearrange("(j s p) c -> j s p c", j=4, s=4)
            B_srcA1 = sc.buf("rs_srcA1"); B_dstA1 = sc.buf("rs_dstA1"); B_srcA2 = sc.buf("rs_srcA2"); B_dstA2 = sc.buf("rs_dstA2")
            rs_srcB2 = nc.dram_tensor("rs_srcB", [4 * 4 * 128, MT], BF16).ap()
            rs_dstB2 = nc.dram_tensor("rs_dstB", [4 * 128, MT], BF16).ap()
            rs_srcs = {4: rs_srcB2.rearrange("(j s p) c -> j s p c", j=4, s=4)}
            rs_dstB = rs_dstB2.rearrange("(s p) c -> s p c", s=4)
            B_srcB = sc.buf("rs_srcB"); B_dstB = sc.buf("rs_dstB")
            B_srcs = {4: B_srcB}

            def do_rs(src2, dst2, Bs, Bd):
                if os.environ.get("DBG_NOCC"):
                    sc.op("pool", lambda e: e.dma_start(out=dst2, in_=src2[0:4 * 128, :]), reads=[Bs], writes=[Bd], dma=True)
                else:
                    sc.op("pool", lambda e: e.collective_compute("ReduceScatter", ALU.add, replica_groups=[[0, 1, 2, 3], [4, 5, 6, 7]],
                                                                 ins=[src2], outs=[dst2]), reads=[Bs], writes=[Bd], dma=False)

            msk_s = sb("msk_s", [128, 4]); B_msk = sc.buf("msk")
            c15_s = sb("c15_s", [128, 1]); B_c15 = sc.buf("c15")
            lam_s = kout[0][:].rearrange("p a b -> p (a b)")[:, 0:256].rearrange("p (a b) -> p a b", a=4)
            B_lam = fence(sc.buf("lam"), [B_kout[0]])
            lamt = sb("lamt", [128, 8]); B_lamt = sc.buf("lamt")
            onesf_s = sb("onesf_s", [128, 128]); B_onesf = sc.buf("onesf")
            onesb = sb("onesb", [128, 128], BF16); B_onesb = sc.buf("onesb")
            bAs_s = sb("bAs_s", [128, 8, 32]); B_bAs = sc.buf("bAs")
            bAsn_s = sb("bAsn_s", [128, 128]); B_bAsn = sc.buf("bAsn")

            bBs_s = sb("bBs_s", [128, 2, 4, 32]); B_bBs = sc.buf("bBs")
            bBsn_s = sb("bBsn_s", [128, 2, 128]); B_bBsn = sc.buf("bBsn")
            es_s = hT[1][:, 6, :].rearrange("p (k m q) -> p k m q", k=8, m=2)
            B_es = fence(sc.buf("es"), [B_hT[1]])
            for (dst_t, src_t, Bb) in [(msk_s, msk, B_msk), (c15_s, c15, B_c15), (lam_s, lamv, B_lam), (onesf_s, onesf, B_onesf),
                                       (bAs_s, bAs, B_bAs), (bAsn_s, bAsn, B_bAsn), (bBs_s, bBs, B_bBs),
                                       (bBsn_s, bBsn, B_bBsn)]:
                sc.op("sp", lambda e, d=dst_t, s_=src_t: e.dma_start(out=d[:], in_=s_), writes=[Bb], dma=True)
            sc.op("pool", lambda e: e.memset(onesb[:], 1.0), writes=[B_onesb])
            sc.op("dve", lambda e: e.tensor_tensor(out=lam_s[:, 0, :], in0=lam_s[:, 0, :], in1=lam_s[:, 1, :], op=ALU.mult),
                  reads=[B_lam], writes=[B_lam])
            sc.op("dve", lambda e: e.tensor_tensor(out=lam_s[:, 2, :], in0=lam_s[:, 2, :], in1=lam_s[:, 3, :], op=ALU.mult),
                  reads=[B_lam], writes=[B_lam])
            sc.op("dve", lambda e: e.reduce_sum(out=lamt[:, 0:1], in_=lam_s[:, 0, :], axis=AX.X), reads=[B_lam], writes=[B_lamt])
            sc.op("dve", lambda e: e.reduce_sum(out=lamt[:, 1:2], in_=lam_s[:, 2, :], axis=AX.X), reads=[B_lam], writes=[B_lamt])
            sc.op("act", lambda e: e.activation(out=lamt[:, 2:4], in_=lamt[:, 0:2], func=AF.Exp), reads=[B_lamt], writes=[B_lamt])
            sc.op("dve", lambda e: e.tensor_tensor(out=lamt[:, 4:5], in0=lamt[:, 3:4], in1=lamt[:, 2:3], op=ALU.subtract),
                  reads=[B_lamt], writes=[B_lamt])
            sc.op("dve", lambda e: e.tensor_scalar(out=lamt[:, 4:5], in0=lamt[:, 4:5], scalar1=-float(LAM_INIT), scalar2=None,
                                                   op0=ALU.add), reads=[B_lamt], writes=[B_lamt])
            sc.op("dve", lambda e: e.tensor_scalar(out=lamt[:, 5:6], in0=gvec_s[:, 4:5], scalar1=float(1.0 - LAM_INIT),
                                                   scalar2=None, op0=ALU.mult), reads=[B_gvec, B_lamt], writes=[B_lamt])

            B_e = [fence(sc.buf("e0"), [B_hT[0]]), fence(sc.buf("e1"), [B_hT[0]]), fence(sc.buf("e2"), [B_hT[1]])]
            e_t = [hT[0][:, 0:2, :], hT[0][:, 2:4, :], hT[1][:, 4:6, :]]
            B_EBA = fence(sc.buf("EBA"), [B_wc])
            EBA = wc_s[:, 2:5, :].rearrange("p a b -> p (a b)")[:, 0:2560].rearrange("p (v c) -> p v c", v=5)
            B_kcT = fence(sc.buf("kcT"), [B_hT[0]]); kcT = hT[0][:, 4:6, :].rearrange("p a b -> p (a b)")
            B_vc = fence(sc.buf("vc"), [B_hT[0]]); vc = hT[0][:, 6:8, :].rearrange("p a (b c) -> p (a b) c", c=128)
            B_gst = fence(sc.buf("gst"), [B_hT[1]]); gst = hT[1][:, 0:4, :]
            B_eB = fence(sc.buf("eB"), [B_wc]); eB = [wc_s[:, 0, 0:640], wc_s[:, 1, 0:640]]
            bA_v = [xsl[0][:, 0:512], xsl[0][:, 512:1024], xsl[1][:, 0:512], xsl[1][:, 512:1024], kf32[:, 0:512]]
            bA_B = [B_xsl[0], B_xsl[0], B_xsl[1], B_xsl[1], B_kf32]
            sc.op("sp", lambda e: e.dma_start(out=xsl[0][:].rearrange("p (a b) -> p a b", a=2), in_=bA[:, 0:2, :]),
                  writes=[B_xsl[0]], dma=True)
            sc.op("sp", lambda e: e.dma_start(out=xsl[1][:].rearrange("p (a b) -> p a b", a=2), in_=bA[:, 2:4, :]),
                  writes=[B_xsl[1]], dma=True)
            sc.op("sp", lambda e: e.dma_start(out=kf32[:], in_=bA[:, 4, :]), writes=[B_kf32], dma=True)
            for i5 in range(5):
                sc.op("act", lambda e, i5=i5: e.activation(out=EBA[:, i5, :], in_=bA_v[i5], func=AF.Exp),
                      reads=[bA_B[i5]], writes=[B_EBA])
            rz = xn[0][:].rearrange("p (a b) -> p a b", a=2); B_rz = B_xn[0]
            dtmp = xn[1][:].rearrange("p (a b) -> p a b", a=2); B_dtmp = B_xn[1]
            psS = [PA, PB]; B_psS = [[B_pt], [B_pf[0], B_pf[1]]]
            po = PC; B_po = [B_pn, B_pk]
            pz = PD; B_pz = [B_pv[0], B_pv[1]]
            o_b = tmpn

            def write_slots(gated_ap_fn, ncol, tok0, slot_base, Bsrcs):
                for s_ in range(4):
                    sc.op("dve", lambda e, s_=s_: gated_ap_fn(e, gst[:, s_, 0:ncol], msk_s[:, s_:s_ + 1]),
                          reads=Bsrcs + [B_msk], writes=[B_gst])
                if tok0 < S:
                    j, c0 = tok0 // 2048, tok0 % 2048
                    if slot_base == 4:
                        dsrc, Bsrc_ = rs_srcs[4][j, :, :, c0:c0 + ncol], B_srcB
                    elif c0 < 1024:
                        dsrc, Bsrc_ = rs_srcA1[j, :, :, c0:c0 + ncol], B_srcA1
                    else:
                        dsrc, Bsrc_ = rs_srcA2[j, :, :, c0 - 1024:c0 - 1024 + ncol], B_srcA2
                    sc.op("pool", lambda e: e.dma_start(out=dsrc.rearrange("s p c -> p s c"), in_=gst[:, :, 0:ncol]),
                          reads=[B_gst], writes=[Bsrc_], dma=True)
                    if slot_base == 0 and tok0 == 13 * 512:
                        do_rs(rs_srcA1_2, rs_dstA1_2, B_srcA1, B_dstA1)
                else:
                    for i in range(4):
                        if slot_base == 4:
                            dsrc, Bsrc_ = rs_srcs[4][i, :, :, 2048:2080], B_srcB
                        else:
                            dsrc, Bsrc_ = rs_srcA2[i, :, :, 1024:1056], B_srcA2
                        sc.op("pool", lambda e, i=i, dsrc=dsrc: e.dma_start(out=dsrc.rearrange("s p c -> p s c"),
                                                                         in_=gst[:, :, 32 * i:32 * i + 32]),
                              reads=[B_gst], writes=[Bsrc_], dma=True)

            def finalize_A_steps(ncol, tok0, ssq_ps, B_ssq_ps):
                def s_a():
                    sc.op("act", lambda e: e.activation(out=dtmp[:, :, 0:ncol], in_=pz[:, :, 0:ncol], func=AF.Ln), reads=B_pz, writes=[B_dtmp])
                    sc.op("act", lambda e: e.activation(out=dtmp[:, :, 0:ncol], in_=dtmp[:, :, 0:ncol], func=AF.Exp, scale=-1.0),
                          reads=[B_dtmp], writes=[B_dtmp])
                    sc.op("dve", lambda e: e.tensor_tensor(out=sq[:, 0:ncol], in0=po[:, 0, 0:ncol], in1=dtmp[:, 0, 0:ncol], op=ALU.mult),
                          reads=B_po + [B_dtmp], writes=[B_sq])
                    sc.op("dve", lambda e: e.tensor_tensor(out=rs[:, 0:ncol], in0=po[:, 1, 0:ncol], in1=dtmp[:, 1, 0:ncol], op=ALU.mult),
                          reads=B_po + [B_dtmp], writes=[B_rs])

                def s_b():
                    sc.op("dve", lambda e: e.scalar_tensor_tensor(out=o_b[:, 0:ncol], in0=rs[:, 0:ncol], scalar=lamt[:, 4:5],
                                                                  in1=sq[:, 0:ncol], op0=ALU.mult, op1=ALU.add),
                          reads=[B_rs, B_sq, B_lamt], writes=[B_tmpn])

                def s_c():
                    sc.op("act", lambda e: e.activation(out=sq[:, 0:ncol], in_=o_b[:, 0:ncol], func=AF.Square),
                          reads=[B_tmpn], writes=[B_sq])

                def s_d():
                    sc.op("pe", lambda e: e.matmul(ssq_ps[:, 0:ncol], lhsT=onesf_s[:], rhs=sq[:, 0:ncol], start=True, stop=True),
                          reads=[B_onesf, B_sq], writes=B_ssq_ps)

                def s_e():
                    sc.op("act", lambda e: e.activation(out=rs[:, 0:ncol], in_=ssq_ps[:, 0:ncol], func=AF.Ln, bias=epsc[:, 2:3]),
                          reads=B_ssq_ps + [B_epsc], writes=[B_rs])
                    sc.op("act", lambda e: e.activation(out=rs[:, 0:ncol], in_=rs[:, 0:ncol], func=AF.Exp, scale=-0.5),
                          reads=[B_rs], writes=[B_rs])

                def s_f():
                    sc.op("dve", lambda e: e.tensor_tensor(out=o_b[:, 0:ncol], in0=o_b[:, 0:ncol], in1=rs[:, 0:ncol], op=ALU.mult),
                          reads=[B_tmpn, B_rs], writes=[B_tmpn])
                    sc.op("dve", lambda e: e.scalar_tensor_tensor(out=sq[:, 0:ncol], in0=o_b[:, 0:ncol], scalar=lamt[:, 5:6],
                                                                  in1=gaT[:, tok0:tok0 + ncol], op0=ALU.mult, op1=ALU.mult),
                          reads=[B_tmpn, B_lamt, B_gaT], writes=[B_sq])

                def s_g():
                    write_slots(lambda e, o_ap, m_ap: e.tensor_scalar(out=o_ap, in0=sq[:, 0:ncol], scalar1=m_ap, scalar2=float(math.sqrt(128.0)),
                                                                      op0=ALU.mult, op1=ALU.mult), ncol, tok0, 0, [B_sq])
                return [s_a, s_b, s_c, s_d, s_e, s_f, s_g]

            def finalize_A(ncol, tok0, ssq_ps, B_ssq_ps):
                for st_ in finalize_A_steps(ncol, tok0, ssq_ps, B_ssq_ps):
                    st_()

            TS0 = 8192
            tmpB = [xn[0], xn[1]]; B_tmpB = [B_xn[0], B_xn[1]]
            psB = [PA[:].rearrange("p a b -> p (a b)"), PB[:].rearrange("p a b -> p (a b)")]
            gB = sq

            poB2 = [PC[:, 0, :], PC[:, 1, :]]; B_poB2 = [B_pn, B_pk]
            pzB2 = [PD[:, 0, :], PD[:, 1, :]]; B_pzB2 = [B_pv[0], B_pv[1]]
            EBB = [wc_s[:, 5, 0:640], wc_s[:, 6, 0:640]]; B_EBB = fence(sc.buf("EBB"), [B_wc])
            B_eBa = [fence(sc.buf("eB0"), [B_wc, B_eB]), fence(sc.buf("eB1"), [B_wc, B_eB])]
            for a in range(2):
                sc.op("sp", lambda e, a=a: e.dma_start(out=xsl[a][:, 0:640], in_=bB[:, a, :]), writes=[B_xsl[a]], dma=True)
                sc.op("act", lambda e, a=a: e.activation(out=EBB[a], in_=xsl[a][:, 0:640], func=AF.Exp), reads=[B_xsl[a]], writes=[B_EBB])

            def finalize_B(ncol, c0, tok0, pi=0):
                sc.op("dve", lambda e: e.reciprocal(out=rs[:, 0:ncol], in_=pzB2[pi][:, 0:ncol]), reads=[B_pzB2[pi]], writes=[B_rs])
                sc.op("dve", lambda e: e.tensor_tensor(out=rs[:, 0:ncol], in0=poB2[pi][:, 0:ncol], in1=rs[:, 0:ncol], op=ALU.mult),
                      reads=[B_poB2[pi], B_rs], writes=[B_rs])
                sc.op("dve", lambda e: e.tensor_tensor(out=gB[:, c0:c0 + ncol], in0=rs[:, 0:ncol], in1=gbT[:, tok0:tok0 + ncol],
                                                       op=ALU.mult), reads=[B_rs, B_gbT], writes=[B_sq])

            eBd = [[eB[0], e_t[0].rearrange("p a b -> p (a b)")[:, 0:640]], [eB[1], e_t[1].rearrange("p a b -> p (a b)")[:, 0:640]]]
            B_eBd = [[B_eBa[0], B_e[0]], [B_eBa[1], B_e[1]]]

            def b_qk(qt):
                d0 = max(0, 4 - qt)
                for d in range(d0, 5):
                    kb = qt + d - 4
                    for a in range(2):
                        sc.op("pe", lambda e, a=a, d=d, kb=kb, qt=qt: e.matmul(
                            psB[a][:, d * 128:(d + 1) * 128], lhsT=kbT[64 * a:64 * a + 64, kb * 128:(kb + 1) * 128],
                            rhs=qbT[64 * a:64 * a + 64, qt * 128:(qt + 1) * 128], start=True, stop=True),
                            reads=[B_kbT, B_qbT], writes=B_psS[a])

            def b_exp(qt, a):
                d0 = max(0, 4 - qt)
                eb, Beb = eBd[a][qt % 2], B_eBd[a][qt % 2]
                sc.op("act", lambda e, a=a, d0=d0, eb=eb: e.activation(out=eb[:, d0 * 128:640], in_=psB[a][:, d0 * 128:640], func=AF.Exp),
                      reads=B_psS[a], writes=[Beb])
                sc.op("dve", lambda e, a=a, d0=d0, eb=eb: e.tensor_tensor(out=eb[:, d0 * 128:640], in0=eb[:, d0 * 128:640],
                                                                          in1=EBB[a][:, d0 * 128:640], op=ALU.mult),
                      reads=[Beb, B_EBB], writes=[Beb])

            def b_pv(qt, a):
                d0 = max(0, 4 - qt)
                pi = qt % 2
                eb, Beb = eBd[a][qt % 2], B_eBd[a][qt % 2]
                for d in range(d0, 5):
                    kb = qt + d - 4
                    sc.op("pe", lambda e, a=a, d=d, kb=kb, d0=d0, pi=pi, eb=eb: e.matmul(
                        poB2[pi][64 * a:64 * a + 64, 0:128], lhsT=vb[:, kb, 64 * a:64 * a + 64], rhs=eb[:, d * 128:(d + 1) * 128],
                        start=(d == d0), stop=(d == 4)), reads=[B_vb, Beb], writes=[B_poB2[pi]])
                    sc.op("pe", lambda e, a=a, d=d, d0=d0, pi=pi, eb=eb: e.matmul(
                        pzB2[pi][64 * a:64 * a + 64, 0:128], lhsT=onesb[:, 0:64], rhs=eb[:, d * 128:(d + 1) * 128],
                        start=(d == d0), stop=(d == 4)), reads=[B_onesb, Beb], writes=[B_pzB2[pi]])

            def b_fin(qt):
                finalize_B(128, (qt % 4) * 128, qt * 128, qt % 2)
                if qt % 4 == 3:
                    write_slots(lambda e, o_ap, m_ap: e.tensor_scalar(out=o_ap, in0=gB[:, 0:512], scalar1=m_ap, scalar2=None,
                                                                      op0=ALU.mult), 512, (qt - 3) * 128, 4, [B_sq])

            b_qk(0)
            for qt in range(64):
                for a in range(2):
                    b_exp(qt, a)
                if qt + 1 < 64:
                    b_qk(qt + 1)
                if qt >= 1:
                    b_fin(qt - 1)
                for a in range(2):
                    b_pv(qt, a)
            b_fin(63)
            poB = poB2[0]; pzB = pzB2[0]
            B_eB = fence(B_eB, B_eBa)

            for a in range(2):
                sc.op("pe", lambda e, a=a: e.matmul(psB[a][:, 0:128], lhsT=kbT[64 * a:64 * a + 64, TS0:TS0 + 128],
                                                    rhs=qbT[64 * a:64 * a + 64, TS0:TS0 + 128], start=True, stop=True),
                      reads=[B_kbT, B_qbT], writes=B_psS[a])
                sc.op("dve", lambda e, a=a: e.tensor_tensor(out=tmpB[a][:, 0:128], in0=psB[a][:, 0:128], in1=bBsn_s[:, a, :], op=ALU.add),
                      reads=B_psS[a] + [B_bBsn], writes=[B_tmpB[a]])
                sc.op("act", lambda e, a=a: e.activation(out=eB[a][:, 0:128], in_=tmpB[a][:, 0:128], func=AF.Exp),
                      reads=[B_tmpB[a]], writes=[B_eB])
            for a in range(2):
                sc.op("pe", lambda e, a=a: e.matmul(poB[64 * a:64 * a + 64, 0:128], lhsT=vb[:, 64, 64 * a:64 * a + 64], rhs=eB[a][:, 0:128],
                                                    start=True, stop=False), reads=[B_vb, B_eB], writes=[B_pn])
                sc.op("pe", lambda e, a=a: e.matmul(pzB[64 * a:64 * a + 64, 0:128], lhsT=onesb[:, 0:64], rhs=eB[a][:, 0:128],
                                                    start=True, stop=False), reads=[B_onesb, B_eB], writes=[B_pv[0]])
            for i in range(4):
                sc.op("sp", lambda e, i=i: e.dma_start(out=xsl[0][:, 0:512].rearrange("p (t c) -> p t c", c=128),
                                                       in_=cbk[i].rearrange("(t p) c -> p t c", p=128)), writes=[B_xsl[0]], dma=True)
                sc.op("sp", lambda e, i=i: e.dma_start(out=xsl[1][:, 0:512].rearrange("p (t c) -> p t c", c=128),
                                                       in_=cbv[i].rearrange("(t p) c -> p t c", p=128)), writes=[B_xsl[1]], dma=True)
                for t in range(4):
                    sc.op("pe", lambda e, t=t: e.transpose(out=pt[:, t, :], in_=xsl[0][:, t * 128:(t + 1) * 128], identity=ident_s[:]),
                          reads=[B_xsl[0], B_ident], writes=[B_pt])
                sc.op("act", lambda e: e.copy(out=kcT[:, 0:512], in_=PA[:, 0, :]), reads=[B_pt], writes=[B_kcT])
                sc.op("dve", lambda e: e.tensor_copy(out=vc[:, 0:4, :], in_=xsl[1][:, 0:512].rearrange("p (t c) -> p t c", c=128)),
                      reads=[B_xsl[1]], writes=[B_vc])
                pbs = [PB[:, a, 0:128].rearrange("p (k q) -> p k q", k=4) for a in range(2)]
                for kb in range(4):
                    for a in range(2):
                        sc.op("pe", lambda e, kb=kb, a=a, i=i: e.matmul(
                            pbs[a][:, kb, :], lhsT=kcT[64 * a:64 * a + 64, kb * 128:(kb + 1) * 128],
                            rhs=qbT[64 * a:64 * a + 64, TS0 + 32 * i:TS0 + 32 * i + 32], start=True, stop=True),
                            reads=[B_kcT, B_qbT], writes=[B_pf[a]])
                stb = kf32[:, 0:256].rearrange("p (a k q) -> p a k q", a=2, k=4)
                esb = es_s[:].rearrange("p k m q -> p (k m q)")[:, 0:256].rearrange("p (a k q) -> p a k q", a=2, k=4)
                for a in range(2):
                    sc.op("dve", lambda e, a=a: e.tensor_tensor(out=stb[:, a, :, :], in0=pbs[a], in1=bBs_s[:, a, :, :], op=ALU.add),
                          reads=[B_pf[a], B_bBs], writes=[B_kf32])
                sc.op("act", lambda e: e.activation(out=esb, in_=stb, func=AF.Exp), reads=[B_kf32], writes=[B_es])
                for kb in range(4):
                    for a in range(2):
                        sc.op("pe", lambda e, kb=kb, a=a, i=i: e.matmul(
                            poB[64 * a:64 * a + 64, 32 * i:32 * i + 32], lhsT=vc[:, kb, 64 * a:64 * a + 64], rhs=esb[:, a, kb, :],
                            start=False, stop=(i == 3 and kb == 3)), reads=[B_vc, B_es], writes=[B_pn])
                        sc.op("pe", lambda e, kb=kb, a=a, i=i: e.matmul(
                            pzB[64 * a:64 * a + 64, 32 * i:32 * i + 32], lhsT=onesb[:, 0:64], rhs=esb[:, a, kb, :],
                            start=False, stop=(i == 3 and kb == 3)), reads=[B_onesb, B_es], writes=[B_pv[0]])
            finalize_B(128, 0, TS0)
            write_slots(lambda e, o_ap, m_ap: e.tensor_scalar(out=o_ap, in0=gB[:, 0:128], scalar1=m_ap, scalar2=None, op0=ALU.mult),
                        128, TS0, 4, [B_sq])

            do_rs(rs_srcB2, rs_dstB2, B_srcB, B_dstB)
            B_wout = fence(sc.buf("wout"), [B_qbT]); wout = qbT[:, 0:8192].rearrange("p (k n) -> p k n", k=8)
            B_wmg = fence(sc.buf("wmg"), [B_kbT, B_gbT])
            wmg = [kbT[:, 0:8192].rearrange("p (k n) -> p k n", k=4), gbT[:, 0:8192].rearrange("p (k n) -> p k n", k=4)]
            vbf = vb[:].rearrange("p t e -> p (t e)")
            B_woa = fence(sc.buf("woa"), [B_vb]); woa = vbf[:, 0:4096].rearrange("p (k n) -> p k n", k=4)
            B_wob = fence(sc.buf("wob"), [B_vb]); wob = vbf[:, 4096:8192].rearrange("p (k n) -> p k n", k=4)
            wpieces = []
            for kc in range(4):
                wpieces.append(lambda kc=kc: sc.op("pool", lambda e: e.dma_start(out=woa[:, kc, :], in_=w_oa[kc * 128:(kc + 1) * 128, :]),
                                                   writes=[B_woa], dma=True))
                wpieces.append(lambda kc=kc: sc.op("pool", lambda e: e.dma_start(out=wob[:, kc, :], in_=w_ob[kc * 128:(kc + 1) * 128, :]),
                                                   writes=[B_wob], dma=True))
            for kc in range(8):
                wpieces.append(lambda kc=kc: sc.op("pool", lambda e: e.dma_start(out=wout[:, kc, :], in_=w_out[kc * 128:(kc + 1) * 128, :]),
                                                   writes=[B_wout], dma=True))
            for kc in range(8):
                for half in range(2):
                    wpieces.append(lambda kc=kc, half=half: sc.op("pool", lambda e: e.dma_start(
                        out=wmg[kc // 4][:, kc % 4, half * 1024:(half + 1) * 1024],
                        in_=w_mg[kc * 128:(kc + 1) * 128, half * 1024:(half + 1) * 1024]), writes=[B_wmg], dma=True))

            units = [(g, kb) for g in range(16) for kb in range(4 * g + 4)]
            zacc = xn[0][:].rearrange("p (a b) -> p a b", a=2); B_zacc = B_xn[0]

            def emit_qk(u):
                g, kb = units[u]
                si = u % 2
                for m_ in range(2):
                    sc.op("pe", lambda e, m_=m_, si=si, kb=kb, g=g: e.matmul(
                        psS[si][:, m_, :], lhsT=kaT[64 * m_:64 * m_ + 64, kb * 128:(kb + 1) * 128],
                        rhs=qaT[64 * m_:64 * m_ + 64, g * 512:(g + 1) * 512], start=True, stop=True),
                        reads=[B_kaT, B_qaT], writes=B_psS[si])

            pending = []
            zb = hT[1][:, 6:8, :]; B_zb = B_es

            def emit_exp(u):
                g, kb = units[u]
                si = u % 2
                ei = u % 3
                iv = kb - 4 * g
                if iv >= -1:
                    sc.op("act", lambda e, si=si, ei=ei: e.activation(out=e_t[ei], in_=psS[si][:], func=AF.Exp),
                          reads=B_psS[si], writes=[B_e[ei]])
                    sc.op("dve", lambda e, ei=ei, iv=iv: e.tensor_tensor(out=e_t[ei], in0=e_t[ei],
                                                                         in1=EBA[:, iv + 1:iv + 2, :].to_broadcast([128, 2, 512]), op=ALU.mult),
                          reads=[B_e[ei], B_EBA], writes=[B_e[ei]])
                else:
                    sc.op("act", lambda e, si=si, ei=ei: e.activation(out=e_t[ei], in_=psS[si][:], func=AF.Exp, bias=c15_s[:, 0:1]),
                          reads=B_psS[si] + [B_c15], writes=[B_e[ei]])

            def emit_pv(u):
                g, kb = units[u]
                ei = u % 3
                nkb = 4 * g + 4
                for m_ in range(2):
                    sc.op("pe", lambda e, m_=m_, ei=ei, kb=kb, nkb=nkb: e.matmul(
                        po[:, m_, :], lhsT=va[:, kb, :], rhs=e_t[ei][:, m_, :], start=(kb == 0), stop=(kb == nkb - 1)),
                        reads=[B_va, B_e[ei]], writes=B_po)
                if kb == 0:
                    sc.op("dve", lambda e, ei=ei: e.tensor_copy(out=zacc, in_=e_t[ei]), reads=[B_e[ei]], writes=[B_zacc])
                elif kb == nkb - 1:
                    sc.op("dve", lambda e, ei=ei: e.tensor_tensor(out=zb, in0=zacc, in1=e_t[ei], op=ALU.add),
                          reads=[B_e[ei], B_zacc], writes=[B_zb])
                else:
                    sc.op("dve", lambda e, ei=ei: e.tensor_tensor(out=zacc, in0=zacc, in1=e_t[ei], op=ALU.add),
                          reads=[B_e[ei], B_zacc], writes=[B_zacc])
                if kb == nkb - 1:
                    while pending:
                        pending.pop(0)()
                    def pz_mm():
                        for m_ in range(2):
                            sc.op("pe", lambda e, m_=m_: e.matmul(pz[:, m_, :], lhsT=onesb[:], rhs=zb[:, m_, :], start=True, stop=True),
                                  reads=[B_onesb, B_zb], writes=B_pz)
                    hold["pz"] = pz_mm
                    steps = finalize_A_steps(512, g * 512, pz[:, 0, :], [B_pz[0]])
                    hold["sa"] = steps[0]
                    hold["n"] = 2
                    for st_ in steps[1:]:
                        pending.append(st_)
                        pending.append(lambda: None)
                elif pending:
                    pending.pop(0)()

            hold = {"n": 0, "sa": None, "pz": None}
            held = []
            emit_qk(0)
            for u in range(len(units)):
                if u + 1 < len(units):
                    emit_qk(u + 1)
                emit_exp(u)
                if u >= 60 and u % 12 == 0 and wpieces:
                    wpieces.pop(0)()
                if hold["pz"] is not None:
                    hold["pz"]()
                    hold["pz"] = None
                if hold["n"] > 0:
                    held.append(u)
                    hold["n"] -= 1
                    if hold["n"] == 0:
                        hold["sa"]()
                        hold["sa"] = None
                        for v in held:
                            emit_pv(v)
                        held = []
                else:
                    emit_pv(u)
            if hold["pz"] is not None:
                hold["pz"]()
            if hold["sa"] is not None:
                hold["sa"]()
            while pending:
                pending.pop(0)()

            while wpieces:
                wpieces.pop(0)()
            TS0 = 8192
            for m_ in range(2):
                sc.op("pe", lambda e, m_=m_: e.matmul(psS[0][:, m_, 0:128], lhsT=kaT[64 * m_:64 * m_ + 64, TS0:TS0 + 128],
                                                      rhs=qaT[64 * m_:64 * m_ + 64, TS0:TS0 + 128], start=True, stop=True),
                      reads=[B_kaT, B_qaT], writes=B_psS[0])
                sc.op("dve", lambda e, m_=m_: e.tensor_tensor(out=dtmp[:, m_, 0:128], in0=psS[0][:, m_, 0:128], in1=bAsn_s[:],
                                                              op=ALU.add), reads=B_psS[0] + [B_bAsn], writes=[B_dtmp])
            sc.op("act", lambda e: e.activation(out=e_t[0][:, :, 0:128], in_=dtmp[:, :, 0:128], func=AF.Exp),
                  reads=[B_dtmp], writes=[B_e[0]])
            for m_ in range(2):
                sc.op("pe", lambda e, m_=m_: e.matmul(po[:, m_, 0:128], lhsT=va[:, 64, :], rhs=e_t[0][:, m_, 0:128],
                                                      start=True, stop=False), reads=[B_va, B_e[0]], writes=B_po)
                sc.op("pe", lambda e, m_=m_: e.matmul(pz[:, m_, 0:128], lhsT=onesb[:], rhs=e_t[0][:, m_, 0:128],
                                                      start=True, stop=False), reads=[B_onesb, B_e[0]], writes=B_pz)
            for i in range(4):
                sc.op("sp", lambda e, i=i: e.dma_start(out=xsl[0][:].rearrange("p (t c) -> p t c", c=128),
                                                       in_=cak[i].rearrange("(t p) c -> p t c", p=128)), writes=[B_xsl[0]], dma=True)
                sc.op("sp", lambda e, i=i: e.dma_start(out=xsl[1][:].rearrange("p (t c) -> p t c", c=128),
                                                       in_=cav[i].rearrange("(t p) c -> p t c", p=128)), writes=[B_xsl[1]], dma=True)
                for t in range(8):
                    sc.op("pe", lambda e, t=t: e.transpose(out=pt[:, t, :], in_=xsl[0][:, t * 128:(t + 1) * 128], identity=ident_s[:]),
                          reads=[B_xsl[0], B_ident], writes=[B_pt])
                sc.op("act", lambda e: e.copy(out=kcT, in_=PA[:].rearrange("p a b -> p (a b)")), reads=[B_pt], writes=[B_kcT])
                sc.op("dve", lambda e: e.tensor_copy(out=vc, in_=xsl[1][:].rearrange("p (t c) -> p t c", c=128)),
                      reads=[B_xsl[1]], writes=[B_vc])
                pss = [PB[:, m2, 0:256].rearrange("p (k q) -> p k q", k=8) for m2 in range(2)]
                esv = es_s[:].rearrange("p k m q -> p (k m q)").rearrange("p (m k q) -> p m k q", m=2, k=8)
                for kb in range(8):
                    for m_ in range(2):
                        sc.op("pe", lambda e, kb=kb, m_=m_, i=i: e.matmul(
                            pss[m_][:, kb, :], lhsT=kcT[64 * m_:64 * m_ + 64, kb * 128:(kb + 1) * 128],
                            rhs=qaT[64 * m_:64 * m_ + 64, TS0 + 32 * i:TS0 + 32 * i + 32], start=True, stop=True),
                            reads=[B_kcT, B_qaT], writes=[B_pf[m_]])
                stmp = kf32[:, 0:512].rearrange("p (m k q) -> p m k q", m=2, k=8)
                for m_ in range(2):
                    sc.op("dve", lambda e, m_=m_: e.tensor_tensor(out=stmp[:, m_, :, :], in0=pss[m_], in1=bAs_s[:],
                                                                  op=ALU.add), reads=[B_pf[m_], B_bAs], writes=[B_kf32])
                sc.op("act", lambda e: e.activation(out=esv, in_=stmp, func=AF.Exp), reads=[B_kf32], writes=[B_es])
                for kb in range(8):
                    for m_ in range(2):
                        sc.op("pe", lambda e, kb=kb, m_=m_, i=i: e.matmul(
                            po[:, m_, 32 * i:32 * i + 32], lhsT=vc[:, kb, :], rhs=esv[:, m_, kb, :], start=False, stop=(i == 3 and kb == 7)),
                            reads=[B_vc, B_es], writes=B_po)
                        sc.op("pe", lambda e, kb=kb, m_=m_, i=i: e.matmul(
                            pz[:, m_, 32 * i:32 * i + 32], lhsT=onesb[:], rhs=esv[:, m_, kb, :], start=False, stop=(i == 3 and kb == 7)),
                            reads=[B_onesb, B_es], writes=B_pz)
            finalize_A(128, TS0, pz[:, 0, :], [B_pz[0]])

            do_rs(rs_srcA2_2, rs_dstA2_2, B_srcA2, B_dstA2)
            B_GA1 = fence(sc.buf("GA1"), [B_qaT]); B_GA2 = fence(sc.buf("GA2"), [B_qaT])
            GA = qaT[:, 0:4 * MT].rearrange("p (s c) -> p s c", s=4)
            B_GB = fence(sc.buf("GB"), [B_kaT]); GB = kaT[:, 0:4 * MT].rearrange("p (s c) -> p s c", s=4)
            sc.op("sp", lambda e: e.dma_start(out=GA[:, :, 0:1024], in_=rs_dstA1_2.rearrange("(s p) c -> p s c", s=4)),
                  reads=[B_dstA1], writes=[B_GA1], dma=True)
            sc.op("pool", lambda e: e.dma_start(out=GA[:, :, 1024:MT], in_=rs_dstA2_2.rearrange("(s p) c -> p s c", s=4)),
                  reads=[B_dstA2], writes=[B_GA2], dma=True)
            sc.op("sp", lambda e: e.dma_start(out=GB, in_=rs_dstB.rearrange("s p c -> p s c")), reads=[B_dstB], writes=[B_GB], dma=True)
            mT = hT[0]; B_mT = B_hT[0]
            for bb in B_e + [B_kcT, B_vc]:
                fence(B_mT, [bb])
            hTm = hT[1]; B_hTm = B_hT[1]
            fence(B_hTm, [B_gst])
            B_y = sc.buf("y_out")
            out_bufs.append(B_y)
            zT = kf32; B_zT = B_kf32

            def p5_group(tok0, ncol, ctx):
                ntile = (ncol + 127) // 128
                for t in range(ntile):
                    rows = min(128, ncol - t * 128)
                    sl = t % 2
                    sc.op("sp", lambda e, sl=sl, t=t, rows=rows: e.dma_start(out=xsl[sl][0:rows, :],
                                                                            in_=xm[tok0 + t * 128:tok0 + t * 128 + rows, :]),
                          writes=[B_xsl[sl]], dma=True)
                    sc.op("act", lambda e, sl=sl, rows=rows: e.activation(out=xn[sl][0:rows, :], in_=xsl[sl][0:rows, :], func=AF.Square,
                                                                          accum_out=ssq[0:rows, sl:sl + 1]),
                          reads=[B_xsl[sl]], writes=[B_xn[sl], B_ssq[sl]])
                    sc.op("act", lambda e, sl=sl, rows=rows: e.activation(out=ssq[0:rows, sl:sl + 1], in_=ssq[0:rows, sl:sl + 1],
                                                                          func=AF.Ln, bias=epsc[0:rows, 0:1]),
                          reads=[B_ssq[sl], B_epsc], writes=[B_ssq[sl]])
                    sc.op("act", lambda e, sl=sl, rows=rows: e.activation(out=rstd[0:rows, sl:sl + 1], in_=ssq[0:rows, sl:sl + 1],
                                                                          func=AF.Exp, scale=-0.5),
                          reads=[B_ssq[sl]], writes=[B_rstd[sl]])
                    sc.op("dve", lambda e, sl=sl, rows=rows: e.tensor_scalar(out=xnb[sl][0:rows, :], in0=xsl[sl][0:rows, :],
                                                                             scalar1=rstd[0:rows, sl:sl + 1], scalar2=None, op0=ALU.mult),
                          reads=[B_xsl[sl], B_rstd[sl]], writes=[B_xnb[sl]])
                    for kc in range(8):
                        sc.op("pe", lambda e, sl=sl, kc=kc, rows=rows: e.matmul(pt[:, kc, 0:rows], lhsT=xnb[sl][0:rows, kc * 128:(kc + 1) * 128],
                                                                                rhs=identb[0:rows, 0:rows], start=True, stop=True),
                              reads=[B_xnb[sl], B_identb], writes=[B_pt])
                    for kc in range(8):
                        sc.op("dve", lambda e, kc=kc, t=t, rows=rows: e.tensor_scalar(
                            out=hTm[:, kc, t * 128:t * 128 + rows], in0=pt[:, kc, 0:rows],
                            scalar1=Amod[:, kc, ctx:ctx + 1], scalar2=modT[:, kc, ctx:ctx + 1], op0=ALU.mult, op1=ALU.add),
                            reads=[B_pt, B_Amod, B_modT], writes=[B_hTm])
                for fo in range(8):
                    par = fo % 2
                    Pab = PB if par == 0 else PA
                    B_ab = [B_pf[0], B_pf[1]] if par == 0 else [B_pt, B_pt]
                    Pmg = PC if par == 0 else PD
                    B_mg = [B_pn, B_pk] if par == 0 else [B_pv[0], B_pv[1]]
                    sqx, B_sqx, rsx, B_rsx = sq2[par], B_sq2[par], rs2[par], B_rs2[par]
                    for kc in range(4):
                        sc.op("pe", lambda e, fo=fo, kc=kc, Pab=Pab: e.matmul(Pab[:, 0, 0:ncol], lhsT=woa[:, kc, fo * 128:(fo + 1) * 128],
                                                                              rhs=GA[:, kc, tok0:tok0 + ncol], start=(kc == 0), stop=(kc == 3)),
                              reads=[B_woa, B_GA1 if tok0 < 1024 else B_GA2], writes=[B_ab[0]])
                    for kc in range(4):
                        sc.op("pe", lambda e, fo=fo, kc=kc, Pab=Pab: e.matmul(Pab[:, 1, 0:ncol], lhsT=wob[:, kc, fo * 128:(fo + 1) * 128],
                                                                              rhs=GB[:, kc, tok0:tok0 + ncol], start=(kc == 0), stop=(kc == 3)),
                              reads=[B_wob, B_GB], writes=[B_ab[1]])
                    for half in range(2):
                        for kc in range(8):
                            sc.op("pe", lambda e, fo=fo, kc=kc, half=half, Pmg=Pmg: e.matmul(
                                Pmg[:, half, 0:ncol], lhsT=wmg[kc // 4][:, kc % 4, half * 1024 + fo * 128:half * 1024 + (fo + 1) * 128],
                                rhs=hTm[:, kc, 0:ncol], start=(kc == 0), stop=(kc == 7)),
                                reads=[B_wmg, B_hTm], writes=[B_mg[half]])
                    sc.op("act", lambda e, Pmg=Pmg, sqx=sqx: e.activation(out=sqx[:, 0:ncol], in_=Pmg[:, 0, 0:ncol], func=AF.Sigmoid),
                          reads=[B_mg[0]], writes=[B_sqx])
                    sc.op("act", lambda e, Pmg=Pmg, rsx=rsx: e.activation(out=rsx[:, 0:ncol], in_=Pmg[:, 1, 0:ncol], func=AF.Sigmoid),
                          reads=[B_mg[1]], writes=[B_rsx])
                    sc.op("dve", lambda e, Pab=Pab, sqx=sqx: e.tensor_tensor(out=sqx[:, 0:ncol], in0=Pab[:, 0, 0:ncol], in1=sqx[:, 0:ncol], op=ALU.mult),
                          reads=[B_ab[0], B_sqx], writes=[B_sqx])
                    sc.op("dve", lambda e, Pab=Pab, rsx=rsx: e.tensor_tensor(out=rsx[:, 0:ncol], in0=Pab[:, 1, 0:ncol], in1=rsx[:, 0:ncol], op=ALU.mult),
                          reads=[B_ab[1], B_rsx], writes=[B_rsx])
                    sc.op("dve", lambda e, fo=fo, sqx=sqx, rsx=rsx: e.tensor_tensor(out=mT[:, fo, 0:ncol], in0=sqx[:, 0:ncol], in1=rsx[:, 0:ncol], op=ALU.add),
                          reads=[B_sqx, B_rsx], writes=[B_mT])
                if gate_ctx[0] != ctx:
                    gate_ctx[0] = ctx
                    for fo in range(8):
                        sc.op("dve", lambda e, fo=fo: e.tensor_scalar(out=vst[fo % 2][:, 0:128], in0=ident_s[:], scalar1=modT[:, 16 + fo, ctx:ctx + 1],
                                                                      scalar2=None, op0=ALU.mult), reads=[B_ident, B_modT], writes=[B_vst[fo % 2]])
                        sc.op("pe", lambda e, fo=fo: e.matmul(PD[:, fo // 4, (fo % 4) * 128:(fo % 4 + 1) * 128], lhsT=onesf_s[:], rhs=vst[fo % 2][:, 0:128],
                                                              start=True, stop=True), reads=[B_onesf, B_vst[fo % 2]], writes=[B_pv[fo // 4]])
                    for hf in range(2):
                        sc.op("dve", lambda e, hf=hf: e.tensor_copy(out=gate_bc[hf][:], in_=PD[:, hf, :]), reads=[B_pv[hf]], writes=[B_gbc[hf]])
                for t in range(ntile):
                    rows = min(128, ncol - t * 128)
                    sl = t % 2
                    sc.op("sp", lambda e, sl=sl, t=t, rows=rows: e.dma_start(out=xsl[sl][0:rows, :],
                                                                            in_=xm[tok0 + t * 128:tok0 + t * 128 + rows, :]),
                          writes=[B_xsl[sl]], dma=True)
                    for hf in range(2):
                        Pz = PD[:, hf, :]
                        B_Pz = B_pv[hf]
                        for kc in range(8):
                            sc.op("pe", lambda e, hf=hf, kc=kc, t=t, rows=rows, Pz=Pz: e.matmul(
                                Pz[0:rows, :], lhsT=mT[:, kc, t * 128:t * 128 + rows], rhs=wout[:, kc, hf * 512:(hf + 1) * 512],
                                start=(kc == 0), stop=(kc == 7)), reads=[B_wout, B_mT], writes=[B_Pz])
                        sc.op("dve", lambda e, hf=hf, sl=sl, rows=rows, Pz=Pz: e.tensor_tensor(
                            out=xn[sl][0:rows, hf * 512:(hf + 1) * 512], in0=Pz[0:rows, :], in1=gate_bc[hf][0:rows, :], op=ALU.mult),
                            reads=[B_Pz, B_gbc[hf]], writes=[B_xn[sl]])
                        sc.op("dve", lambda e, hf=hf, sl=sl, rows=rows: e.tensor_tensor(
                            out=xn[sl][0:rows, hf * 512:(hf + 1) * 512], in0=xn[sl][0:rows, hf * 512:(hf + 1) * 512],
                            in1=xsl[sl][0:rows, hf * 512:(hf + 1) * 512], op=ALU.add),
                            reads=[B_xsl[sl], B_xn[sl]], writes=[B_xn[sl]])
                    sc.op("pool", lambda e, sl=sl, t=t, rows=rows: e.dma_start(out=y[tok0 + t * 128:tok0 + t * 128 + rows, :],
                                                                              in_=xn[sl][0:rows, :]),
                          reads=[B_xn[sl]], writes=[B_y], dma=True)

            gate_ctx = [None]
            gate_bc = [tmpn, kout[1][:].rearrange("p a b -> p (a b)")]; B_gbc = [B_tmpn, fence(sc.buf("gbc1"), [B_kout[1]])]
            for gq in range(4):
                p5_group(gq * 512, 512, 0)
            p5_group(2048, 32, 5)


        except _Stop:
            pass
        sc.op("sp", lambda e: e.nop(), reads=out_bufs)
        with nc.Block() as block:
            sc.emit(block)
    return nc


_NC_CACHE = {}


def kernel(x_prompt, x_sample, cache_a_k, cache_a_v, cache_b_k, cache_b_v, c_prompt, c_sample, g_norm, w_ada, b_ada,
           w_in, g_qa, g_ka, lam_q1, lam_k1, lam_q2, lam_k2, g_subln, t5_bias, g_qb, g_kb, rel_bias_b, w_oa, w_ob, w_out):
    f32 = np.float32
    A = lambda a: np.ascontiguousarray(np.asarray(a, dtype=f32))
    x_prompt, x_sample = A(x_prompt), A(x_sample)
    w_in0 = A(w_in)[0]
    if "nc" not in _NC_CACHE:
        _NC_CACHE["nc"] = build_program()
    nc = _NC_CACHE["nc"]

    ident = np.eye(128, dtype=f32)
    bones = np.zeros((128, 128), f32); bones[:64, :64] = 1; bones[64:, 64:] = 1
    p = np.arange(128)[:, None, None]; iv = np.arange(-1, 4)[None, :, None]; jj = np.arange(512)[None, None, :]
    relA = 128 * iv + p - jj
    bktA = _t5_bucket_np(relA)
    visA = (2 * iv + p // 64) <= (jj // 64)
    mA = np.where(visA, 0.0, NEGM).astype(f32)

    p1 = np.arange(128)
    relAs = 128 * np.arange(8)[None, :, None] + p1[:, None, None] - 1024 - np.arange(32)[None, None, :]
    bktAs = _t5_bucket_np(relAs)
    same = (p1[:, None] // 32) == (p1[None, :] // 32)
    relN = (p1[:, None] % 32) - (p1[None, :] % 32)
    bktN = _t5_bucket_np(relN)
    dd = np.arange(5)[None, :, None]; jq = np.arange(128)[None, None, :]; pp = p1[:, None, None]
    relB = 128 * (dd - 4) + pp - jq
    dlt = 2 * (dd - 4) + pp // 64 - jq // 64
    visB = (dlt <= 0) & (dlt >= -8)
    idxB = np.clip(relB, -128, 128) + 128
    relBs = 128 * (np.arange(4)[None, :, None] - 4) + p1[:, None, None] - np.arange(32)[None, None, :]
    idxBs = np.clip(relBs, -128, 128) + 128
    idxBn = np.clip(relN, -128, 128) + 128
    cak_, cav_, cbk_, cbv_ = A(cache_a_k)[0], A(cache_a_v)[0], A(cache_b_k)[0], A(cache_b_v)[0]
    rbb = A(rel_bias_b)[0]

    in_maps = []
    for c in range(8):
        g, h = c // 4, c % 4
        t5h = A(t5_bias)[:, h]
        gv = np.zeros((128, 8), f32)
        gv[:, 0] = np.tile(A(g_qa)[0], 2); gv[:, 1] = np.tile(A(g_ka)[0], 2)
        gv[:, 2] = np.tile(A(g_qb)[0], 2); gv[:, 3] = np.tile(A(g_kb)[0], 2)
        gv[:, 4] = A(g_subln)[0]
        cols = lambda base, w, idx: w_in0[:, base + idx * w: base + (idx + 1) * w]
        wc = np.concatenate([cols(0, 128, h), cols(512, 128, h), cols(1536, 128, h),
                             cols(2048, 128, h), cols(2560, 128, h), cols(3584, 128, h),
                             cols(1024, 128, h), cols(3072, 128, h)], axis=1)
        cv = np.stack([A(c_prompt)[g]] + [A(c_sample)[4 * g + i] for i in range(4)] + [A(c_sample)[c]], axis=1)
        lam = np.stack([A(lam_q1)[0], A(lam_k1)[0], A(lam_q2)[0], A(lam_k2)[0]], axis=0)
        m = {
            "xp": x_prompt[g],
            "xs4": x_sample[4 * g:4 * g + 4].reshape(128, D),
            "xm": np.concatenate([x_prompt[g, 2048 * h:2048 * h + 2048], x_sample[c]], axis=0),
            "cvec": np.ascontiguousarray(cv),
            "w_ada": A(w_ada)[0],
            "b_ada": np.ascontiguousarray(A(b_ada)[0].reshape(24, 128).T),
            "g_norm": np.ascontiguousarray(A(g_norm)[0].reshape(8, 128).T),
            "w_c": np.ascontiguousarray(wc),
            "w_mg": np.ascontiguousarray(w_in0[:, 4096:6144]),
            "w_oa": A(w_oa)[0], "w_ob": A(w_ob)[0], "w_out": A(w_out)[0],
            "gvec": gv,
            "lamv": np.ascontiguousarray(np.broadcast_to(lam[None], (128, 4, 64))),
            "bA": np.ascontiguousarray(np.where(visA, t5h[bktA], NEGM).astype(f32)),
            "mA": mA,
            "msk": np.ascontiguousarray(np.broadcast_to(np.eye(4, dtype=f32)[h][None], (128, 4))),
            "cak": np.ascontiguousarray(cak_[4 * g:4 * g + 4, :, h, :]),
            "cav": np.ascontiguousarray(cav_[4 * g:4 * g + 4, :, h, :]),
            "cbk": np.ascontiguousarray(cbk_[4 * g:4 * g + 4, :, 2 * h:2 * h + 2, :].reshape(4, 512, 128)),
            "cbv": np.ascontiguousarray(cbv_[4 * g:4 * g + 4, :, 2 * h:2 * h + 2, :].reshape(4, 512, 128)),
            "bAs": np.ascontiguousarray(t5h[bktAs].astype(f32)),
            "bAsn": np.ascontiguousarray(np.where(same, t5h[bktN], NEGM).astype(f32)),
            "bB": np.ascontiguousarray(np.stack([np.where(visB, rbb[idxB, 2 * h + a], NEGM).reshape(128, 640) for a in range(2)], axis=1).astype(f32)),
            "bBs": np.ascontiguousarray(np.stack([rbb[idxBs, 2 * h + a] for a in range(2)], axis=1).astype(f32)),
            "bBsn": np.ascontiguousarray(np.stack([np.where(same, rbb[idxBn, 2 * h + a], NEGM) for a in range(2)], axis=1).astype(f32)),
            "onesf": np.ones((128, 128), f32),
            "c15": np.full((128, 1), t5h[15], f32),
            "ident": ident, "bones": bones,
        }
        in_maps.append(m)
    res = run_bass_kernel_spmd(nc, in_maps, core_ids=list(range(8)))
    R = res.results
    yp = np.zeros((2, S, D), f32); ys = np.zeros((8, 32, D), f32)
    akp = np.zeros((1, 2, S, 4, 128), f32); avp = np.zeros((1, 2, S, 4, 128), f32)
    bkp = np.zeros((1, 2, 512, 8, 64), f32); bvp = np.zeros((1, 2, 512, 8, 64), f32)
    aks = np.zeros((1, 8, 32, 4, 128), f32); avs = np.zeros((1, 8, 32, 4, 128), f32)
    bks = np.zeros((1, 8, 32, 8, 64), f32); bvs = np.zeros((1, 8, 32, 8, 64), f32)
    for c in range(8):
        g, h = c // 4, c % 4
        r = R[c]
        yy = np.asarray(r["y"], f32)
        yp[g, 2048 * h:2048 * h + 2048] = yy[:2048]
        ys[c] = yy[2048:]
        akp[0, g, :, h, :] = np.asarray(r["ak"], f32)
        avp[0, g, :, h, :] = np.asarray(r["av"], f32)
        bkp[0, g, :, 2 * h:2 * h + 2, :] = np.asarray(r["bk"], f32).reshape(512, 2, 64)
        bvp[0, g, :, 2 * h:2 * h + 2, :] = np.asarray(r["bv"], f32).reshape(512, 2, 64)
        aks[0, 4 * g:4 * g + 4, :, h, :] = np.asarray(r["aks"], f32).reshape(4, 32, 128)
        avs[0, 4 * g:4 * g + 4, :, h, :] = np.asarray(r["avs"], f32).reshape(4, 32, 128)
        bks[0, 4 * g:4 * g + 4, :, 2 * h:2 * h + 2, :] = np.asarray(r["bks"], f32).reshape(4, 32, 2, 64)
        bvs[0, 4 * g:4 * g + 4, :, 2 * h:2 * h + 2, :] = np.asarray(r["bvs"], f32).reshape(4, 32, 2, 64)
    return (yp, ys, akp, avp, bkp, bvp, aks, avs, bks, bvs)
```
